# Optimizing a Trainium2 kernel written in Bass

```python
import jax
import jax.numpy as jnp
from jax import lax
import numpy as np

D_MODEL = 2048
BATCH = 8
SEQ = 2048
DEPTH = 2

GRID_W = 64
CTX_LEN = 256
N_MOD = 9
D_FF = 5632
D_SC = D_MODEL // 2
SC_WIDTH = 3
D_CC = D_MODEL // 2
CC_WIDTH = 31
D_EVEN_IN = 3 * D_SC + 2 * D_CC
HEAD_DIM = 64
D_NA = D_MODEL // 2
NA_HEADS = D_NA // HEAD_DIM
WIN_ROWS = 8
WIN_COLS = 16
D_LRU = D_MODEL // 2
LRU_BLOCKS = 16
LRU_CONV = 4
LRU_C = 8.0
D_ODD_IN = 3 * D_NA + 2 * D_LRU
D_MIX = D_MODEL
EPS = 1e-6
NEG_INF = -1e30

kernel_name = 'hybrid_dit_shortconv_conformer_natten_rglru'


def rms_norm(x, g):
    xf = x.astype(jnp.float32)
    y = xf * lax.rsqrt(jnp.mean(xf * xf, axis=-1, keepdims=True) + EPS)
    return (y * g.astype(jnp.float32)).astype(x.dtype)


def layer_norm(x, g, b):
    xf = x.astype(jnp.float32)
    mu = jnp.mean(xf, axis=-1, keepdims=True)
    xc = xf - mu
    y = xc * lax.rsqrt(jnp.mean(xc * xc, axis=-1, keepdims=True) + EPS)
    return (y * g.astype(jnp.float32) + b.astype(jnp.float32)).astype(x.dtype)


def modulate(x, g, shift, scale):
    return rms_norm(x, g) * (1.0 + scale) + shift


def swiglu_ffn(h, w_in, w_out):
    gate, up = jnp.split(h @ w_in, 2, axis=-1)
    return (jax.nn.silu(gate) * up) @ w_out


def half_ffn(x, g, mod, w_in, w_out):
    shift, scale, gate = mod
    return x + 0.5 * gate * swiglu_ffn(modulate(x, g, shift, scale), w_in, w_out)


def dwconv(u, w, b):
    k, ch = w.shape
    lo = (k - 1) // 2
    y = lax.conv_general_dilated(u, w[:, None, :].astype(u.dtype), (1,), [(lo, k - 1 - lo)],
                                 dimension_numbers=('NWC', 'WIO', 'NWC'), feature_group_count=ch)
    return y + b


def even_mixer(h, w_in, sc_w, sc_b, cc_w, cc_b, cc_ln_g, cc_ln_b, w_out):
    z = h @ w_in
    b_gate, c_gate, val, glu_v, glu_g = jnp.split(
        z, [D_SC, 2 * D_SC, 3 * D_SC, 3 * D_SC + D_CC], axis=-1)
    y_sc = b_gate * dwconv(c_gate * val, sc_w, sc_b)
    u = dwconv(glu_v * jax.nn.sigmoid(glu_g), cc_w, cc_b)
    y_cc = jax.nn.silu(layer_norm(u, cc_ln_g, cc_ln_b))
    return jnp.concatenate([y_sc, y_cc], axis=-1) @ w_out


def neighbourhood_attention(q, k, v, k_ctx, v_ctx, rpb):
    bsz, seq, nh, hd = q.shape
    rows = seq // GRID_W
    kr = min(WIN_ROWS, rows)
    qg = jnp.moveaxis(q.reshape(bsz, rows, GRID_W, nh, hd), 1, 0)
    kg = k.reshape(bsz, rows, GRID_W, nh, hd)
    vg = v.reshape(bsz, rows, GRID_W, nh, hd)
    col = jnp.arange(GRID_W)
    col_start = jnp.clip(col - WIN_COLS // 2, 0, GRID_W - WIN_COLS)
    col_in = (col[None, :] >= col_start[:, None]) & (col[None, :] < col_start[:, None] + WIN_COLS)
    dcol = jnp.clip(col[None, :] - col[:, None] + WIN_COLS - 1, 0, 2 * WIN_COLS - 2)
    bias_cols = jnp.where(col_in, rpb.astype(jnp.float32)[:, :, dcol], NEG_INF)
    n_loc = kr * GRID_W

    def row_block(args):
        q_row, r = args
        r0 = jnp.clip(r - kr // 2, 0, rows - kr)
        k_blk = lax.dynamic_slice_in_dim(kg, r0, kr, axis=1).reshape(bsz, n_loc, nh, hd)
        v_blk = lax.dynamic_slice_in_dim(vg, r0, kr, axis=1).reshape(bsz, n_loc, nh, hd)
        drow = r0 + jnp.arange(kr) - r + WIN_ROWS - 1
        bias = jnp.moveaxis(bias_cols[:, drow], 1, 2).reshape(nh, GRID_W, n_loc)
        s_loc = jnp.einsum('bqhd,bkhd->bhqk', q_row, k_blk).astype(jnp.float32) + bias
        s_ctx = jnp.einsum('bqhd,bkhd->bhqk', q_row, k_ctx).astype(jnp.float32)
        p = jax.nn.softmax(jnp.concatenate([s_loc, s_ctx], axis=-1), axis=-1).astype(v.dtype)
        return (jnp.einsum('bhqk,bkhd->bqhd', p[..., :n_loc], v_blk)
                + jnp.einsum('bhqk,bkhd->bqhd', p[..., n_loc:], v_ctx))

    out = lax.map(row_block, (qg, jnp.arange(rows)))
    return jnp.moveaxis(out, 0, 1).reshape(bsz, seq, nh * hd)


def context_attention(q, k, v):
    s = jnp.einsum('bqhd,bkhd->bhqk', q, k).astype(jnp.float32)
    p = jax.nn.softmax(s, axis=-1).astype(v.dtype)
    o = jnp.einsum('bhqk,bkhd->bqhd', p, v)
    return o.reshape(o.shape[0], o.shape[1], -1)


def rglru(x, w_gate, b_gate, lam, h0, reverse):
    bsz, t, _ = x.shape
    xb = x.reshape(bsz, t, LRU_BLOCKS, D_LRU // LRU_BLOCKS)
    gates = jnp.einsum('btni,gnij->gbtnj', xb, w_gate).reshape(2, bsz, t, D_LRU)
    gates = (gates + b_gate[:, None, None, :]).astype(jnp.float32)
    r = jax.nn.sigmoid(gates[0])
    i = jax.nn.sigmoid(gates[1])
    log_a = -LRU_C * r * jax.nn.softplus(-lam.astype(jnp.float32))
    a = jnp.exp(log_a)
    u = jnp.sqrt(-jnp.expm1(2.0 * log_a)) * i * x.astype(jnp.float32)
    start = -1 if reverse else 0
    end = 0 if reverse else -1
    u = u.at[:, start].add(a[:, start] * h0)

    def combine(e1, e2):
        a1, b1 = e1
        a2, b2 = e2
        return a1 * a2, a2 * b1 + b2

    _, h = lax.associative_scan(combine, (a, u), reverse=reverse, axis=1)
    return h, h[:, end]


def odd_mixer(h_ctx, h_lat, w_in, q_g, k_g, rpb, conv_w, conv_b, gate_w, gate_b, lam, w_out, ctx_out):
    bsz, seq, _ = h_lat.shape
    heads = lambda t: t.reshape(t.shape[0], t.shape[1], NA_HEADS, HEAD_DIM)
    scale = HEAD_DIM ** -0.5
    q, k, v, xr, gr = jnp.split(h_lat @ w_in, [D_NA, 2 * D_NA, 3 * D_NA, 3 * D_NA + D_LRU], axis=-1)
    if ctx_out:
        qc, kc, vc, xrc, grc = jnp.split(h_ctx @ w_in, [D_NA, 2 * D_NA, 3 * D_NA, 3 * D_NA + D_LRU], axis=-1)
    else:
        kc, vc, xrc = jnp.split(h_ctx @ w_in[:, D_NA:3 * D_NA + D_LRU], [D_NA, 2 * D_NA], axis=-1)
    qh = rms_norm(heads(q), q_g) * scale
    kh = rms_norm(heads(k), k_g)
    kch = rms_norm(heads(kc), k_g)
    vch = heads(vc)
    o_lat = neighbourhood_attention(qh, kh, heads(v), kch, vch, rpb)
    xr = dwconv(xr, conv_w, conv_b)
    xrc = dwconv(xrc, conv_w, conv_b)
    h0 = jnp.zeros((bsz, D_LRU), jnp.float32)
    hc_f, s_f = rglru(xrc, gate_w[0], gate_b[0], lam[0], h0, False)
    hc_b, s_b = rglru(xrc, gate_w[1], gate_b[1], lam[1], h0, True)
    hl_f, _ = rglru(xr, gate_w[0], gate_b[0], lam[0], s_f, False)
    hl_b, _ = rglru(xr, gate_w[1], gate_b[1], lam[1], s_b, True)
    r_lat = (hl_f + hl_b).astype(h_lat.dtype) * jax.nn.gelu(gr)
    y_lat = jnp.concatenate([o_lat, r_lat], axis=-1) @ w_out
    if ctx_out:
        o_ctx = context_attention(rms_norm(heads(qc), q_g) * scale, kch, vch)
        r_ctx = (hc_f + hc_b).astype(h_ctx.dtype) * jax.nn.gelu(grc)
        y_ctx = jnp.concatenate([o_ctx, r_ctx], axis=-1) @ w_out
    else:
        y_ctx = None
    return y_ctx, y_lat


def setup_inputs(seed: int = 0) -> dict:
    key = jax.random.key(seed)
    keys = jax.random.split(key, 32)
    ks = iter([keys[i] for i in range(32)])
    f32 = jnp.float32
    n_even = (DEPTH + 1) // 2
    n_odd = DEPTH // 2

    def nrm(shape, scale):
        return jax.random.normal(next(ks), shape, f32) * scale

    def gain(shape):
        return 1.0 + nrm(shape, 0.02)

    lru_u = jax.random.uniform(next(ks), (n_odd, 2, D_LRU), f32, 0.9, 0.999)
    lru_a = lru_u ** (1.0 / LRU_C)
    lru_lam = jnp.log(lru_a) - jnp.log1p(-lru_a)
    bs = D_LRU // LRU_BLOCKS
    return {
        'x': nrm((BATCH, SEQ, D_MODEL), 1.0),
        'c': nrm((BATCH, D_MODEL), 1.0),
        'ctx': nrm((BATCH, CTX_LEN, D_MODEL), 1.0),
        'c_ctx': nrm((D_MODEL,), 1.0),
        'w_mod': nrm((DEPTH, D_MODEL, N_MOD * D_MODEL), 0.5 * D_MODEL ** -0.5),
        'b_mod': nrm((DEPTH, N_MOD * D_MODEL), 0.02),
        'norm_g': gain((DEPTH, 3, D_MODEL)),
        'ffn_w_in': nrm((DEPTH, 2, D_MODEL, 2 * D_FF), D_MODEL ** -0.5),
        'ffn_w_out': nrm((DEPTH, 2, D_FF, D_MODEL), D_FF ** -0.5),
        'ev_w_in': nrm((n_even, D_MODEL, D_EVEN_IN), D_MODEL ** -0.5),
        'sc_w': nrm((n_even, SC_WIDTH, D_SC), SC_WIDTH ** -0.5),
        'sc_b': nrm((n_even, D_SC), 0.02),
        'cc_w': nrm((n_even, CC_WIDTH, D_CC), CC_WIDTH ** -0.5),
        'cc_b': nrm((n_even, D_CC), 0.02),
        'cc_ln_g': gain((n_even, D_CC)),
        'cc_ln_b': nrm((n_even, D_CC), 0.02),
        'ev_w_out': nrm((n_even, D_MIX, D_MODEL), D_MIX ** -0.5),
        'od_w_in': nrm((n_odd, D_MODEL, D_ODD_IN), D_MODEL ** -0.5),
        'q_norm_g': gain((n_odd, HEAD_DIM)),
        'k_norm_g': gain((n_odd, HEAD_DIM)),
        'na_rpb': nrm((n_odd, NA_HEADS, 2 * WIN_ROWS - 1, 2 * WIN_COLS - 1), 0.1),
        'lru_conv_w': nrm((n_odd, LRU_CONV, D_LRU), LRU_CONV ** -0.5),
        'lru_conv_b': nrm((n_odd, D_LRU), 0.02),
        'lru_gate_w': nrm((n_odd, 2, 2, LRU_BLOCKS, bs, bs), bs ** -0.5),
        'lru_gate_b': nrm((n_odd, 2, 2, D_LRU), 0.02),
        'lru_lam': lru_lam,
        'od_w_out': nrm((n_odd, D_MIX, D_MODEL), D_MIX ** -0.5),
    }


def reference(x, c, ctx, c_ctx, w_mod, b_mod, norm_g, ffn_w_in, ffn_w_out,
              ev_w_in, sc_w, sc_b, cc_w, cc_b, cc_ln_g, cc_ln_b, ev_w_out,
              od_w_in, q_norm_g, k_norm_g, na_rpb, lru_conv_w, lru_conv_b,
              lru_gate_w, lru_gate_b, lru_lam, od_w_out):
    silu_c = jax.nn.silu(c)
    silu_cc = jax.nn.silu(c_ctx)
    x_lat, x_ctx = x, ctx
    for l in range(DEPTH):
        last = l == DEPTH - 1
        odd = l % 2 == 1
        j = l // 2
        ctx_in = odd or not last
        ctx_out = not last
        m_lat = jnp.split((silu_c @ w_mod[l] + b_mod[l])[:, None, :], N_MOD, axis=-1)
        m_ctx = jnp.split(silu_cc @ w_mod[l] + b_mod[l], N_MOD, axis=-1)
        g = norm_g[l]
        x_lat = half_ffn(x_lat, g[0], m_lat[0:3], ffn_w_in[l, 0], ffn_w_out[l, 0])
        if ctx_in:
            x_ctx = half_ffn(x_ctx, g[0], m_ctx[0:3], ffn_w_in[l, 0], ffn_w_out[l, 0])
        h_lat = modulate(x_lat, g[1], m_lat[3], m_lat[4])
        h_ctx = modulate(x_ctx, g[1], m_ctx[3], m_ctx[4]) if ctx_in else None
        if odd:
            y_ctx, y_lat = odd_mixer(h_ctx, h_lat, od_w_in[j], q_norm_g[j], k_norm_g[j], na_rpb[j],
                                     lru_conv_w[j], lru_conv_b[j], lru_gate_w[j], lru_gate_b[j],
                                     lru_lam[j], od_w_out[j], ctx_out)
        else:
            y_lat = even_mixer(h_lat, ev_w_in[j], sc_w[j], sc_b[j], cc_w[j], cc_b[j],
                               cc_ln_g[j], cc_ln_b[j], ev_w_out[j])
            y_ctx = (even_mixer(h_ctx, ev_w_in[j], sc_w[j], sc_b[j], cc_w[j], cc_b[j],
                                cc_ln_g[j], cc_ln_b[j], ev_w_out[j]) if ctx_out else None)
        x_lat = x_lat + m_lat[5] * y_lat
        x_lat = half_ffn(x_lat, g[2], m_lat[6:9], ffn_w_in[l, 1], ffn_w_out[l, 1])
        if ctx_out:
            x_ctx = x_ctx + m_ctx[5] * y_ctx
            x_ctx = half_ffn(x_ctx, g[2], m_ctx[6:9], ffn_w_in[l, 1], ffn_w_out[l, 1])
    return x_lat
```

```python
import contextlib
import numpy as np
import concourse.bass as bass
import concourse.mybir as mybir
from concourse.bass_utils import run_bass_kernel_spmd

F32 = mybir.dt.float32
BF16 = mybir.dt.bfloat16
AF = mybir.ActivationFunctionType
ALU = mybir.AluOpType

ENGS = ("pe", "act", "dve", "pool", "sp")
NDMA = {"sp": 12, "pool": 12, "act": 4}


class Prog:
    def __init__(self):
        self.nc = bass.Bass("TRN2", target_bir_lowering=False)
        nc = self.nc
        self.streams = {e: [] for e in ENGS}
        self.esem = {e: nc.alloc_semaphore("s_" + e) for e in ("pe", "act", "dve", "pool")}
        self.ecount = {e: 0 for e in self.esem}
        self.known = {e: {} for e in ENGS}
        self.dsem = {q: [nc.alloc_semaphore("d_%s%d" % (q, i)) for i in range(n)] for q, n in NDMA.items()}
        self.dval = {q: [0] * n for q, n in NDMA.items()}
        self.dnext = {q: 0 for q in NDMA}
        self.bufs = {}
        self.n_ops = 0

    def _st(self, key):
        s = self.bufs.get(key)
        if s is None:
            s = self.bufs[key] = [{}, {}]
        return s

    def _deps(self, reads, writes):
        deps = {}

        def add(d):
            for s, v in d.items():
                if deps.get(s, 0) < v:
                    deps[s] = v

        for k in reads:
            add(self._st(k)[0])
        for k in writes:
            st = self._st(k)
            add(st[0])
            add(st[1])
        return deps

    def _waits(self, eng, deps):
        kn = self.known[eng]
        pes = self.esem["pe"]
        for s, v in deps.items():
            if kn.get(s, 0) < v:
                if not (eng == "pe" and s is pes):
                    self.streams[eng].append(("wait", s, v))
                kn[s] = v

    def _record(self, sem, val, reads, writes):
        for k in reads:
            st = self._st(k)
            if st[1].get(sem, 0) < val:
                st[1][sem] = val
        for k in writes:
            st = self._st(k)
            st[0] = {sem: val}
            st[1] = {}

    def op(self, eng, fn, reads=(), writes=()):
        self._waits(eng, self._deps(reads, writes))
        self.ecount[eng] += 1
        sem, val = self.esem[eng], self.ecount[eng]
        self.streams[eng].append(("op", fn, sem))
        self._record(sem, val, reads, writes)
        self.n_ops += 1

    def group(self, eng, fns, reads=(), writes=()):
        self._waits(eng, self._deps(reads, writes))
        self.ecount[eng] += 1
        sem, val = self.esem[eng], self.ecount[eng]
        for f in fns[:-1]:
            self.streams[eng].append(("op", f, None))
        self.streams[eng].append(("op", fns[-1], sem))
        self._record(sem, val, reads, writes)
        self.n_ops += len(fns)

    def dma(self, q, out, in_, reads=(), writes=(), **kw):
        self._waits(q, self._deps(reads, writes))
        k = self.dnext[q]
        self.dnext[q] = (k + 1) % len(self.dsem[q])
        sem = self.dsem[q][k]
        prev = self.dval[q][k]
        if prev and self.known[q].get(sem, 0) < prev:
            self.streams[q].append(("wait", sem, prev))
            self.known[q][sem] = prev
        val = prev + 16
        self.dval[q][k] = val
        self.streams[q].append(("dma", out, in_, sem, kw))
        self._record(sem, val, reads, writes)
        self.n_ops += 1

    def barrier(self):
        deps = {self.esem[e]: self.ecount[e] for e in self.esem if self.ecount[e]}
        for q in NDMA:
            for s, v in zip(self.dsem[q], self.dval[q]):
                if v:
                    deps[s] = v
        for e in ENGS:
            kn = self.known[e]
            for s, v in deps.items():
                if kn.get(s, 0) < v:
                    self.streams[e].append(("wait", s, v))
                    kn[s] = v
        self.bufs = {}

    def emit(self):
        nc = self.nc
        streams = self.streams

        def replay(name, eng):
            for it in streams[name]:
                if it[0] == "wait":
                    eng.wait_ge(it[1], it[2])
                elif it[0] == "op":
                    ins = it[1](eng)
                    if it[2] is not None:
                        ins.then_inc(it[2], 1)
                else:
                    _, out, in_, sem, kw = it
                    eng.dma_start(out=out, in_=in_, **kw).then_inc(sem, 16)

        with nc.Block() as block:
            @block.sync
            def _(e):
                replay("sp", e)

            @block.tensor
            def _(e):
                replay("pe", e)

            @block.scalar
            def _(e):
                replay("act", e)

            @block.vector
            def _(e):
                replay("dve", e)

            @block.gpsimd
            def _(e):
                replay("pool", e)
        return nc


D = 2048
DFF = 5632
NLAT = 2048
NCTX = 256
NTOK = NLAT + NCTX
EPS = 1e-6
TILES = [(0, 512, 0), (512, 512, 0), (1024, 512, 0), (1536, 512, 0), (2048, 256, 1)]
LAT_TILES = TILES[:4]
NEG_INF = -1e30
STQ = "sp"

VEC_LAYOUT = {}
_off = 0
for _n, _r in [("c", 16), ("c_ctx", 16), ("b_mod0", 144), ("b_mod1", 144),
               ("g00", 16), ("g01", 16), ("g02", 16), ("g10", 16), ("g11", 16), ("g12", 16),
               ("sc_w", 24), ("sc_b", 8), ("cc_w", 248), ("cc_b", 8), ("ln_g", 8), ("ln_b", 8),
               ("lcw", 32), ("lcb", 8), ("lgb", 32), ("lam", 16), ("qg", 1), ("kg", 1)]:
    VEC_LAYOUT[_n] = (_off, _r)
    _off += _r
NVROWS = 896
assert _off <= NVROWS


ATT_GROUPS = [
    [(c, c) for c in range(6)],
    [(2 + c, 6 + c) for c in range(8)],
    [(6 + c, 6 + c) for c in range(8)],
    [(10 + c, 14 + c) for c in range(6)],
]


def build(debug_outs=False):
    P = Prog()
    nc = P.nc
    uid = [0]

    def din(name, shape, dt=F32):
        return nc.dram_tensor(name, list(shape), dt, kind="ExternalInput").ap()

    x_d = din("x", [NLAT, D])
    ctx_d = din("ctx", [NCTX, D])
    vecs_d = din("vecs", [NVROWS, 128])
    wmod_d = din("w_mod", [2, D, 9 * D])
    fwi_d = din("ffn_w_in", [2, 2, D, 2 * DFF])
    fwo_d = din("ffn_w_out", [2, 2, DFF, D])
    evi_d = din("ev_w_in", [D, 5120])
    evo_d = din("ev_w_out", [D, D])
    odi_d = din("od_w_in", [D, 5120])
    odo_d = din("od_w_out", [D, D])
    lgw_d = din("lru_gate_w", [2, 2, 16, 64, 64])
    rpb_d = din("rpbT", [16, 20, 128, 512])
    out_d = nc.dram_tensor("out", [NLAT, D], F32, kind="ExternalOutput").ap()

    XT = nc.dram_tensor("XT", [D, NTOK], F32).ap()
    ACTD = nc.dram_tensor("ACTD", [DFF, NTOK], BF16).ap()
    MIXD = nc.dram_tensor("MIXD", [D, NTOK], BF16).ap()
    UD = nc.dram_tensor("UD", [1024, NTOK], F32).ap()
    WOB = nc.dram_tensor("WOB", [8, 128, 44, 256], BF16).ap()
    XTv = XT.rearrange("(c p) t -> p c t", p=128)
    ACTDv = ACTD.rearrange("(c p) t -> p c t", p=128)
    MIXDv = MIXD.rearrange("(c p) t -> p c t", p=128)
    UDv = UD.rearrange("(c p) t -> p c t", p=128)

    def sb(stack, name, shape, dt):
        uid[0] += 1
        return stack.enter_context(nc.sbuf_tensor("%s_%d" % (name, uid[0]), list(shape), dt))

    ps = [nc.alloc_psum_tensor("psb%d" % i, [128, 512], F32) for i in range(8)]
    psk = [("ps", i) for i in range(8)]

    ident_f = nc.alloc_sbuf_tensor("ident_f", [128, 128], F32)
    ident_b = nc.alloc_sbuf_tensor("ident_b", [128, 128], BF16)
    ones_b = nc.alloc_sbuf_tensor("ones_b", [128, 128], BF16)
    blk_b = nc.alloc_sbuf_tensor("blk_b", [128, 128], BF16)
    VP = nc.alloc_sbuf_tensor("VP", [128, NVROWS], F32)
    MOD = nc.alloc_sbuf_tensor("MOD", [128, 2, 144, 2], F32)
    AM = nc.alloc_sbuf_tensor("AM", [128, 2, 3, 2, 3, 16], F32)
    S_bf = nc.alloc_sbuf_tensor("S_bf", [128, 16, 2], BF16)
    LC = nc.alloc_sbuf_tensor("LC", [128, 8, 16], F32)
    QK8 = nc.alloc_sbuf_tensor("QK8", [128, 2], F32)

    def vcol(name, i=0, n=1):
        o = VEC_LAYOUT[name][0] + i
        return VP[:, o:o + n]

    def mm_group(out_ap, pairs, reads, writes):
        n = len(pairs)
        fns = [(lambda e, l=l, r=r, i=i: e.matmul(out_ap, lhsT=l, rhs=r, start=(i == 0), stop=(i == n - 1)))
               for i, (l, r) in enumerate(pairs)]
        P.group("pe", fns, reads, writes)

    WM = []
    bg_state = {"next": 0}

    MR = [nc.alloc_sbuf_tensor("MR%d" % i, [2, 512], F32) for i in range(2)]
    pend = []

    def mod_finish():
        if not pend:
            return
        l, nb = pend.pop(0)
        mr = MR[nb % 2]
        for f4 in range(4):
            P.op("pe", lambda e, mr=mr, f4=f4: e.transpose(out=ps[6][:, 2 * f4:2 * f4 + 2], in_=mr[0:2, f4 * 128:(f4 + 1) * 128], identity=ident_f[0:2, 0:2]),
                 reads=[("MR", nb % 2), "ident_f"], writes=[psk[6]])
        bo = VEC_LAYOUT["b_mod%d" % l][0] + nb * 4
        for v in range(2):
            P.op("dve", lambda e, l=l, v=v, bo=bo, nb=nb: e.tensor_tensor(
                out=MOD[:, l, nb * 4:(nb + 1) * 4, v], in0=ps[6][:, v:8:2], in1=VP[:, bo:bo + 4], op=ALU.add),
                reads=[psk[6], "VP"], writes=["MOD"])
        mod_derive(l, nb)

    def mod_task(l, nb):
        w = WM[nb % 2]
        wk = ("WM", nb % 2)
        wvw = wmod_d[l].rearrange("(kc p) n -> p kc n", p=128)
        P.dma("pool", w[:], wvw[:, :, nb * 512:(nb + 1) * 512], writes=[wk])
        mod_finish()
        mm_group(ps[7][0:2, :], [(S_bf[:, kc, :], w[:, kc, :]) for kc in range(16)],
                 reads=[wk, "S_bf"], writes=[psk[7]])
        mr = MR[nb % 2]
        P.op("act", lambda e, mr=mr: e.copy(out=mr[:], in_=ps[7][0:2, :]), reads=[psk[7]], writes=[("MR", nb % 2)])
        pend.append((l, nb))

    def mod_derive(l, nb):
        n = nb // 12
        if nb % 12 == 7:
            go = VEC_LAYOUT["g%d%d" % (l, n)][0]
            for v in range(2):
                sh = MOD[:, l, (3 * n) * 16:(3 * n + 1) * 16, v]
                sc = MOD[:, l, (3 * n + 1) * 16:(3 * n + 2) * 16, v]
                P.op("dve", lambda e, l=l, n=n, v=v, sc=sc, go=go: e.scalar_tensor_tensor(
                    out=AM[:, l, n, v, 0, :], in0=sc, scalar=1.0, in1=VP[:, go:go + 16], op0=ALU.add, op1=ALU.mult),
                    reads=["MOD", "VP"], writes=["AM"])
                P.op("dve", lambda e, l=l, n=n, v=v, sh=sh: e.tensor_copy(out=AM[:, l, n, v, 1, :], in_=sh),
                     reads=["MOD"], writes=["AM"])
        if nb % 12 == 11:
            for v in range(2):
                ga = MOD[:, l, (3 * n + 2) * 16:(3 * n + 3) * 16, v]
                P.op("dve", lambda e, l=l, n=n, v=v, ga=ga: e.tensor_scalar(
                    out=AM[:, l, n, v, 2, :], in0=ga, scalar1=(1.0 if n == 1 else 0.5), scalar2=None, op0=ALU.mult),
                    reads=["MOD"], writes=["AM"])

    def bg_tasks(n):
        for _ in range(n):
            t = bg_state["next"]
            if t >= 72:
                mod_finish()
                return
            bg_state["next"] = t + 1
            mod_task(t // 36, t % 36)

    def prologue():
        with contextlib.ExitStack() as st:
            P.op("pool", lambda e: e.memset(ident_f[:], 0.0), writes=["ident_f"])
            P.op("pool", lambda e: e.affine_select(out=ident_f[:], in_=ident_f[:], pattern=[[-1, 128]],
                                                   compare_op=ALU.not_equal, fill=1.0, base=0, channel_multiplier=1),
                 reads=["ident_f"], writes=["ident_f"])
            P.op("dve", lambda e: e.tensor_copy(out=ident_b[:], in_=ident_f[:]), reads=["ident_f"], writes=["ident_b"])
            P.op("dve", lambda e: e.memset(ones_b[:], 1.0), writes=["ones_b"])
            P.op("dve", lambda e: e.memset(blk_b[:], 0.0), writes=["blk_b"])
            P.op("dve", lambda e: e.memset(blk_b[0:64, 0:64], 1.0), reads=["blk_b"], writes=["blk_b"])
            P.op("dve", lambda e: e.memset(blk_b[64:128, 64:128], 1.0), reads=["blk_b"], writes=["blk_b"])
            VR = sb(st, "VR", [128, 7, 128], F32)
            P.dma("sp", VR[:], vecs_d.rearrange("(b p) c -> p b c", p=128), writes=["VR"])
            for b in range(7):
                P.op("pe", lambda e, b=b: e.transpose(out=ps[b][:, 0:128], in_=VR[:, b, :], identity=ident_f[:]),
                     reads=["VR", "ident_f"], writes=[psk[b]])
                P.op("dve", lambda e, b=b: e.tensor_copy(out=VP[:, b * 128:(b + 1) * 128], in_=ps[b][:, 0:128]),
                     reads=[psk[b]], writes=["VP"])
            SC = sb(st, "SC", [128, 32], F32)
            P.op("act", lambda e: e.activation(out=SC[:], in_=VP[:, 0:32], func=AF.Silu), reads=["VP"], writes=["SC"])
            for v in range(2):
                P.op("dve", lambda e, v=v: e.tensor_copy(out=S_bf[:, :, v], in_=SC[:, v * 16:(v + 1) * 16]),
                     reads=["SC"], writes=["S_bf"])
            XB = [sb(st, "XB%d" % i, [128, D], F32) for i in range(2)]
            XS = [sb(st, "XS%d" % i, [128, 16, 128], F32) for i in range(2)]
            for tb in range(18):
                src = x_d[tb * 128:(tb + 1) * 128, :] if tb < 16 else ctx_d[(tb - 16) * 128:(tb - 15) * 128, :]
                xb, xs = XB[tb % 2], XS[tb % 2]
                P.dma("sp", xb[:], src, writes=[("XB", tb % 2)])
                for g in range(4):
                    bank = 4 + (tb % 2) * 2 + (g % 2)
                    for q in range(4):
                        dc = g * 4 + q
                        P.op("pe", lambda e, bank=bank, q=q, dc=dc, xb=xb: e.transpose(
                            out=ps[bank][:, q * 128:(q + 1) * 128], in_=xb[:, dc * 128:(dc + 1) * 128], identity=ident_f[:]),
                            reads=[("XB", tb % 2), "ident_f"], writes=[psk[bank]])
                    eng = "act" if g % 2 else "dve"
                    if eng == "act":
                        P.op("act", lambda e, bank=bank, g=g, xs=xs: e.copy(out=xs[:, g * 4:(g + 1) * 4, :], in_=ps[bank][:].rearrange("p (a b) -> p a b", a=4)),
                             reads=[psk[bank]], writes=[("XS", tb % 2)])
                    else:
                        P.op("dve", lambda e, bank=bank, g=g, xs=xs: e.tensor_copy(out=xs[:, g * 4:(g + 1) * 4, :], in_=ps[bank][:].rearrange("p (a b) -> p a b", a=4)),
                             reads=[psk[bank]], writes=[("XS", tb % 2)])
                P.dma("pool", XTv[:, :, tb * 128:(tb + 1) * 128], xs[:], reads=[("XS", tb % 2)], writes=["XT"])
                if tb < 8:
                    bg_tasks(1)
            lo = VEC_LAYOUT["lam"][0]
            e_ = LC[:, 2, :]
            z = LC[:, 3, :]
            z2 = LC[:, 4, :]
            pl = LC[:, 5, :]
            tm = LC[:, 6, :]
            P.op("act", lambda e: e.activation(out=e_, in_=VP[:, lo:lo + 16], func=AF.Exp, scale=-1.0), reads=["VP"], writes=["LC"])
            P.op("dve", lambda e: e.tensor_scalar(out=tm, in0=e_, scalar1=2.0, scalar2=None, op0=ALU.add), reads=["LC"], writes=["LC"])
            P.op("dve", lambda e: e.reciprocal(out=tm, in_=tm), reads=["LC"], writes=["LC"])
            P.op("dve", lambda e: e.tensor_tensor(out=z, in0=e_, in1=tm, op=ALU.mult), reads=["LC"], writes=["LC"])
            P.op("dve", lambda e: e.tensor_tensor(out=z2, in0=z, in1=z, op=ALU.mult), reads=["LC"], writes=["LC"])
            P.op("dve", lambda e: e.tensor_scalar(out=pl, in0=z2, scalar1=1.0 / 13, scalar2=1.0 / 11, op0=ALU.mult, op1=ALU.add), reads=["LC"], writes=["LC"])
            for cf in (1.0 / 9, 1.0 / 7, 1.0 / 5, 1.0 / 3, 1.0):
                P.op("dve", lambda e: e.tensor_tensor(out=pl, in0=pl, in1=z2, op=ALU.mult), reads=["LC"], writes=["LC"])
                P.op("dve", lambda e, cf=cf: e.tensor_scalar(out=pl, in0=pl, scalar1=cf, scalar2=None, op0=ALU.add), reads=["LC"], writes=["LC"])
            P.op("dve", lambda e: e.tensor_tensor(out=pl, in0=pl, in1=z, op=ALU.mult), reads=["LC"], writes=["LC"])
            P.op("dve", lambda e: e.tensor_scalar(out=LC[:, 0, :], in0=pl, scalar1=-16.0, scalar2=None, op0=ALU.mult), reads=["LC"], writes=["LC"])
            P.op("dve", lambda e: e.tensor_scalar(out=LC[:, 1, :], in0=pl, scalar1=-32.0, scalar2=None, op0=ALU.mult), reads=["LC"], writes=["LC"])
            P.op("dve", lambda e: e.tensor_scalar(out=QK8[:, 0:1], in0=vcol("qg"), scalar1=0.125, scalar2=None, op0=ALU.mult), reads=["VP"], writes=["QK8"])
            P.op("dve", lambda e: e.tensor_copy(out=QK8[:, 1:2], in_=vcol("kg")), reads=["VP"], writes=["QK8"])
        P.barrier()

    def norm_phase(st, H, l, n, tiles):
        NXB = 4
        SW = 256
        XS = [sb(st, "NX%d" % i, [128, 16, SW], F32) for i in range(NXB)]
        SQ = [sb(st, "NQ%d" % i, [128, 16, SW], BF16) for i in range(2)]
        TM = [sb(st, "NT%d" % i, [128, SW], F32) for i in range(4)]
        mod_finish()
        subs = []
        for ti, (t0, tw, v) in enumerate(tiles):
            for o in range(0, tw, SW):
                subs.append((ti, t0 + o, v))
        ns = len(subs)

        def load(si):
            ti, t0, v = subs[si]
            b = si % NXB
            P.dma("sp", XS[b][:], XTv[:, :, t0:t0 + SW], reads=["XT"], writes=[("NX", b)])

        def front(si):
            b = si % NXB
            q = si % 2
            bank = si % 4
            P.op("pool", lambda e, b=b, q=q: e.tensor_tensor(out=SQ[q][:, 0:8, :], in0=XS[b][:, 0:8, :], in1=XS[b][:, 0:8, :], op=ALU.mult),
                 reads=[("NX", b)], writes=[("NQ", q, 0)])
            P.op("act", lambda e, b=b, q=q: e.activation(out=SQ[q][:, 8:16, :], in_=XS[b][:, 8:16, :], func=AF.Square),
                 reads=[("NX", b)], writes=[("NQ", q, 1)])
            mm_group(ps[bank][:, 0:SW], [(ones_b[:], SQ[q][:, dc, :]) for dc in range(16)],
                     reads=[("NQ", q, 0), ("NQ", q, 1), "ones_b"], writes=[psk[bank]])

        def mid(si):
            bank = si % 4
            P.op("act", lambda e, bank=bank: e.activation(out=ps[bank][:, 0:SW], in_=ps[bank][:, 0:SW], func=AF.Ln, scale=1.0 / D, bias=EPSB[:, 0:1]),
                 reads=[psk[bank], "EPSB"], writes=[psk[bank]])
            P.op("act", lambda e, bank=bank: e.activation(out=ps[bank][:, 0:SW], in_=ps[bank][:, 0:SW], func=AF.Exp, scale=-0.5),
                 reads=[psk[bank]], writes=[psk[bank]])

        def back(si, dcs):
            ti, t0, v = subs[si]
            b = si % NXB
            bank = si % 4
            for dc in dcs:
                tm = TM[dc % 4]
                P.op("dve", lambda e, tm=tm, b=b, bank=bank, dc=dc: e.tensor_tensor(out=tm[:], in0=XS[b][:, dc, :], in1=ps[bank][:, 0:SW], op=ALU.mult),
                     reads=[("NX", b), psk[bank]], writes=[("NT", dc % 4)])
                if dc % 8 < 3:
                    P.op("dve", lambda e, tm=tm, dc=dc, t0=t0, v=v: e.tensor_scalar(
                        out=H[:, dc, t0:t0 + SW], in0=tm[:], scalar1=AM[:, l, n, v, 0, dc:dc + 1], scalar2=AM[:, l, n, v, 1, dc:dc + 1],
                        op0=ALU.mult, op1=ALU.add),
                        reads=[("NT", dc % 4), "AM"], writes=[("Hw", si, dc)])
                else:
                    P.op("act", lambda e, tm=tm, dc=dc, t0=t0, v=v: e.activation(
                        out=H[:, dc, t0:t0 + SW], in_=tm[:], func=AF.Identity,
                        scale=AM[:, l, n, v, 0, dc:dc + 1], bias=AM[:, l, n, v, 1, dc:dc + 1]),
                        reads=[("NT", dc % 4), "AM"], writes=[("Hw", si, dc)])

        for si in range(min(3, ns)):
            load(si)
        front(0)
        mid(0)
        for si in range(ns):
            if si + 3 < ns:
                load(si + 3)
            if si + 1 < ns:
                front(si + 1)
            back(si, range(0, 8))
            if si + 1 < ns:
                mid(si + 1)
            back(si, range(8, 16))

    def ffn(l, i, tiles):
        nrm = 0 if i == 0 else 2
        wi = fwi_d[l, i].rearrange("(kc p) n -> p kc n", p=128)
        wo = fwo_d[l, i].rearrange("(jc p) n -> p jc n", p=128)
        with contextlib.ExitStack() as st:
            H = sb(st, "H", [128, 16, NTOK], BF16)
            with contextlib.ExitStack() as st2:
                norm_phase(st2, H, l, nrm, tiles)
            P.barrier()
            hk = [("H", ti) for ti in range(len(tiles))]
            NB = 2
            WG = [sb(st, "WG%d" % k, [128, 16, 512], BF16) for k in range(NB)]
            WU = [sb(st, "WU%d" % k, [128, 16, 512], BF16) for k in range(NB)]
            AS = [sb(st, "AS%d" % k, [128, NTOK], BF16) for k in range(2)]
            SG = [sb(st, "SG%d" % k, [128, 512], F32) for k in range(2)]
            cnt = 0
            for jg in range(11):
                b = jg % NB
                P.dma("pool", WG[b][:], wi[:, :, jg * 512:(jg + 1) * 512], writes=[("WG", b)])
                P.dma("pool", WU[b][:], wi[:, :, DFF + jg * 512:DFF + (jg + 1) * 512], writes=[("WU", b)])
                for f2 in range(8):
                    P.dma("pool", WOB[f2, :, 4 * jg:4 * jg + 4, :],
                          fwo_d[l, i][jg * 512:(jg + 1) * 512, f2 * 256:(f2 + 1) * 256].rearrange("(jc p) c -> p jc c", p=128),
                          writes=["WOB"])
                bg_tasks(2)
                for j4 in range(4):
                    j = jg * 4 + j4
                    a_s = AS[j % 2]
                    for ti, (t0, tw, v) in enumerate(tiles):
                        bg = (cnt % 4) * 2
                        bu = bg + 1
                        sgk = cnt % 2
                        cnt += 1
                        mm_group(ps[bg][:, 0:tw], [(WG[b][:, kc, j4 * 128:(j4 + 1) * 128], H[:, kc, t0:t0 + tw]) for kc in range(16)],
                                 reads=[("WG", b), ("H", ti)], writes=[psk[bg]])
                        mm_group(ps[bu][:, 0:tw], [(WU[b][:, kc, j4 * 128:(j4 + 1) * 128], H[:, kc, t0:t0 + tw]) for kc in range(16)],
                                 reads=[("WU", b), ("H", ti)], writes=[psk[bu]])
                        sg = SG[sgk]
                        P.op("act", lambda e, sg=sg, bg=bg, tw=tw: e.activation(out=sg[:, 0:tw], in_=ps[bg][:, 0:tw], func=AF.Silu),
                             reads=[psk[bg]], writes=[("SG", sgk)])
                        P.op("dve", lambda e, sg=sg, bu=bu, tw=tw, t0=t0, a_s=a_s: e.tensor_tensor(
                            out=a_s[:, t0:t0 + tw], in0=ps[bu][:, 0:tw], in1=sg[:, 0:tw], op=ALU.mult),
                            reads=[psk[bu], ("SG", sgk)], writes=[("AS", j % 2)])
                    n0, n1 = tiles[0][0], tiles[-1][0] + tiles[-1][1]
                    P.dma("sp", ACTD[j * 128:(j + 1) * 128, n0:n1], a_s[:, n0:n1], reads=[("AS", j % 2)], writes=["ACTD"])
        P.barrier()
        if len(tiles) == 5:
            tiles = tiles[-1:] + tiles[:-1]
        with contextlib.ExitStack() as st:
            AT = [sb(st, "AT%d" % k, [128, 44, 512], BF16) for k in range(2)]
            NW = 2 if l == 0 else 3
            WO = [sb(st, "WO%d" % k, [128, 44, 256], BF16) for k in range(NW)]
            XC = [sb(st, "XC%d" % k, [128, 512], F32) for k in range(4)]
            XN = [sb(st, "XN%d" % k, [128, 512], F32) for k in range(4)]
            wcnt = 0
            cnt = 0
            def at_load(ti):
                t0, tw, v = tiles[ti]
                P.dma("sp", AT[ti % 2][:, :, 0:tw], ACTDv[:, :, t0:t0 + tw], reads=["ACTD"], writes=[("AT", ti % 2)])

            at_load(0)
            for ti, (t0, tw, v) in enumerate(tiles):
                at = AT[ti % 2]
                if ti + 1 < len(tiles):
                    at_load(ti + 1)
                for f2 in range(8):
                    wb = wcnt % NW
                    wcnt += 1
                    P.dma("act", WO[wb][:], WOB[f2], reads=["WOB"], writes=[("WO", wb)])
                    for fh in range(2):
                        fo = f2 * 2 + fh
                        k4 = cnt % 4
                        bank = cnt % 8
                        cnt += 1
                        P.dma("sp", XC[k4][:, 0:tw], XT[fo * 128:(fo + 1) * 128, t0:t0 + tw], reads=[("XT", fo, ti)], writes=[("XC", k4)])
                        mm_group(ps[bank][:, 0:tw], [(WO[wb][:, jc, fh * 128:(fh + 1) * 128], at[:, jc, 0:tw]) for jc in range(44)],
                                 reads=[("WO", wb), ("AT", ti % 2)], writes=[psk[bank]])
                        P.op("dve", lambda e, k4=k4, bank=bank, tw=tw, fo=fo, v=v: e.scalar_tensor_tensor(
                            out=XN[k4][:, 0:tw], in0=ps[bank][:, 0:tw], scalar=AM[:, l, nrm, v, 2, fo:fo + 1], in1=XC[k4][:, 0:tw],
                            op0=ALU.mult, op1=ALU.add),
                            reads=[psk[bank], ("XC", k4), "AM"], writes=[("XN", k4)])
                        P.dma("pool", XT[fo * 128:(fo + 1) * 128, t0:t0 + tw], XN[k4][:, 0:tw], reads=[("XN", k4)], writes=[("XT", fo, ti)])
        P.barrier()

    def mixer_out(l, w_d, tiles):
        wv = w_d.rearrange("(kc p) n -> p kc n", p=128)
        with contextlib.ExitStack() as st:
            M = sb(st, "MX", [128, 16, NTOK], BF16)
            for ti, (t0, tw, v) in enumerate(tiles):
                P.dma("sp", M[:, :, t0:t0 + tw], MIXDv[:, :, t0:t0 + tw], reads=["MIXD"], writes=[("MX", ti)])
            NW = 3
            WO = [sb(st, "WOM%d" % k, [128, 16, 256], BF16) for k in range(NW)]
            XC = [sb(st, "XC%d" % k, [128, 512], F32) for k in range(4)]
            XN = [sb(st, "XN%d" % k, [128, 512], F32) for k in range(4)]
            cnt = 0
            for f2 in range(8):
                wb = f2 % NW
                P.dma("pool", WO[wb][:], wv[:, :, f2 * 256:(f2 + 1) * 256], writes=[("WOM", wb)])
                for fh in range(2):
                    fo = f2 * 2 + fh
                    for ti, (t0, tw, v) in enumerate(tiles):
                        k4 = cnt % 4
                        bank = cnt % 8
                        cnt += 1
                        P.dma("act", XC[k4][:, 0:tw], XT[fo * 128:(fo + 1) * 128, t0:t0 + tw], reads=[("XT", fo, ti)], writes=[("XC", k4)])
                        mm_group(ps[bank][:, 0:tw], [(WO[wb][:, kc, fh * 128:(fh + 1) * 128], M[:, kc, t0:t0 + tw]) for kc in range(16)],
                                 reads=[("WOM", wb), ("MX", ti)], writes=[psk[bank]])
                        P.op("dve", lambda e, k4=k4, bank=bank, tw=tw, fo=fo, v=v: e.scalar_tensor_tensor(
                            out=XN[k4][:, 0:tw], in0=ps[bank][:, 0:tw], scalar=AM[:, l, 1, v, 2, fo:fo + 1], in1=XC[k4][:, 0:tw],
                            op0=ALU.mult, op1=ALU.add),
                            reads=[psk[bank], ("XC", k4), "AM"], writes=[("XN", k4)])
                        P.dma("sp", XT[fo * 128:(fo + 1) * 128, t0:t0 + tw], XN[k4][:, 0:tw], reads=[("XN", k4)], writes=[("XT", fo, ti)])
        P.barrier()

    def even_mixer(l):
        tiles = TILES
        wv = evi_d.rearrange("(kc p) n -> p kc n", p=128)
        segs = [(0, NLAT, LAT_TILES), (NLAT, NCTX, TILES[4:])]
        with contextlib.ExitStack() as st:
            H = sb(st, "H", [128, 16, NTOK], BF16)
            with contextlib.ExitStack() as st2:
                norm_phase(st2, H, l, 1, tiles)
            P.barrier()
            W5 = [[sb(st, "W5_%d_%d" % (k, g), [128, 16, 128], BF16) for g in range(5)] for k in range(2)]
            DG = [sb(st, "DG%d" % k, [128, 34, 128], BF16) for k in range(2)]
            CV = sb(st, "CV", [128, NLAT + 2], BF16)
            GL = sb(st, "GL", [128, NLAT + 30], BF16)
            BG = sb(st, "BG", [128, NLAT], F32)
            TS = [sb(st, "TS%d" % k, [128, 512], F32) for k in range(2)]
            YS = [sb(st, "YS%d" % k, [128, 512], BF16) for k in range(2)]
            UO = [sb(st, "UO%d" % k, [128, 512], F32) for k in range(2)]
            cnt = 0
            for m in range(8):
                k = m % 2
                for g in range(5):
                    P.dma("pool", W5[k][g][:], wv[:, :, g * 1024 + m * 128:g * 1024 + (m + 1) * 128], writes=[("W5", k, g)])
                bg_tasks(1)
                for tp in range(34):
                    col = vcol("sc_w", tp * 8 + m) if tp < 3 else vcol("cc_w", (tp - 3) * 8 + m)
                    P.op("dve", lambda e, k=k, tp=tp, col=col: e.tensor_scalar(out=DG[k][:, tp, :], in0=ident_f[:], scalar1=col, scalar2=None, op0=ALU.mult),
                         reads=["ident_f", "VP"], writes=[("DG", k)])
                for (s0, sl, stiles) in segs:
                    P.op("dve", lambda e: e.memset(CV[:, 0:1], 0.0), writes=["CV"])
                    P.op("dve", lambda e, sl=sl: e.memset(CV[:, sl + 1:sl + 2], 0.0), writes=["CV"])
                    P.op("dve", lambda e: e.memset(GL[:, 0:15], 0.0), writes=["GL"])
                    P.op("dve", lambda e, sl=sl: e.memset(GL[:, sl + 15:sl + 30], 0.0), writes=["GL"])
                    for (t0, tw, v) in stiles:
                        ti = [x[0] for x in TILES].index(t0)
                        lt = t0 - s0
                        bs = [(cnt * 5 + q) % 8 for q in range(5)]
                        cnt += 1
                        for g in range(5):
                            mm_group(ps[bs[g]][:, 0:tw], [(W5[k][g][:, kc, :], H[:, kc, t0:t0 + tw]) for kc in range(16)],
                                     reads=[("W5", k, g), ("H", ti)], writes=[psk[bs[g]]])
                        P.op("act", lambda e, lt=lt, tw=tw, bk=bs[0]: e.copy(out=BG[:, lt:lt + tw], in_=ps[bk][:, 0:tw]),
                             reads=[psk[bs[0]]], writes=["BG"])
                        P.op("act", lambda e, tw=tw, bk=bs[2]: e.copy(out=TS[0][:, 0:tw], in_=ps[bk][:, 0:tw]),
                             reads=[psk[bs[2]]], writes=[("TS", 0)])
                        P.op("dve", lambda e, lt=lt, tw=tw, bk=bs[1]: e.tensor_tensor(out=CV[:, 1 + lt:1 + lt + tw], in0=ps[bk][:, 0:tw], in1=TS[0][:, 0:tw], op=ALU.mult),
                             reads=[psk[bs[1]], ("TS", 0)], writes=["CV"])
                        P.op("act", lambda e, tw=tw, bk=bs[4]: e.activation(out=TS[1][:, 0:tw], in_=ps[bk][:, 0:tw], func=AF.Sigmoid),
                             reads=[psk[bs[4]]], writes=[("TS", 1)])
                        P.op("dve", lambda e, lt=lt, tw=tw, bk=bs[3]: e.tensor_tensor(out=GL[:, 15 + lt:15 + lt + tw], in0=ps[bk][:, 0:tw], in1=TS[1][:, 0:tw], op=ALU.mult),
                             reads=[psk[bs[3]], ("TS", 1)], writes=["GL"])
                    bg_tasks(1)
                    for (t0, tw, v) in stiles:
                        lt = t0 - s0
                        b1 = (cnt * 2) % 8
                        b2 = (cnt * 2 + 1) % 8
                        y = cnt % 2
                        cnt += 1
                        mm_group(ps[b1][:, 0:tw], [(DG[k][:, tp, :], CV[:, lt + tp:lt + tp + tw]) for tp in range(3)],
                                 reads=[("DG", k), "CV"], writes=[psk[b1]])
                        P.op("dve", lambda e, b1=b1, tw=tw, lt=lt, y=y, m=m: e.scalar_tensor_tensor(
                            out=YS[y][:, 0:tw], in0=ps[b1][:, 0:tw], scalar=vcol("sc_b", m), in1=BG[:, lt:lt + tw], op0=ALU.add, op1=ALU.mult),
                            reads=[psk[b1], "BG", "VP"], writes=[("YS", y)])
                        P.dma("sp", MIXD[m * 128:(m + 1) * 128, t0:t0 + tw], YS[y][:, 0:tw], reads=[("YS", y)], writes=["MIXD"])
                        mm_group(ps[b2][:, 0:tw], [(DG[k][:, 3 + tp, :], GL[:, lt + tp:lt + tp + tw]) for tp in range(31)],
                                 reads=[("DG", k), "GL"], writes=[psk[b2]])
                        P.op("act", lambda e, b2=b2, tw=tw, y=y, m=m: e.activation(out=UO[y][:, 0:tw], in_=ps[b2][:, 0:tw], func=AF.Identity, bias=vcol("cc_b", m)),
                             reads=[psk[b2], "VP"], writes=[("UO", y)])
                        P.dma("sp", UD[m * 128:(m + 1) * 128, t0:t0 + tw], UO[y][:, 0:tw], reads=[("UO", y)], writes=["UD"])
                    if s0 == 0:
                        bg_tasks(1)
        P.barrier()
        with contextlib.ExitStack() as st:
            UT = [sb(st, "UT%d" % k, [128, 8, 512], F32) for k in range(3)]
            UB = [sb(st, "UB%d" % k, [128, 8, 512], BF16) for k in range(1)]
            UQ = [sb(st, "UQ%d" % k, [128, 8, 512], BF16) for k in range(1)]
            MSQ = [sb(st, "MSQ%d" % k, [128, 512], F32) for k in range(2)]
            T1 = [sb(st, "T1_%d" % k, [128, 512], F32) for k in range(4)]
            YC = [sb(st, "YC%d" % k, [128, 8, 512], BF16) for k in range(2)]
            nt = len(tiles)

            def lload(ti):
                t0, tw, v = tiles[ti]
                u3 = ti % 3
                P.dma("sp", UT[u3][:, :, 0:tw], UDv[:, :, t0:t0 + tw], reads=["UD"], writes=[("UT", u3)])

            def lfront(ti):
                t0, tw, v = tiles[ti]
                b = ti % 2
                u3 = ti % 3
                ut, ub, uq = UT[u3], UB[0], UQ[0]
                b1, b2 = b * 2, b * 2 + 1
                P.op("act", lambda e, ut=ut, ub=ub, tw=tw: e.copy(out=ub[:, :, 0:tw], in_=ut[:, :, 0:tw]), reads=[("UT", u3)], writes=[("UB", 0)])
                P.op("act", lambda e, ut=ut, uq=uq, tw=tw: e.activation(out=uq[:, :, 0:tw], in_=ut[:, :, 0:tw], func=AF.Square), reads=[("UT", u3)], writes=[("UQ", 0)])
                mm_group(ps[b1][:, 0:tw], [(ones_b[:], ub[:, mc, 0:tw]) for mc in range(8)], reads=[("UB", 0), "ones_b"], writes=[psk[b1]])
                mm_group(ps[b2][:, 0:tw], [(ones_b[:], uq[:, mc, 0:tw]) for mc in range(8)], reads=[("UQ", 0), "ones_b"], writes=[psk[b2]])

            def lmid(ti):
                t0, tw, v = tiles[ti]
                b = ti % 2
                b1, b2 = b * 2, b * 2 + 1
                msq = MSQ[b]
                P.op("dve", lambda e, b1=b1, tw=tw: e.tensor_scalar(out=ps[b1][:, 0:tw], in0=ps[b1][:, 0:tw], scalar1=1.0 / 1024, scalar2=None, op0=ALU.mult),
                     reads=[psk[b1]], writes=[psk[b1]])
                P.op("act", lambda e, msq=msq, b1=b1, tw=tw: e.activation(out=msq[:, 0:tw], in_=ps[b1][:, 0:tw], func=AF.Square),
                     reads=[psk[b1]], writes=[("MSQ", b)])
                P.op("dve", lambda e, msq=msq, b2=b2, tw=tw: e.scalar_tensor_tensor(out=ps[b2][:, 0:tw], in0=ps[b2][:, 0:tw], scalar=1.0 / 1024, in1=msq[:, 0:tw], op0=ALU.mult, op1=ALU.subtract),
                     reads=[psk[b2], ("MSQ", b)], writes=[psk[b2]])
                P.op("act", lambda e, b2=b2, tw=tw: e.activation(out=ps[b2][:, 0:tw], in_=ps[b2][:, 0:tw], func=AF.Ln, bias=EPSB[:, 0:1]), reads=[psk[b2], "EPSB"], writes=[psk[b2]])
                P.op("act", lambda e, b2=b2, tw=tw: e.activation(out=ps[b2][:, 0:tw], in_=ps[b2][:, 0:tw], func=AF.Exp, scale=-0.5), reads=[psk[b2]], writes=[psk[b2]])

            def lback(ti, mcs):
                t0, tw, v = tiles[ti]
                b = ti % 2
                b1, b2 = b * 2, b * 2 + 1
                u3 = ti % 3
                ut, yc = UT[u3], YC[b]
                for mc in mcs:
                    t1 = T1[mc % 4]
                    P.op("dve", lambda e, t1=t1, ut=ut, b1=b1, mc=mc, tw=tw: e.tensor_tensor(out=t1[:, 0:tw], in0=ut[:, mc, 0:tw], in1=ps[b1][:, 0:tw], op=ALU.subtract),
                         reads=[("UT", u3), psk[b1]], writes=[("T1", mc % 4)])
                    P.op("dve", lambda e, t1=t1, b2=b2, tw=tw: e.tensor_tensor(out=t1[:, 0:tw], in0=t1[:, 0:tw], in1=ps[b2][:, 0:tw], op=ALU.mult),
                         reads=[("T1", mc % 4), psk[b2]], writes=[("T1", mc % 4)])
                    P.op("act", lambda e, t1=t1, yc=yc, mc=mc, tw=tw: e.activation(out=yc[:, mc, 0:tw], in_=t1[:, 0:tw], func=AF.Silu, scale=vcol("ln_g", mc), bias=vcol("ln_b", mc)),
                         reads=[("T1", mc % 4), "VP"], writes=[("YC", b)])
                if mcs[-1] == 7:
                    P.dma("sp", MIXDv[:, 8:16, t0:t0 + tw], yc[:, :, 0:tw], reads=[("YC", b)], writes=["MIXD"])

            lload(0)
            if nt > 1:
                lload(1)
            lfront(0)
            lmid(0)
            for ti in range(nt):
                if ti + 2 < nt:
                    lload(ti + 2)
                if ti + 1 < nt:
                    lfront(ti + 1)
                lback(ti, list(range(0, 4)))
                if ti + 1 < nt:
                    lmid(ti + 1)
                lback(ti, list(range(4, 8)))
        P.barrier()
        mixer_out(l, evo_d, tiles)

    def odd_mixer(l):
        tiles = TILES
        wv = odi_d.rearrange("(kc p) n -> p kc n", p=128)
        with contextlib.ExitStack() as st:
            H = sb(st, "H", [128, 16, NTOK], BF16)
            with contextlib.ExitStack() as st2:
                norm_phase(st2, H, l, 1, tiles)
            P.barrier()
            with contextlib.ExitStack() as sa:
                W3 = [[sb(sa, "W3_%d_%d" % (k, g), [128, 16, 128], BF16) for g in range(3)] for k in range(2)]
                QT = sb(sa, "QT", [128, NLAT], BF16)
                KT = sb(sa, "KT", [128, NTOK], BF16)
                V = sb(sa, "V", [128, 18, 128], BF16)
                BT = [sb(sa, "BT%d" % k, [128, 20, 512], BF16) for k in range(2)]
                SQ = [sb(sa, "SQ%d" % k, [128, 512], BF16) for k in range(2)]
                RS = [sb(sa, "RS%d" % k, [128, 512], F32) for k in range(2)]
                TN = [sb(sa, "TN%d" % k, [128, 512], F32) for k in range(2)]
                NPB = 4
                PB = [sb(sa, "PB%d" % k, [128, 512], BF16) for k in range(NPB)]
                RD = [sb(sa, "RD%d" % k, [128, 512], F32) for k in range(2)]
                OS = [sb(sa, "OS%d" % k, [128, 512], BF16) for k in range(2)]
                cq = 0
                LEAD = 2
                for m in range(8):
                    k = m % 2
                    for g in range(3):
                        P.dma("pool", W3[k][g][:], wv[:, :, g * 1024 + m * 128:g * 1024 + (m + 1) * 128], writes=[("W3", k, g)])
                    for hh in range(2):
                        P.dma("pool", BT[hh][:], rpb_d[2 * m + hh].rearrange("t p q -> p t q"), writes=[("BT", hh)])
                    for g, (dst, tl, gcol) in enumerate([(QT, LAT_TILES, 0), (KT, TILES, 1)]):
                        for (t0, tw, v) in tl:
                            ti = [x[0] for x in TILES].index(t0)
                            b = cq % 2
                            bank = (cq % 2) * 2
                            cq += 1
                            mm_group(ps[bank][:, 0:tw], [(W3[k][g][:, kc, :], H[:, kc, t0:t0 + tw]) for kc in range(16)],
                                     reads=[("W3", k, g), ("H", ti)], writes=[psk[bank]])
                            P.op("act", lambda e, b=b, bank=bank, tw=tw: e.activation(out=SQ[b][:, 0:tw], in_=ps[bank][:, 0:tw], func=AF.Square),
                                 reads=[psk[bank]], writes=[("SQ", b)])
                            mm_group(ps[bank + 1][:, 0:tw], [(blk_b[:], SQ[b][:, 0:tw])], reads=[("SQ", b), "blk_b"], writes=[psk[bank + 1]])
                            P.op("act", lambda e, b=b, bank=bank, tw=tw: e.activation(out=RS[b][:, 0:tw], in_=ps[bank + 1][:, 0:tw], func=AF.Ln, scale=1.0 / 64, bias=EPSB[:, 0:1]),
                                 reads=[psk[bank + 1], "EPSB"], writes=[("RS", b)])
                            P.op("act", lambda e, b=b, tw=tw: e.activation(out=RS[b][:, 0:tw], in_=RS[b][:, 0:tw], func=AF.Exp, scale=-0.5),
                                 reads=[("RS", b)], writes=[("RS", b)])
                            P.op("dve", lambda e, b=b, bank=bank, tw=tw: e.tensor_tensor(out=TN[b][:, 0:tw], in0=ps[bank][:, 0:tw], in1=RS[b][:, 0:tw], op=ALU.mult),
                                 reads=[psk[bank], ("RS", b)], writes=[("TN", b)])
                            P.op("act", lambda e, b=b, tw=tw, t0=t0, dst=dst, gcol=gcol: e.activation(out=dst[:, t0:t0 + tw], in_=TN[b][:, 0:tw], func=AF.Identity, scale=QK8[:, gcol:gcol + 1]),
                                 reads=[("TN", b), "QK8"], writes=["QT" if gcol == 0 else "KT"])
                    for tcg in range(5):
                        bank = 4 + tcg % 2
                        n_in = 4 if tcg < 4 else 2
                        for q in range(n_in):
                            tc = tcg * 4 + q
                            ti = min(tc // 4, 4)
                            mm_group(ps[bank][:, q * 128:(q + 1) * 128], [(H[:, kc, tc * 128:(tc + 1) * 128], W3[k][2][:, kc, :]) for kc in range(16)],
                                     reads=[("W3", k, 2), ("H", ti)], writes=[psk[bank]])
                        P.op("act", lambda e, bank=bank, tcg=tcg, n_in=n_in: e.copy(
                            out=V[:, tcg * 4:tcg * 4 + n_in, :], in_=ps[bank][:, 0:n_in * 128].rearrange("p (a b) -> p a b", a=n_in)),
                            reads=[psk[bank]], writes=["V"])
                    items = []
                    for i4 in range(4):
                        loc = ATT_GROUPS[i4]
                        for hh in range(2):
                            n = len(loc) + 2
                            for ci, (kc, et) in enumerate(loc + [(16, None), (17, None)]):
                                items.append((i4, hh, kc, et, ci == 0, ci == n - 1))
                    for idx in range(len(items) + LEAD):
                        if idx < len(items):
                            i4, hh, kc, et, first, last = items[idx]
                            pb = hh * 64
                            sbank = idx % 3
                            pk = idx % NPB
                            q0 = i4 * 512
                            pairs = []
                            rd = ["KT", "QT"]
                            if et is not None:
                                pairs.append((ident_b[:], BT[hh][:, et, :]))
                                rd += ["ident_b", ("BT", hh)]
                            fns = []
                            if et is not None:
                                fns.append(lambda e, sbank=sbank, hh=hh, et=et: e.matmul(ps[sbank][:], lhsT=ident_b[:], rhs=BT[hh][:, et, :], start=True, stop=False))
                            fns.append(lambda e, sbank=sbank, pb=pb, kc=kc, q0=q0, st_=(et is None): e.matmul(
                                ps[sbank][:], lhsT=KT[pb:pb + 64, kc * 128:(kc + 1) * 128], rhs=QT[pb:pb + 64, q0:q0 + 512], start=st_, stop=True))
                            P.group("pe", fns, reads=rd, writes=[psk[sbank]])
                            P.op("act", lambda e, sbank=sbank, pk=pk: e.activation(out=PB[pk][:], in_=ps[sbank][:], func=AF.Exp),
                                 reads=[psk[sbank]], writes=[("PB", pk)])
                        j = idx - LEAD
                        if j >= 0:
                            i4, hh, kc, et, first, last = items[j]
                            pb = hh * 64
                            pk = j % NPB
                            ob, db = (3, 4) if i4 % 2 == 0 else (5, 6)
                            P.group("pe", [
                                lambda e, ob=ob, pb=pb, kc=kc, pk=pk, first=first, last=last: e.matmul(
                                    ps[ob][pb:pb + 64, :], lhsT=V[:, kc, pb:pb + 64], rhs=PB[pk][:], start=first, stop=last),
                                lambda e, db=db, pb=pb, pk=pk, first=first, last=last: e.matmul(
                                    ps[db][pb:pb + 64, :], lhsT=ones_b[:, 0:64], rhs=PB[pk][:], start=first, stop=last),
                            ], reads=["V", ("PB", pk), "ones_b"], writes=[psk[ob], psk[db]])
                            if last and hh == 1:
                                rk = i4 % 2
                                P.op("dve", lambda e, rk=rk, db=db: e.reciprocal(out=RD[rk][:], in_=ps[db][:]), reads=[psk[db]], writes=[("RD", rk)])
                                P.op("dve", lambda e, rk=rk, ob=ob: e.tensor_tensor(out=OS[rk][:], in0=ps[ob][:], in1=RD[rk][:], op=ALU.mult),
                                     reads=[psk[ob], ("RD", rk)], writes=[("OS", rk)])
                                P.dma("sp", MIXD[m * 128:(m + 1) * 128, i4 * 512:(i4 + 1) * 512], OS[rk][:], reads=[("OS", rk)], writes=["MIXD"])
            P.barrier()
            with contextlib.ExitStack() as sl_:
                W2 = [[sb(sl_, "W2_%d_%d" % (k, g), [128, 16, 128], BF16) for g in range(2)] for k in range(2)]
                GW = [sb(sl_, "GW%d" % k, [128, 4, 128], BF16) for k in range(2)]
                XR = sb(sl_, "XR", [128, NTOK + 6], F32)
                X1 = sb(sl_, "X1", [128, NTOK], F32)
                XBF = sb(sl_, "XBF", [128, NTOK], BF16)
                GEL = sb(sl_, "GEL", [128, NLAT], F32)
                A_ = [sb(sl_, "A%d" % k, [128, NTOK], F32) for k in range(2)]
                IA_ = [sb(sl_, "IA%d" % k, [128, NTOK], F32) for k in range(2)]
                HF = sb(sl_, "HF", [128, NTOK], F32)
                HB = sb(sl_, "HB", [128, NTOK], F32)
                T2 = sb(sl_, "T2", [128, NTOK], F32)
                RL = [sb(sl_, "RL%d" % k, [128, NLAT], BF16) for k in range(1)]
                RA_ = [XR, HB]
                xoff = {0: 1, 1: 2052}
                cnt = 0
                for m in range(8):
                    k = m % 2
                    for g in range(2):
                        P.dma("pool", W2[k][g][:], wv[:, :, (3 + g) * 1024 + m * 128:(3 + g) * 1024 + (m + 1) * 128], writes=[("W2", k, g)])
                    P.op("dve", lambda e, k=k: e.memset(GW[k][:], 0.0), writes=[("GW", k)])
                    for dr in range(2):
                        for g in range(2):
                            for nb in range(2):
                                P.dma("pool", GW[k][nb * 64:(nb + 1) * 64, dr * 2 + g, nb * 64:(nb + 1) * 64], lgw_d[dr, g, 2 * m + nb],
                                      reads=[("GW", k)], writes=[("GW", k)])
                    P.op("dve", lambda e: e.memset(XR[:, 0:1], 0.0), writes=["XR"])
                    P.op("dve", lambda e: e.memset(XR[:, 2049:2052], 0.0), writes=["XR"])
                    P.op("dve", lambda e: e.memset(XR[:, 2308:2310], 0.0), writes=["XR"])
                    for ti, (t0, tw, v) in enumerate(tiles):
                        bank = cnt % 4
                        cnt += 1
                        xo = xoff[v] + (t0 - (0 if v == 0 else NLAT))
                        mm_group(ps[bank][:, 0:tw], [(W2[k][0][:, kc, :], H[:, kc, t0:t0 + tw]) for kc in range(16)],
                                 reads=[("W2", k, 0), ("H", ti)], writes=[psk[bank]])
                        P.op("act", lambda e, bank=bank, tw=tw, xo=xo: e.copy(out=XR[:, xo:xo + tw], in_=ps[bank][:, 0:tw]), reads=[psk[bank]], writes=["XR"])
                    for ti, (t0, tw, v) in enumerate(LAT_TILES):
                        mm_group(ps[4 + ti][:, 0:tw], [(W2[k][1][:, kc, :], H[:, kc, t0:t0 + tw]) for kc in range(16)],
                                 reads=[("W2", k, 1), ("H", ti)], writes=[psk[4 + ti]])
                    for (s0, sl, v) in [(0, NLAT, 0), (NLAT, NCTX, 1)]:
                        xo = xoff[v] - 1
                        P.op("act", lambda e, s0=s0, sl=sl, xo=xo, m=m: e.activation(out=X1[:, s0:s0 + sl], in_=XR[:, xo:xo + sl], func=AF.Identity,
                                                                                    scale=vcol("lcw", 0 * 8 + m), bias=vcol("lcb", m)),
                             reads=["XR", "VP"], writes=["X1"])
                        for tp in range(1, 4):
                            P.op("dve", lambda e, s0=s0, sl=sl, xo=xo, m=m, tp=tp: e.scalar_tensor_tensor(
                                out=X1[:, s0:s0 + sl], in0=XR[:, xo + tp:xo + tp + sl], scalar=vcol("lcw", tp * 8 + m), in1=X1[:, s0:s0 + sl], op0=ALU.mult, op1=ALU.add),
                                reads=["XR", "X1", "VP"], writes=["X1"])
                    P.op("act", lambda e: e.copy(out=XBF[:], in_=X1[:]), reads=["X1"], writes=["XBF"])
                    for dr in range(2):
                        Hd = HF if dr == 0 else HB
                        hk = "HF" if dr == 0 else "HB"
                        A, IA, RA = A_[dr], IA_[dr], RA_[dr]
                        ak, ik, rk_ = ("A", dr), ("IA", dr), ("XR" if dr == 0 else "HB")
                        for ti, (t0, tw, v) in enumerate(tiles):
                            b1 = (cnt % 2) * 2
                            b2 = b1 + 1
                            cnt += 1
                            mm_group(ps[b1][:, 0:tw], [(GW[k][:, dr * 2 + 0, :], XBF[:, t0:t0 + tw])], reads=[("GW", k), "XBF"], writes=[psk[b1]])
                            mm_group(ps[b2][:, 0:tw], [(GW[k][:, dr * 2 + 1, :], XBF[:, t0:t0 + tw])], reads=[("GW", k), "XBF"], writes=[psk[b2]])
                            P.op("act", lambda e, b1=b1, tw=tw, t0=t0, dr=dr, m=m, RA=RA: e.activation(out=RA[:, t0:t0 + tw], in_=ps[b1][:, 0:tw], func=AF.Sigmoid, bias=vcol("lgb", (dr * 2 + 0) * 8 + m)),
                                 reads=[psk[b1], "VP"], writes=[rk_])
                            P.op("act", lambda e, b2=b2, tw=tw, t0=t0, dr=dr, m=m, IA=IA: e.activation(out=IA[:, t0:t0 + tw], in_=ps[b2][:, 0:tw], func=AF.Sigmoid, bias=vcol("lgb", (dr * 2 + 1) * 8 + m)),
                                 reads=[psk[b2], "VP"], writes=[ik])
                        P.op("act", lambda e, dr=dr, m=m, A=A, RA=RA: e.activation(out=A[:], in_=RA[:, 0:NTOK], func=AF.Exp, scale=LC[:, 0, dr * 8 + m:dr * 8 + m + 1]),
                             reads=[rk_, "LC"], writes=[ak])
                        P.op("act", lambda e, dr=dr, m=m, RA=RA: e.activation(out=T2[:], in_=RA[:, 0:NTOK], func=AF.Exp, scale=LC[:, 1, dr * 8 + m:dr * 8 + m + 1]),
                             reads=[rk_, "LC"], writes=["T2"])
                        P.op("act", lambda e: e.activation(out=T2[:], in_=T2[:], func=AF.Sqrt, scale=-1.0, bias=ONEB[:, 0:1]),
                             reads=["T2", "ONEB"], writes=["T2"])
                        P.op("dve", lambda e, IA=IA: e.tensor_tensor(out=IA[:], in0=IA[:], in1=T2[:], op=ALU.mult), reads=["T2", ik], writes=[ik])
                        P.op("dve", lambda e, IA=IA: e.tensor_tensor(out=IA[:], in0=IA[:], in1=X1[:], op=ALU.mult), reads=[ik, "X1"], writes=[ik])
                        c0, c1 = NLAT, NTOK
                        if dr == 0:
                            P.op("dve", lambda e, Hd=Hd, A=A, IA=IA: e.tensor_tensor_scan(out=Hd[:, c0:c1], data0=A[:, c0:c1], data1=IA[:, c0:c1], initial=0.0, op0=ALU.mult, op1=ALU.add),
                                 reads=[ak, ik], writes=[hk])
                            P.op("dve", lambda e, Hd=Hd, A=A, IA=IA: e.tensor_tensor_scan(out=Hd[:, 0:NLAT], data0=A[:, 0:NLAT], data1=IA[:, 0:NLAT], initial=Hd[:, c1 - 1:c1], op0=ALU.mult, op1=ALU.add),
                                 reads=[ak, ik, hk], writes=[hk])
                        else:
                            P.op("dve", lambda e, Hd=Hd, A=A, IA=IA: e.tensor_tensor_scan(out=Hd[:, c0:c1][:, ::-1], data0=A[:, c0:c1][:, ::-1], data1=IA[:, c0:c1][:, ::-1], initial=0.0, op0=ALU.mult, op1=ALU.add),
                                 reads=[ak, ik], writes=[hk])
                            P.op("dve", lambda e, Hd=Hd, A=A, IA=IA: e.tensor_tensor_scan(out=Hd[:, 0:NLAT][:, ::-1], data0=A[:, 0:NLAT][:, ::-1], data1=IA[:, 0:NLAT][:, ::-1], initial=Hd[:, c0:c0 + 1], op0=ALU.mult, op1=ALU.add),
                                 reads=[ak, ik, hk], writes=[hk])
                    for ti, (t0, tw, v) in enumerate(LAT_TILES):
                        P.op("act", lambda e, ti=ti, tw=tw, t0=t0: e.activation(out=GEL[:, t0:t0 + tw], in_=ps[4 + ti][:, 0:tw], func=AF.Gelu_apprx_tanh),
                             reads=[psk[4 + ti]], writes=["GEL"])
                    P.op("dve", lambda e: e.tensor_tensor(out=HF[:, 0:NLAT], in0=HF[:, 0:NLAT], in1=HB[:, 0:NLAT], op=ALU.add), reads=["HF", "HB"], writes=["HF"])
                    P.op("dve", lambda e, k=k: e.tensor_tensor(out=RL[0][:], in0=HF[:, 0:NLAT], in1=GEL[:], op=ALU.mult), reads=["HF", "GEL"], writes=[("RL", 0)])
                    P.dma("sp", MIXD[1024 + m * 128:1024 + (m + 1) * 128, 0:NLAT], RL[0][:], reads=[("RL", 0)], writes=["MIXD"])
        P.barrier()
        mixer_out(l, odo_d, LAT_TILES)

    def epilogue():
        with contextlib.ExitStack() as st:
            XB = [sb(st, "OXB%d" % i, [128, 16, 128], F32) for i in range(2)]
            OB = [sb(st, "OB%d" % i, [128, D], F32) for i in range(2)]
            for tb in range(16):
                xb, ob = XB[tb % 2], OB[tb % 2]
                P.dma("sp", xb[:], XTv[:, :, tb * 128:(tb + 1) * 128], reads=["XT"], writes=[("OXB", tb % 2)])
                for g in range(4):
                    bank = (tb % 2) * 4 + g
                    for q in range(4):
                        dc = g * 4 + q
                        P.op("pe", lambda e, bank=bank, q=q, dc=dc, xb=xb: e.transpose(
                            out=ps[bank][:, q * 128:(q + 1) * 128], in_=xb[:, dc, :], identity=ident_f[:]),
                            reads=[("OXB", tb % 2), "ident_f"], writes=[psk[bank]])
                    if g % 2:
                        P.op("act", lambda e, bank=bank, g=g, ob=ob: e.copy(out=ob[:, g * 512:(g + 1) * 512], in_=ps[bank][:]), reads=[psk[bank]], writes=[("OB", tb % 2)])
                    else:
                        P.op("dve", lambda e, bank=bank, g=g, ob=ob: e.tensor_copy(out=ob[:, g * 512:(g + 1) * 512], in_=ps[bank][:]), reads=[psk[bank]], writes=[("OB", tb % 2)])
                P.dma("pool", out_d[tb * 128:(tb + 1) * 128, :], ob[:], reads=[("OB", tb % 2)], writes=["out"])
        P.barrier()

    EPSB = nc.alloc_sbuf_tensor("EPSB", [128, 1], F32)
    ONEB = nc.alloc_sbuf_tensor("ONEB", [128, 1], F32)
    P.op("dve", lambda e: e.memset(EPSB[:], EPS), writes=["EPSB"])
    P.op("dve", lambda e: e.memset(ONEB[:], 1.0 + 2.4e-7), writes=["ONEB"])

    stages = build.stages
    with contextlib.ExitStack() as wst:
        WM.extend(sb(wst, "WM%d" % i, [128, 16, 512], BF16) for i in range(2))
        prologue()
        if stages >= 1:
            ffn(0, 0, TILES)
        if stages >= 2:
            even_mixer(0)
        if stages >= 3:
            ffn(0, 1, TILES)
        bg_tasks(72)
        P.barrier()
    if stages >= 4:
        ffn(1, 0, TILES)
    if stages >= 5:
        odd_mixer(1)
    if stages >= 6:
        ffn(1, 1, LAT_TILES)
    epilogue()
    P.emit()
    return P


build.stages = 6


def _rpb_tiles(rpb):
    H = rpb.shape[0]
    out = np.full((H, 20, 2, 64, 8, 64), NEG_INF, np.float32)
    col = np.arange(64)
    cs = np.clip(col - 8, 0, 48)
    kc = col[:, None]
    qc = col[None, :]
    col_in = (kc >= cs[None, :]) & (kc < cs[None, :] + 16)
    dcol = np.clip(kc - qc + 15, 0, 30)
    for i4, grp in [(0, ATT_GROUPS[0]), (1, ATT_GROUPS[1]), (3, ATT_GROUPS[3])]:
        for (chunk, t) in grp:
            for a in range(2):
                for b in range(8):
                    kr = 2 * chunk + a
                    r = 8 * i4 + b
                    r0 = min(max(r - 4, 0), 24)
                    if not (r0 <= kr <= r0 + 7):
                        continue
                    vals = rpb[:, kr - r + 7, :][:, dcol]
                    out[:, t, a, :, b, :] = np.where(col_in[None], vals, np.float32(NEG_INF))
    return np.ascontiguousarray(out.reshape(H, 20, 128, 512))


def _pack_vecs(b, inp):
    rows = np.zeros((NVROWS, 128), np.float32)

    def put(name, arr):
        o, n = VEC_LAYOUT[name]
        rows[o:o + n] = np.asarray(arr, np.float32).reshape(n, 128)

    put("c", inp["c"][b])
    put("c_ctx", inp["c_ctx"])
    put("b_mod0", inp["b_mod"][0])
    put("b_mod1", inp["b_mod"][1])
    for l in range(2):
        for n in range(3):
            put("g%d%d" % (l, n), inp["norm_g"][l, n])
    put("sc_w", inp["sc_w"][0])
    put("sc_b", inp["sc_b"][0])
    put("cc_w", inp["cc_w"][0])
    put("cc_b", inp["cc_b"][0])
    put("ln_g", inp["cc_ln_g"][0])
    put("ln_b", inp["cc_ln_b"][0])
    put("lcw", inp["lru_conv_w"][0])
    put("lcb", inp["lru_conv_b"][0])
    put("lgb", inp["lru_gate_b"][0])
    put("lam", inp["lru_lam"][0])
    put("qg", np.concatenate([inp["q_norm_g"][0], inp["q_norm_g"][0]]))
    put("kg", np.concatenate([inp["k_norm_g"][0], inp["k_norm_g"][0]]))
    return rows


_CACHE = {}


def kernel(**inp):
    inp = {k: np.asarray(v) for k, v in inp.items()}
    if "P" not in _CACHE:
        _CACHE["P"] = build()
    P = _CACHE["P"]
    rpbT = _rpb_tiles(inp["na_rpb"][0])
    shared = {
        "w_mod": inp["w_mod"], "ffn_w_in": inp["ffn_w_in"], "ffn_w_out": inp["ffn_w_out"],
        "ev_w_in": inp["ev_w_in"][0], "ev_w_out": inp["ev_w_out"][0],
        "od_w_in": inp["od_w_in"][0], "od_w_out": inp["od_w_out"][0],
        "lru_gate_w": inp["lru_gate_w"][0], "rpbT": rpbT,
    }
    shared = {k: np.ascontiguousarray(v, dtype=np.float32) for k, v in shared.items()}
    in_maps = []
    for b in range(8):
        m = dict(shared)
        m["x"] = np.ascontiguousarray(inp["x"][b], dtype=np.float32)
        m["ctx"] = np.ascontiguousarray(inp["ctx"][b], dtype=np.float32)
        m["vecs"] = _pack_vecs(b, inp)
        in_maps.append(m)
    res = run_bass_kernel_spmd(P.nc, in_maps, core_ids=list(range(8)))
    return np.stack([np.asarray(r["out"]) for r in res.results], axis=0).astype(np.float32)
```

```python
import contextlib
import numpy as np
import concourse.bass as bass
import concourse.mybir as mybir
from concourse.bass_utils import run_bass_kernel_spmd

F32 = mybir.dt.float32
BF16 = mybir.dt.bfloat16
AF = mybir.ActivationFunctionType
ALU = mybir.AluOpType

ENGS = ("pe", "act", "dve", "pool", "sp")
NDMA = {"sp": 12, "pool": 12, "act": 4}


class Prog:
    def __init__(self):
        self.nc = bass.Bass("TRN2", target_bir_lowering=False)
        nc = self.nc
        self.streams = {e: [] for e in ENGS}
        self.esem = {e: nc.alloc_semaphore("s_" + e) for e in ("pe", "act", "dve", "pool")}
        self.ecount = {e: 0 for e in self.esem}
        self.known = {e: {} for e in ENGS}
        self.dsem = {q: [nc.alloc_semaphore("d_%s%d" % (q, i)) for i in range(n)] for q, n in NDMA.items()}
        self.dval = {q: [0] * n for q, n in NDMA.items()}
        self.dnext = {q: 0 for q in NDMA}
        self.bufs = {}
        self.n_ops = 0

    def _st(self, key):
        s = self.bufs.get(key)
        if s is None:
            s = self.bufs[key] = [{}, {}]
        return s

    def _deps(self, reads, writes):
        deps = {}

        def add(d):
            for s, v in d.items():
                if deps.get(s, 0) < v:
                    deps[s] = v

        for k in reads:
            add(self._st(k)[0])
        for k in writes:
            st = self._st(k)
            add(st[0])
            add(st[1])
        return deps

    def _waits(self, eng, deps):
        kn = self.known[eng]
        pes = self.esem["pe"]
        for s, v in deps.items():
            if kn.get(s, 0) < v:
                if not (eng == "pe" and s is pes):
                    self.streams[eng].append(("wait", s, v))
                kn[s] = v

    def _record(self, sem, val, reads, writes):
        for k in reads:
            st = self._st(k)
            if st[1].get(sem, 0) < val:
                st[1][sem] = val
        for k in writes:
            st = self._st(k)
            st[0] = {sem: val}
            st[1] = {}

    def op(self, eng, fn, reads=(), writes=()):
        self._waits(eng, self._deps(reads, writes))
        self.ecount[eng] += 1
        sem, val = self.esem[eng], self.ecount[eng]
        self.streams[eng].append(("op", fn, sem))
        self._record(sem, val, reads, writes)
        self.n_ops += 1

    def group(self, eng, fns, reads=(), writes=()):
        self._waits(eng, self._deps(reads, writes))
        self.ecount[eng] += 1
        sem, val = self.esem[eng], self.ecount[eng]
        for f in fns[:-1]:
            self.streams[eng].append(("op", f, None))
        self.streams[eng].append(("op", fns[-1], sem))
        self._record(sem, val, reads, writes)
        self.n_ops += len(fns)

    def dma(self, q, out, in_, reads=(), writes=(), **kw):
        self._waits(q, self._deps(reads, writes))
        k = self.dnext[q]
        self.dnext[q] = (k + 1) % len(self.dsem[q])
        sem = self.dsem[q][k]
        prev = self.dval[q][k]
        if prev and self.known[q].get(sem, 0) < prev:
            self.streams[q].append(("wait", sem, prev))
            self.known[q][sem] = prev
        val = prev + 16
        self.dval[q][k] = val
        self.streams[q].append(("dma", out, in_, sem, kw))
        self._record(sem, val, reads, writes)
        self.n_ops += 1

    def barrier(self):
        deps = {self.esem[e]: self.ecount[e] for e in self.esem if self.ecount[e]}
        for q in NDMA:
            for s, v in zip(self.dsem[q], self.dval[q]):
                if v:
                    deps[s] = v
        for e in ENGS:
            kn = self.known[e]
            for s, v in deps.items():
                if kn.get(s, 0) < v:
                    self.streams[e].append(("wait", s, v))
                    kn[s] = v
        self.bufs = {}

    def emit(self):
        nc = self.nc
        streams = self.streams

        def replay(name, eng):
            for it in streams[name]:
                if it[0] == "wait":
                    eng.wait_ge(it[1], it[2])
                elif it[0] == "op":
                    ins = it[1](eng)
                    if it[2] is not None:
                        ins.then_inc(it[2], 1)
                else:
                    _, out, in_, sem, kw = it
                    eng.dma_start(out=out, in_=in_, **kw).then_inc(sem, 16)

        with nc.Block() as block:
            @block.sync
            def _(e):
                replay("sp", e)

            @block.tensor
            def _(e):
                replay("pe", e)

            @block.scalar
            def _(e):
                replay("act", e)

            @block.vector
            def _(e):
                replay("dve", e)

            @block.gpsimd
            def _(e):
                replay("pool", e)
        return nc


D = 2048
DFF = 5632
NLAT = 2048
NCTX = 256
NTOK = NLAT + NCTX
EPS = 1e-6
TILES = [(0, 512, 0), (512, 512, 0), (1024, 512, 0), (1536, 512, 0), (2048, 256, 1)]
LAT_TILES = TILES[:4]
NEG_INF = -1e30
STQ = "sp"

VEC_LAYOUT = {}
_off = 0
for _n, _r in [("c", 16), ("c_ctx", 16), ("b_mod0", 144), ("b_mod1", 144),
               ("g00", 16), ("g01", 16), ("g02", 16), ("g10", 16), ("g11", 16), ("g12", 16),
               ("sc_w", 24), ("sc_b", 8), ("cc_w", 248), ("cc_b", 8), ("ln_g", 8), ("ln_b", 8),
               ("lcw", 32), ("lcb", 8), ("lgb", 32), ("lam", 16), ("qg", 1), ("kg", 1)]:
    VEC_LAYOUT[_n] = (_off, _r)
    _off += _r
NVROWS = 896
assert _off <= NVROWS


ATT_GROUPS = [
    [(c, c) for c in range(6)],
    [(2 + c, 6 + c) for c in range(8)],
    [(6 + c, 6 + c) for c in range(8)],
    [(10 + c, 14 + c) for c in range(6)],
]


def build(debug_outs=False):
    P = Prog()
    nc = P.nc
    uid = [0]

    def din(name, shape, dt=F32):
        return nc.dram_tensor(name, list(shape), dt, kind="ExternalInput").ap()

    x_d = din("x", [NLAT, D])
    ctx_d = din("ctx", [NCTX, D])
    vecs_d = din("vecs", [NVROWS, 128])
    wmod_d = din("w_mod", [2, D, 9 * D])
    fwi_d = din("ffn_w_in", [2, 2, D, 2 * DFF])
    fwo_d = din("ffn_w_out", [2, 2, DFF, D])
    evi_d = din("ev_w_in", [D, 5120])
    evo_d = din("ev_w_out", [D, D])
    odi_d = din("od_w_in", [D, 5120])
    odo_d = din("od_w_out", [D, D])
    lgw_d = din("lru_gate_w", [2, 2, 16, 64, 64])
    rpb_d = din("rpbT", [16, 20, 128, 512])
    out_d = nc.dram_tensor("out", [NLAT, D], F32, kind="ExternalOutput").ap()

    XT = nc.dram_tensor("XT", [D, NTOK], F32).ap()
    ACTD = nc.dram_tensor("ACTD", [DFF, NTOK], BF16).ap()
    MIXD = nc.dram_tensor("MIXD", [D, NTOK], BF16).ap()
    UD = nc.dram_tensor("UD", [1024, NTOK], F32).ap()
    WOB = nc.dram_tensor("WOB", [8, 128, 44, 256], BF16).ap()
    XTv = XT.rearrange("(c p) t -> p c t", p=128)
    ACTDv = ACTD.rearrange("(c p) t -> p c t", p=128)
    MIXDv = MIXD.rearrange("(c p) t -> p c t", p=128)
    UDv = UD.rearrange("(c p) t -> p c t", p=128)

    def sb(stack, name, shape, dt):
        uid[0] += 1
        return stack.enter_context(nc.sbuf_tensor("%s_%d" % (name, uid[0]), list(shape), dt))

    ps = [nc.alloc_psum_tensor("psb%d" % i, [128, 512], F32) for i in range(8)]
    psk = [("ps", i) for i in range(8)]

    ident_f = nc.alloc_sbuf_tensor("ident_f", [128, 128], F32)
    ident_b = nc.alloc_sbuf_tensor("ident_b", [128, 128], BF16)
    ones_b = nc.alloc_sbuf_tensor("ones_b", [128, 128], BF16)
    blk_b = nc.alloc_sbuf_tensor("blk_b", [128, 128], BF16)
    VP = nc.alloc_sbuf_tensor("VP", [128, NVROWS], F32)
    MOD = nc.alloc_sbuf_tensor("MOD", [128, 2, 144, 2], F32)
    AM = nc.alloc_sbuf_tensor("AM", [128, 2, 3, 2, 3, 16], F32)
    S_bf = nc.alloc_sbuf_tensor("S_bf", [128, 16, 2], BF16)
    LC = nc.alloc_sbuf_tensor("LC", [128, 8, 16], F32)
    QK8 = nc.alloc_sbuf_tensor("QK8", [128, 2], F32)

    def vcol(name, i=0, n=1):
        o = VEC_LAYOUT[name][0] + i
        return VP[:, o:o + n]

    def mm_group(out_ap, pairs, reads, writes):
        n = len(pairs)
        fns = [(lambda e, l=l, r=r, i=i: e.matmul(out_ap, lhsT=l, rhs=r, start=(i == 0), stop=(i == n - 1)))
               for i, (l, r) in enumerate(pairs)]
        P.group("pe", fns, reads, writes)

    WM = []
    bg_state = {"next": 0}

    MR = [nc.alloc_sbuf_tensor("MR%d" % i, [2, 512], F32) for i in range(2)]
    pend = []

    def mod_finish():
        if not pend:
            return
        l, nb = pend.pop(0)
        mr = MR[nb % 2]
        for f4 in range(4):
            P.op("pe", lambda e, mr=mr, f4=f4: e.transpose(out=ps[6][:, 2 * f4:2 * f4 + 2], in_=mr[0:2, f4 * 128:(f4 + 1) * 128], identity=ident_f[0:2, 0:2]),
                 reads=[("MR", nb % 2), "ident_f"], writes=[psk[6]])
        bo = VEC_LAYOUT["b_mod%d" % l][0] + nb * 4
        for v in range(2):
            P.op("dve", lambda e, l=l, v=v, bo=bo, nb=nb: e.tensor_tensor(
                out=MOD[:, l, nb * 4:(nb + 1) * 4, v], in0=ps[6][:, v:8:2], in1=VP[:, bo:bo + 4], op=ALU.add),
                reads=[psk[6], "VP"], writes=["MOD"])
        mod_derive(l, nb)

    def mod_task(l, nb):
        w = WM[nb % 2]
        wk = ("WM", nb % 2)
        wvw = wmod_d[l].rearrange("(kc p) n -> p kc n", p=128)
        P.dma("pool", w[:], wvw[:, :, nb * 512:(nb + 1) * 512], writes=[wk])
        mod_finish()
        mm_group(ps[7][0:2, :], [(S_bf[:, kc, :], w[:, kc, :]) for kc in range(16)],
                 reads=[wk, "S_bf"], writes=[psk[7]])
        mr = MR[nb % 2]
        P.op("act", lambda e, mr=mr: e.copy(out=mr[:], in_=ps[7][0:2, :]), reads=[psk[7]], writes=[("MR", nb % 2)])
        pend.append((l, nb))

    def mod_derive(l, nb):
        n = nb // 12
        if nb % 12 == 7:
            go = VEC_LAYOUT["g%d%d" % (l, n)][0]
            for v in range(2):
                sh = MOD[:, l, (3 * n) * 16:(3 * n + 1) * 16, v]
                sc = MOD[:, l, (3 * n + 1) * 16:(3 * n + 2) * 16, v]
                P.op("dve", lambda e, l=l, n=n, v=v, sc=sc, go=go: e.scalar_tensor_tensor(
                    out=AM[:, l, n, v, 0, :], in0=sc, scalar=1.0, in1=VP[:, go:go + 16], op0=ALU.add, op1=ALU.mult),
                    reads=["MOD", "VP"], writes=["AM"])
                P.op("dve", lambda e, l=l, n=n, v=v, sh=sh: e.tensor_copy(out=AM[:, l, n, v, 1, :], in_=sh),
                     reads=["MOD"], writes=["AM"])
        if nb % 12 == 11:
            for v in range(2):
                ga = MOD[:, l, (3 * n + 2) * 16:(3 * n + 3) * 16, v]
                P.op("dve", lambda e, l=l, n=n, v=v, ga=ga: e.tensor_scalar(
                    out=AM[:, l, n, v, 2, :], in0=ga, scalar1=(1.0 if n == 1 else 0.5), scalar2=None, op0=ALU.mult),
                    reads=["MOD"], writes=["AM"])

    def bg_tasks(n):
        for _ in range(n):
            t = bg_state["next"]
            if t >= 72:
                mod_finish()
                return
            bg_state["next"] = t + 1
            mod_task(t // 36, t % 36)

    def prologue():
        with contextlib.ExitStack() as st:
            P.op("pool", lambda e: e.memset(ident_f[:], 0.0), writes=["ident_f"])
            P.op("pool", lambda e: e.affine_select(out=ident_f[:], in_=ident_f[:], pattern=[[-1, 128]],
                                                   compare_op=ALU.not_equal, fill=1.0, base=0, channel_multiplier=1),
                 reads=["ident_f"], writes=["ident_f"])
            P.op("dve", lambda e: e.tensor_copy(out=ident_b[:], in_=ident_f[:]), reads=["ident_f"], writes=["ident_b"])
            P.op("dve", lambda e: e.memset(ones_b[:], 1.0), writes=["ones_b"])
            P.op("dve", lambda e: e.memset(blk_b[:], 0.0), writes=["blk_b"])
            P.op("dve", lambda e: e.memset(blk_b[0:64, 0:64], 1.0), reads=["blk_b"], writes=["blk_b"])
            P.op("dve", lambda e: e.memset(blk_b[64:128, 64:128], 1.0), reads=["blk_b"], writes=["blk_b"])
            VR = sb(st, "VR", [128, 7, 128], F32)
            P.dma("sp", VR[:], vecs_d.rearrange("(b p) c -> p b c", p=128), writes=["VR"])
            for b in range(7):
                P.op("pe", lambda e, b=b: e.transpose(out=ps[b][:, 0:128], in_=VR[:, b, :], identity=ident_f[:]),
                     reads=["VR", "ident_f"], writes=[psk[b]])
                P.op("dve", lambda e, b=b: e.tensor_copy(out=VP[:, b * 128:(b + 1) * 128], in_=ps[b][:, 0:128]),
                     reads=[psk[b]], writes=["VP"])
            SC = sb(st, "SC", [128, 32], F32)
            P.op("act", lambda e: e.activation(out=SC[:], in_=VP[:, 0:32], func=AF.Silu), reads=["VP"], writes=["SC"])
            for v in range(2):
                P.op("dve", lambda e, v=v: e.tensor_copy(out=S_bf[:, :, v], in_=SC[:, v * 16:(v + 1) * 16]),
                     reads=["SC"], writes=["S_bf"])
            bg_tasks(8)
            lo = VEC_LAYOUT["lam"][0]
            e_ = LC[:, 2, :]
            z = LC[:, 3, :]
            z2 = LC[:, 4, :]
            pl = LC[:, 5, :]
            tm = LC[:, 6, :]
            P.op("act", lambda e: e.activation(out=e_, in_=VP[:, lo:lo + 16], func=AF.Exp, scale=-1.0), reads=["VP"], writes=["LC"])
            P.op("dve", lambda e: e.tensor_scalar(out=tm, in0=e_, scalar1=2.0, scalar2=None, op0=ALU.add), reads=["LC"], writes=["LC"])
            P.op("dve", lambda e: e.reciprocal(out=tm, in_=tm), reads=["LC"], writes=["LC"])
            P.op("dve", lambda e: e.tensor_tensor(out=z, in0=e_, in1=tm, op=ALU.mult), reads=["LC"], writes=["LC"])
            P.op("dve", lambda e: e.tensor_tensor(out=z2, in0=z, in1=z, op=ALU.mult), reads=["LC"], writes=["LC"])
            P.op("dve", lambda e: e.tensor_scalar(out=pl, in0=z2, scalar1=1.0 / 13, scalar2=1.0 / 11, op0=ALU.mult, op1=ALU.add), reads=["LC"], writes=["LC"])
            for cf in (1.0 / 9, 1.0 / 7, 1.0 / 5, 1.0 / 3, 1.0):
                P.op("dve", lambda e: e.tensor_tensor(out=pl, in0=pl, in1=z2, op=ALU.mult), reads=["LC"], writes=["LC"])
                P.op("dve", lambda e, cf=cf: e.tensor_scalar(out=pl, in0=pl, scalar1=cf, scalar2=None, op0=ALU.add), reads=["LC"], writes=["LC"])
            P.op("dve", lambda e: e.tensor_tensor(out=pl, in0=pl, in1=z, op=ALU.mult), reads=["LC"], writes=["LC"])
            P.op("dve", lambda e: e.tensor_scalar(out=LC[:, 0, :], in0=pl, scalar1=-16.0, scalar2=None, op0=ALU.mult), reads=["LC"], writes=["LC"])
            P.op("dve", lambda e: e.tensor_scalar(out=LC[:, 1, :], in0=pl, scalar1=-32.0, scalar2=None, op0=ALU.mult), reads=["LC"], writes=["LC"])
            P.op("dve", lambda e: e.tensor_scalar(out=QK8[:, 0:1], in0=vcol("qg"), scalar1=0.125, scalar2=None, op0=ALU.mult), reads=["VP"], writes=["QK8"])
            P.op("dve", lambda e: e.tensor_copy(out=QK8[:, 1:2], in_=vcol("kg")), reads=["VP"], writes=["QK8"])
        P.barrier()

    def norm_phase(st, H, l, n, tiles):
        NXB = 4
        SW = 256
        XS = [sb(st, "NX%d" % i, [128, 16, SW], F32) for i in range(NXB)]
        SQ = [sb(st, "NQ%d" % i, [128, 16, SW], BF16) for i in range(2)]
        TM = [sb(st, "NT%d" % i, [128, SW], F32) for i in range(4)]
        mod_finish()
        subs = []
        for ti, (t0, tw, v) in enumerate(tiles):
            for o in range(0, tw, SW):
                subs.append((ti, t0 + o, v))
        ns = len(subs)

        def load(si):
            ti, t0, v = subs[si]
            b = si % NXB
            P.dma("sp", XS[b][:], XTv[:, :, t0:t0 + SW], reads=["XT"], writes=[("NX", b)])

        def front(si):
            b = si % NXB
            q = si % 2
            bank = si % 4
            P.op("pool", lambda e, b=b, q=q: e.tensor_tensor(out=SQ[q][:, 0:8, :], in0=XS[b][:, 0:8, :], in1=XS[b][:, 0:8, :], op=ALU.mult),
                 reads=[("NX", b)], writes=[("NQ", q, 0)])
            P.op("act", lambda e, b=b, q=q: e.activation(out=SQ[q][:, 8:16, :], in_=XS[b][:, 8:16, :], func=AF.Square),
                 reads=[("NX", b)], writes=[("NQ", q, 1)])
            mm_group(ps[bank][:, 0:SW], [(ones_b[:], SQ[q][:, dc, :]) for dc in range(16)],
                     reads=[("NQ", q, 0), ("NQ", q, 1), "ones_b"], writes=[psk[bank]])

        def mid(si):
            bank = si % 4
            P.op("act", lambda e, bank=bank: e.activation(out=ps[bank][:, 0:SW], in_=ps[bank][:, 0:SW], func=AF.Ln, scale=1.0 / D, bias=EPSB[:, 0:1]),
                 reads=[psk[bank], "EPSB"], writes=[psk[bank]])
            P.op("act", lambda e, bank=bank: e.activation(out=ps[bank][:, 0:SW], in_=ps[bank][:, 0:SW], func=AF.Exp, scale=-0.5),
                 reads=[psk[bank]], writes=[psk[bank]])

        def back(si, dcs):
            ti, t0, v = subs[si]
            b = si % NXB
            bank = si % 4
            for dc in dcs:
                tm = TM[dc % 4]
                P.op("dve", lambda e, tm=tm, b=b, bank=bank, dc=dc: e.tensor_tensor(out=tm[:], in0=XS[b][:, dc, :], in1=ps[bank][:, 0:SW], op=ALU.mult),
                     reads=[("NX", b), psk[bank]], writes=[("NT", dc % 4)])
                if dc % 8 < 3:
                    P.op("dve", lambda e, tm=tm, dc=dc, t0=t0, v=v: e.tensor_scalar(
                        out=H[:, dc, t0:t0 + SW], in0=tm[:], scalar1=AM[:, l, n, v, 0, dc:dc + 1], scalar2=AM[:, l, n, v, 1, dc:dc + 1],
                        op0=ALU.mult, op1=ALU.add),
                        reads=[("NT", dc % 4), "AM"], writes=[("Hw", si, dc)])
                else:
                    P.op("act", lambda e, tm=tm, dc=dc, t0=t0, v=v: e.activation(
                        out=H[:, dc, t0:t0 + SW], in_=tm[:], func=AF.Identity,
                        scale=AM[:, l, n, v, 0, dc:dc + 1], bias=AM[:, l, n, v, 1, dc:dc + 1]),
                        reads=[("NT", dc % 4), "AM"], writes=[("Hw", si, dc)])

        for si in range(min(3, ns)):
            load(si)
        front(0)
        mid(0)
        for si in range(ns):
            if si + 3 < ns:
                load(si + 3)
            if si + 1 < ns:
                front(si + 1)
            back(si, range(0, 8))
            if si + 1 < ns:
                mid(si + 1)
            back(si, range(8, 16))

    def first_norm(st, H):
        l, n = 0, 0
        XB = [sb(st, "XB%d" % i, [128, D], F32) for i in range(2)]
        XS = [sb(st, "XS%d" % i, [128, 16, 128], F32) for i in range(2)]
        SQ = [sb(st, "FQ%d" % i, [128, 16, 128], BF16) for i in range(2)]
        TM = [sb(st, "FT%d" % i, [128, 128], F32) for i in range(4)]
        mod_finish()

        def xload(tb):
            src = x_d[tb * 128:(tb + 1) * 128, :] if tb < 16 else ctx_d[(tb - 16) * 128:(tb - 15) * 128, :]
            P.dma("sp", XB[tb % 2][:], src, writes=[("XB", tb % 2)])

        xload(0)
        for tb in range(18):
            if tb + 1 < 18:
                xload(tb + 1)
            xb, xs = XB[tb % 2], XS[tb % 2]
            v = 0 if tb < 16 else 1
            t0 = tb * 128
            for g in range(4):
                bank = 4 + (tb % 2) * 2 + (g % 2)
                for q in range(4):
                    dc = g * 4 + q
                    P.op("pe", lambda e, bank=bank, q=q, dc=dc, xb=xb: e.transpose(
                        out=ps[bank][:, q * 128:(q + 1) * 128], in_=xb[:, dc * 128:(dc + 1) * 128], identity=ident_f[:]),
                        reads=[("XB", tb % 2), "ident_f"], writes=[psk[bank]])
                if g % 2:
                    P.op("act", lambda e, bank=bank, g=g, xs=xs: e.copy(out=xs[:, g * 4:(g + 1) * 4, :], in_=ps[bank][:].rearrange("p (a b) -> p a b", a=4)),
                         reads=[psk[bank]], writes=[("XS", tb % 2)])
                else:
                    P.op("dve", lambda e, bank=bank, g=g, xs=xs: e.tensor_copy(out=xs[:, g * 4:(g + 1) * 4, :], in_=ps[bank][:].rearrange("p (a b) -> p a b", a=4)),
                         reads=[psk[bank]], writes=[("XS", tb % 2)])
            P.dma("pool", XTv[:, :, t0:t0 + 128], xs[:], reads=[("XS", tb % 2)], writes=["XT"])
            q2 = tb % 2
            sbank = tb % 2
            P.op("act", lambda e, xs=xs, q2=q2: e.activation(out=SQ[q2][:], in_=xs[:], func=AF.Square),
                 reads=[("XS", tb % 2)], writes=[("FQ", q2)])
            mm_group(ps[sbank][:, 0:128], [(ones_b[:], SQ[q2][:, dc, :]) for dc in range(16)],
                     reads=[("FQ", q2), "ones_b"], writes=[psk[sbank]])
            P.op("act", lambda e, sbank=sbank: e.activation(out=ps[sbank][:, 0:128], in_=ps[sbank][:, 0:128], func=AF.Ln, scale=1.0 / D, bias=EPSB[:, 0:1]),
                 reads=[psk[sbank], "EPSB"], writes=[psk[sbank]])
            P.op("act", lambda e, sbank=sbank: e.activation(out=ps[sbank][:, 0:128], in_=ps[sbank][:, 0:128], func=AF.Exp, scale=-0.5),
                 reads=[psk[sbank]], writes=[psk[sbank]])
            for dc in range(16):
                tm = TM[dc % 4]
                P.op("dve", lambda e, tm=tm, xs=xs, sbank=sbank, dc=dc: e.tensor_tensor(out=tm[:], in0=xs[:, dc, :], in1=ps[sbank][:, 0:128], op=ALU.mult),
                     reads=[("XS", tb % 2), psk[sbank]], writes=[("FT", dc % 4)])
                if dc % 8 < 3:
                    P.op("dve", lambda e, tm=tm, dc=dc, t0=t0, v=v: e.tensor_scalar(
                        out=H[:, dc, t0:t0 + 128], in0=tm[:], scalar1=AM[:, l, n, v, 0, dc:dc + 1], scalar2=AM[:, l, n, v, 1, dc:dc + 1],
                        op0=ALU.mult, op1=ALU.add),
                        reads=[("FT", dc % 4), "AM"], writes=[("Hw", tb, dc)])
                else:
                    P.op("act", lambda e, tm=tm, dc=dc, t0=t0, v=v: e.activation(
                        out=H[:, dc, t0:t0 + 128], in_=tm[:], func=AF.Identity,
                        scale=AM[:, l, n, v, 0, dc:dc + 1], bias=AM[:, l, n, v, 1, dc:dc + 1]),
                        reads=[("FT", dc % 4), "AM"], writes=[("Hw", tb, dc)])

    def ffn(l, i, tiles, first=False):
        nrm = 0 if i == 0 else 2
        wi = fwi_d[l, i].rearrange("(kc p) n -> p kc n", p=128)
        wo = fwo_d[l, i].rearrange("(jc p) n -> p jc n", p=128)
        with contextlib.ExitStack() as st:
            H = sb(st, "H", [128, 16, NTOK], BF16)
            with contextlib.ExitStack() as st2:
                if first:
                    first_norm(st2, H)
                else:
                    norm_phase(st2, H, l, nrm, tiles)
            P.barrier()
            hk = [("H", ti) for ti in range(len(tiles))]
            NB = 2
            WG = [sb(st, "WG%d" % k, [128, 16, 512], BF16) for k in range(NB)]
            WU = [sb(st, "WU%d" % k, [128, 16, 512], BF16) for k in range(NB)]
            AS = [sb(st, "AS%d" % k, [128, NTOK], BF16) for k in range(2)]
            SG = [sb(st, "SG%d" % k, [128, 512], F32) for k in range(2)]
            cnt = 0
            for jg in range(11):
                b = jg % NB
                P.dma("pool", WG[b][:], wi[:, :, jg * 512:(jg + 1) * 512], writes=[("WG", b)])
                P.dma("pool", WU[b][:], wi[:, :, DFF + jg * 512:DFF + (jg + 1) * 512], writes=[("WU", b)])
                for f2 in range(8):
                    P.dma("pool", WOB[f2, :, 4 * jg:4 * jg + 4, :],
                          fwo_d[l, i][jg * 512:(jg + 1) * 512, f2 * 256:(f2 + 1) * 256].rearrange("(jc p) c -> p jc c", p=128),
                          writes=["WOB"])
                bg_tasks(2)
                for j4 in range(4):
                    j = jg * 4 + j4
                    a_s = AS[j % 2]
                    for ti, (t0, tw, v) in enumerate(tiles):
                        bg = (cnt % 4) * 2
                        bu = bg + 1
                        sgk = cnt % 2
                        cnt += 1
                        mm_group(ps[bg][:, 0:tw], [(WG[b][:, kc, j4 * 128:(j4 + 1) * 128], H[:, kc, t0:t0 + tw]) for kc in range(16)],
                                 reads=[("WG", b), ("H", ti)], writes=[psk[bg]])
                        mm_group(ps[bu][:, 0:tw], [(WU[b][:, kc, j4 * 128:(j4 + 1) * 128], H[:, kc, t0:t0 + tw]) for kc in range(16)],
                                 reads=[("WU", b), ("H", ti)], writes=[psk[bu]])
                        sg = SG[sgk]
                        P.op("act", lambda e, sg=sg, bg=bg, tw=tw: e.activation(out=sg[:, 0:tw], in_=ps[bg][:, 0:tw], func=AF.Silu),
                             reads=[psk[bg]], writes=[("SG", sgk)])
                        P.op("dve", lambda e, sg=sg, bu=bu, tw=tw, t0=t0, a_s=a_s: e.tensor_tensor(
                            out=a_s[:, t0:t0 + tw], in0=ps[bu][:, 0:tw], in1=sg[:, 0:tw], op=ALU.mult),
                            reads=[psk[bu], ("SG", sgk)], writes=[("AS", j % 2)])
                    n0, n1 = tiles[0][0], tiles[-1][0] + tiles[-1][1]
                    P.dma("sp", ACTD[j * 128:(j + 1) * 128, n0:n1], a_s[:, n0:n1], reads=[("AS", j % 2)], writes=["ACTD"])
        P.barrier()
        if len(tiles) == 5:
            tiles = tiles[-1:] + tiles[:-1]
        with contextlib.ExitStack() as st:
            AT = [sb(st, "AT%d" % k, [128, 44, 512], BF16) for k in range(2)]
            NW = 2 if l == 0 else 3
            WO = [sb(st, "WO%d" % k, [128, 44, 256], BF16) for k in range(NW)]
            XC = [sb(st, "XC%d" % k, [128, 512], F32) for k in range(4)]
            XN = [sb(st, "XN%d" % k, [128, 512], F32) for k in range(4)]
            wcnt = 0
            cnt = 0
            def at_load(ti):
                t0, tw, v = tiles[ti]
                P.dma("sp", AT[ti % 2][:, :, 0:tw], ACTDv[:, :, t0:t0 + tw], reads=["ACTD"], writes=[("AT", ti % 2)])

            at_load(0)
            for ti, (t0, tw, v) in enumerate(tiles):
                at = AT[ti % 2]
                if ti + 1 < len(tiles):
                    at_load(ti + 1)
                for f2 in range(8):
                    wb = wcnt % NW
                    wcnt += 1
                    P.dma("act", WO[wb][:], WOB[f2], reads=["WOB"], writes=[("WO", wb)])
                    for fh in range(2):
                        fo = f2 * 2 + fh
                        k4 = cnt % 4
                        bank = cnt % 8
                        cnt += 1
                        P.dma("sp", XC[k4][:, 0:tw], XT[fo * 128:(fo + 1) * 128, t0:t0 + tw], reads=[("XT", fo, ti)], writes=[("XC", k4)])
                        mm_group(ps[bank][:, 0:tw], [(WO[wb][:, jc, fh * 128:(fh + 1) * 128], at[:, jc, 0:tw]) for jc in range(44)],
                                 reads=[("WO", wb), ("AT", ti % 2)], writes=[psk[bank]])
                        P.op("dve", lambda e, k4=k4, bank=bank, tw=tw, fo=fo, v=v: e.scalar_tensor_tensor(
                            out=XN[k4][:, 0:tw], in0=ps[bank][:, 0:tw], scalar=AM[:, l, nrm, v, 2, fo:fo + 1], in1=XC[k4][:, 0:tw],
                            op0=ALU.mult, op1=ALU.add),
                            reads=[psk[bank], ("XC", k4), "AM"], writes=[("XN", k4)])
                        P.dma("pool", XT[fo * 128:(fo + 1) * 128, t0:t0 + tw], XN[k4][:, 0:tw], reads=[("XN", k4)], writes=[("XT", fo, ti)])
        P.barrier()

    def mixer_out(l, w_d, tiles):
        wv = w_d.rearrange("(kc p) n -> p kc n", p=128)
        with contextlib.ExitStack() as st:
            M = sb(st, "MX", [128, 16, NTOK], BF16)
            for ti, (t0, tw, v) in enumerate(tiles):
                P.dma("sp", M[:, :, t0:t0 + tw], MIXDv[:, :, t0:t0 + tw], reads=["MIXD"], writes=[("MX", ti)])
            NW = 3
            WO = [sb(st, "WOM%d" % k, [128, 16, 256], BF16) for k in range(NW)]
            XC = [sb(st, "XC%d" % k, [128, 512], F32) for k in range(4)]
            XN = [sb(st, "XN%d" % k, [128, 512], F32) for k in range(4)]
            cnt = 0
            for f2 in range(8):
                wb = f2 % NW
                P.dma("pool", WO[wb][:], wv[:, :, f2 * 256:(f2 + 1) * 256], writes=[("WOM", wb)])
                for fh in range(2):
                    fo = f2 * 2 + fh
                    for ti, (t0, tw, v) in enumerate(tiles):
                        k4 = cnt % 4
                        bank = cnt % 8
                        cnt += 1
                        P.dma("act", XC[k4][:, 0:tw], XT[fo * 128:(fo + 1) * 128, t0:t0 + tw], reads=[("XT", fo, ti)], writes=[("XC", k4)])
                        mm_group(ps[bank][:, 0:tw], [(WO[wb][:, kc, fh * 128:(fh + 1) * 128], M[:, kc, t0:t0 + tw]) for kc in range(16)],
                                 reads=[("WOM", wb), ("MX", ti)], writes=[psk[bank]])
                        P.op("dve", lambda e, k4=k4, bank=bank, tw=tw, fo=fo, v=v: e.scalar_tensor_tensor(
                            out=XN[k4][:, 0:tw], in0=ps[bank][:, 0:tw], scalar=AM[:, l, 1, v, 2, fo:fo + 1], in1=XC[k4][:, 0:tw],
                            op0=ALU.mult, op1=ALU.add),
                            reads=[psk[bank], ("XC", k4), "AM"], writes=[("XN", k4)])
                        P.dma("sp", XT[fo * 128:(fo + 1) * 128, t0:t0 + tw], XN[k4][:, 0:tw], reads=[("XN", k4)], writes=[("XT", fo, ti)])
        P.barrier()

    def even_mixer(l):
        tiles = TILES
        wv = evi_d.rearrange("(kc p) n -> p kc n", p=128)
        segs = [(0, NLAT, LAT_TILES), (NLAT, NCTX, TILES[4:])]
        with contextlib.ExitStack() as st:
            H = sb(st, "H", [128, 16, NTOK], BF16)
            with contextlib.ExitStack() as st2:
                norm_phase(st2, H, l, 1, tiles)
            P.barrier()
            W5 = [[sb(st, "W5_%d_%d" % (k, g), [128, 16, 128], BF16) for g in range(5)] for k in range(2)]
            DG = [sb(st, "DG%d" % k, [128, 34, 128], BF16) for k in range(2)]
            CV = sb(st, "CV", [128, NLAT + 2], BF16)
            GL = sb(st, "GL", [128, NLAT + 30], BF16)
            BG = sb(st, "BG", [128, NLAT], F32)
            TS = [sb(st, "TS%d" % k, [128, 512], F32) for k in range(2)]
            YS = [sb(st, "YS%d" % k, [128, 512], BF16) for k in range(2)]
            UO = [sb(st, "UO%d" % k, [128, 512], F32) for k in range(2)]
            cnt = 0
            for m in range(8):
                k = m % 2
                for g in range(5):
                    P.dma("pool", W5[k][g][:], wv[:, :, g * 1024 + m * 128:g * 1024 + (m + 1) * 128], writes=[("W5", k, g)])
                bg_tasks(1)
                for tp in range(34):
                    col = vcol("sc_w", tp * 8 + m) if tp < 3 else vcol("cc_w", (tp - 3) * 8 + m)
                    P.op("dve", lambda e, k=k, tp=tp, col=col: e.tensor_scalar(out=DG[k][:, tp, :], in0=ident_f[:], scalar1=col, scalar2=None, op0=ALU.mult),
                         reads=["ident_f", "VP"], writes=[("DG", k)])
                for (s0, sl, stiles) in segs:
                    P.op("dve", lambda e: e.memset(CV[:, 0:1], 0.0), writes=["CV"])
                    P.op("dve", lambda e, sl=sl: e.memset(CV[:, sl + 1:sl + 2], 0.0), writes=["CV"])
                    P.op("dve", lambda e: e.memset(GL[:, 0:15], 0.0), writes=["GL"])
                    P.op("dve", lambda e, sl=sl: e.memset(GL[:, sl + 15:sl + 30], 0.0), writes=["GL"])
                    for (t0, tw, v) in stiles:
                        ti = [x[0] for x in TILES].index(t0)
                        lt = t0 - s0
                        bs = [(cnt * 5 + q) % 8 for q in range(5)]
                        cnt += 1
                        for g in range(5):
                            mm_group(ps[bs[g]][:, 0:tw], [(W5[k][g][:, kc, :], H[:, kc, t0:t0 + tw]) for kc in range(16)],
                                     reads=[("W5", k, g), ("H", ti)], writes=[psk[bs[g]]])
                        P.op("act", lambda e, lt=lt, tw=tw, bk=bs[0]: e.copy(out=BG[:, lt:lt + tw], in_=ps[bk][:, 0:tw]),
                             reads=[psk[bs[0]]], writes=["BG"])
                        P.op("act", lambda e, tw=tw, bk=bs[2]: e.copy(out=TS[0][:, 0:tw], in_=ps[bk][:, 0:tw]),
                             reads=[psk[bs[2]]], writes=[("TS", 0)])
                        P.op("dve", lambda e, lt=lt, tw=tw, bk=bs[1]: e.tensor_tensor(out=CV[:, 1 + lt:1 + lt + tw], in0=ps[bk][:, 0:tw], in1=TS[0][:, 0:tw], op=ALU.mult),
                             reads=[psk[bs[1]], ("TS", 0)], writes=["CV"])
                        P.op("act", lambda e, tw=tw, bk=bs[4]: e.activation(out=TS[1][:, 0:tw], in_=ps[bk][:, 0:tw], func=AF.Sigmoid),
                             reads=[psk[bs[4]]], writes=[("TS", 1)])
                        P.op("dve", lambda e, lt=lt, tw=tw, bk=bs[3]: e.tensor_tensor(out=GL[:, 15 + lt:15 + lt + tw], in0=ps[bk][:, 0:tw], in1=TS[1][:, 0:tw], op=ALU.mult),
                             reads=[psk[bs[3]], ("TS", 1)], writes=["GL"])
                    bg_tasks(1)
                    for (t0, tw, v) in stiles:
                        lt = t0 - s0
                        b1 = (cnt * 2) % 8
                        b2 = (cnt * 2 + 1) % 8
                        y = cnt % 2
                        cnt += 1
                        mm_group(ps[b1][:, 0:tw], [(DG[k][:, tp, :], CV[:, lt + tp:lt + tp + tw]) for tp in range(3)],
                                 reads=[("DG", k), "CV"], writes=[psk[b1]])
                        P.op("dve", lambda e, b1=b1, tw=tw, lt=lt, y=y, m=m: e.scalar_tensor_tensor(
                            out=YS[y][:, 0:tw], in0=ps[b1][:, 0:tw], scalar=vcol("sc_b", m), in1=BG[:, lt:lt + tw], op0=ALU.add, op1=ALU.mult),
                            reads=[psk[b1], "BG", "VP"], writes=[("YS", y)])
                        P.dma("sp", MIXD[m * 128:(m + 1) * 128, t0:t0 + tw], YS[y][:, 0:tw], reads=[("YS", y)], writes=["MIXD"])
                        mm_group(ps[b2][:, 0:tw], [(DG[k][:, 3 + tp, :], GL[:, lt + tp:lt + tp + tw]) for tp in range(31)],
                                 reads=[("DG", k), "GL"], writes=[psk[b2]])
                        P.op("act", lambda e, b2=b2, tw=tw, y=y, m=m: e.activation(out=UO[y][:, 0:tw], in_=ps[b2][:, 0:tw], func=AF.Identity, bias=vcol("cc_b", m)),
                             reads=[psk[b2], "VP"], writes=[("UO", y)])
                        P.dma("sp", UD[m * 128:(m + 1) * 128, t0:t0 + tw], UO[y][:, 0:tw], reads=[("UO", y)], writes=["UD"])
                    if s0 == 0:
                        bg_tasks(1)
        P.barrier()
        with contextlib.ExitStack() as st:
            UT = [sb(st, "UT%d" % k, [128, 8, 512], F32) for k in range(3)]
            UB = [sb(st, "UB%d" % k, [128, 8, 512], BF16) for k in range(1)]
            UQ = [sb(st, "UQ%d" % k, [128, 8, 512], BF16) for k in range(1)]
            MSQ = [sb(st, "MSQ%d" % k, [128, 512], F32) for k in range(2)]
            T1 = [sb(st, "T1_%d" % k, [128, 512], F32) for k in range(4)]
            YC = [sb(st, "YC%d" % k, [128, 8, 512], BF16) for k in range(2)]
            nt = len(tiles)

            def lload(ti):
                t0, tw, v = tiles[ti]
                u3 = ti % 3
                P.dma("sp", UT[u3][:, :, 0:tw], UDv[:, :, t0:t0 + tw], reads=["UD"], writes=[("UT", u3)])

            def lfront(ti):
                t0, tw, v = tiles[ti]
                b = ti % 2
                u3 = ti % 3
                ut, ub, uq = UT[u3], UB[0], UQ[0]
                b1, b2 = b * 2, b * 2 + 1
                P.op("act", lambda e, ut=ut, ub=ub, tw=tw: e.copy(out=ub[:, :, 0:tw], in_=ut[:, :, 0:tw]), reads=[("UT", u3)], writes=[("UB", 0)])
                P.op("act", lambda e, ut=ut, uq=uq, tw=tw: e.activation(out=uq[:, :, 0:tw], in_=ut[:, :, 0:tw], func=AF.Square), reads=[("UT", u3)], writes=[("UQ", 0)])
                mm_group(ps[b1][:, 0:tw], [(ones_b[:], ub[:, mc, 0:tw]) for mc in range(8)], reads=[("UB", 0), "ones_b"], writes=[psk[b1]])
                mm_group(ps[b2][:, 0:tw], [(ones_b[:], uq[:, mc, 0:tw]) for mc in range(8)], reads=[("UQ", 0), "ones_b"], writes=[psk[b2]])

            def lmid(ti):
                t0, tw, v = tiles[ti]
                b = ti % 2
                b1, b2 = b * 2, b * 2 + 1
                msq = MSQ[b]
                P.op("dve", lambda e, b1=b1, tw=tw: e.tensor_scalar(out=ps[b1][:, 0:tw], in0=ps[b1][:, 0:tw], scalar1=1.0 / 1024, scalar2=None, op0=ALU.mult),
                     reads=[psk[b1]], writes=[psk[b1]])
                P.op("act", lambda e, msq=msq, b1=b1, tw=tw: e.activation(out=msq[:, 0:tw], in_=ps[b1][:, 0:tw], func=AF.Square),
                     reads=[psk[b1]], writes=[("MSQ", b)])
                P.op("dve", lambda e, msq=msq, b2=b2, tw=tw: e.scalar_tensor_tensor(out=ps[b2][:, 0:tw], in0=ps[b2][:, 0:tw], scalar=1.0 / 1024, in1=msq[:, 0:tw], op0=ALU.mult, op1=ALU.subtract),
                     reads=[psk[b2], ("MSQ", b)], writes=[psk[b2]])
                P.op("act", lambda e, b2=b2, tw=tw: e.activation(out=ps[b2][:, 0:tw], in_=ps[b2][:, 0:tw], func=AF.Ln, bias=EPSB[:, 0:1]), reads=[psk[b2], "EPSB"], writes=[psk[b2]])
                P.op("act", lambda e, b2=b2, tw=tw: e.activation(out=ps[b2][:, 0:tw], in_=ps[b2][:, 0:tw], func=AF.Exp, scale=-0.5), reads=[psk[b2]], writes=[psk[b2]])

            def lback(ti, mcs):
                t0, tw, v = tiles[ti]
                b = ti % 2
                b1, b2 = b * 2, b * 2 + 1
                u3 = ti % 3
                ut, yc = UT[u3], YC[b]
                for mc in mcs:
                    t1 = T1[mc % 4]
                    P.op("dve", lambda e, t1=t1, ut=ut, b1=b1, mc=mc, tw=tw: e.tensor_tensor(out=t1[:, 0:tw], in0=ut[:, mc, 0:tw], in1=ps[b1][:, 0:tw], op=ALU.subtract),
                         reads=[("UT", u3), psk[b1]], writes=[("T1", mc % 4)])
                    P.op("dve", lambda e, t1=t1, b2=b2, tw=tw: e.tensor_tensor(out=t1[:, 0:tw], in0=t1[:, 0:tw], in1=ps[b2][:, 0:tw], op=ALU.mult),
                         reads=[("T1", mc % 4), psk[b2]], writes=[("T1", mc % 4)])
                    P.op("act", lambda e, t1=t1, yc=yc, mc=mc, tw=tw: e.activation(out=yc[:, mc, 0:tw], in_=t1[:, 0:tw], func=AF.Silu, scale=vcol("ln_g", mc), bias=vcol("ln_b", mc)),
                         reads=[("T1", mc % 4), "VP"], writes=[("YC", b)])
                if mcs[-1] == 7:
                    P.dma("sp", MIXDv[:, 8:16, t0:t0 + tw], yc[:, :, 0:tw], reads=[("YC", b)], writes=["MIXD"])

            lload(0)
            if nt > 1:
                lload(1)
            lfront(0)
            lmid(0)
            for ti in range(nt):
                if ti + 2 < nt:
                    lload(ti + 2)
                if ti + 1 < nt:
                    lfront(ti + 1)
                lback(ti, list(range(0, 4)))
                if ti + 1 < nt:
                    lmid(ti + 1)
                lback(ti, list(range(4, 8)))
        P.barrier()
        mixer_out(l, evo_d, tiles)

    def odd_mixer(l):
        tiles = TILES
        wv = odi_d.rearrange("(kc p) n -> p kc n", p=128)
        with contextlib.ExitStack() as st:
            H = sb(st, "H", [128, 16, NTOK], BF16)
            with contextlib.ExitStack() as st2:
                norm_phase(st2, H, l, 1, tiles)
            P.barrier()
            with contextlib.ExitStack() as sa:
                W3 = [[sb(sa, "W3_%d_%d" % (k, g), [128, 16, 128], BF16) for g in range(3)] for k in range(2)]
                QT = sb(sa, "QT", [128, NLAT], BF16)
                KT = sb(sa, "KT", [128, NTOK], BF16)
                V = sb(sa, "V", [128, 18, 128], BF16)
                BT = [sb(sa, "BT%d" % k, [128, 20, 512], BF16) for k in range(2)]
                SQ = [sb(sa, "SQ%d" % k, [128, 512], BF16) for k in range(2)]
                RS = [sb(sa, "RS%d" % k, [128, 512], F32) for k in range(2)]
                TN = [sb(sa, "TN%d" % k, [128, 512], F32) for k in range(2)]
                NPB = 4
                PB = [sb(sa, "PB%d" % k, [128, 512], BF16) for k in range(NPB)]
                RD = [sb(sa, "RD%d" % k, [128, 512], F32) for k in range(2)]
                OS = [sb(sa, "OS%d" % k, [128, 512], BF16) for k in range(2)]
                cq = 0
                LEAD = 2
                for m in range(8):
                    k = m % 2
                    for g in range(3):
                        P.dma("pool", W3[k][g][:], wv[:, :, g * 1024 + m * 128:g * 1024 + (m + 1) * 128], writes=[("W3", k, g)])
                    for hh in range(2):
                        P.dma("pool", BT[hh][:], rpb_d[2 * m + hh].rearrange("t p q -> p t q"), writes=[("BT", hh)])
                    for g, (dst, tl, gcol) in enumerate([(QT, LAT_TILES, 0), (KT, TILES, 1)]):
                        for (t0, tw, v) in tl:
                            ti = [x[0] for x in TILES].index(t0)
                            b = cq % 2
                            bank = (cq % 2) * 2
                            cq += 1
                            mm_group(ps[bank][:, 0:tw], [(W3[k][g][:, kc, :], H[:, kc, t0:t0 + tw]) for kc in range(16)],
                                     reads=[("W3", k, g), ("H", ti)], writes=[psk[bank]])
                            P.op("act", lambda e, b=b, bank=bank, tw=tw: e.activation(out=SQ[b][:, 0:tw], in_=ps[bank][:, 0:tw], func=AF.Square),
                                 reads=[psk[bank]], writes=[("SQ", b)])
                            mm_group(ps[bank + 1][:, 0:tw], [(blk_b[:], SQ[b][:, 0:tw])], reads=[("SQ", b), "blk_b"], writes=[psk[bank + 1]])
                            P.op("act", lambda e, b=b, bank=bank, tw=tw: e.activation(out=RS[b][:, 0:tw], in_=ps[bank + 1][:, 0:tw], func=AF.Ln, scale=1.0 / 64, bias=EPSB[:, 0:1]),
                                 reads=[psk[bank + 1], "EPSB"], writes=[("RS", b)])
                            P.op("act", lambda e, b=b, tw=tw: e.activation(out=RS[b][:, 0:tw], in_=RS[b][:, 0:tw], func=AF.Exp, scale=-0.5),
                                 reads=[("RS", b)], writes=[("RS", b)])
                            P.op("dve", lambda e, b=b, bank=bank, tw=tw: e.tensor_tensor(out=TN[b][:, 0:tw], in0=ps[bank][:, 0:tw], in1=RS[b][:, 0:tw], op=ALU.mult),
                                 reads=[psk[bank], ("RS", b)], writes=[("TN", b)])
                            P.op("act", lambda e, b=b, tw=tw, t0=t0, dst=dst, gcol=gcol: e.activation(out=dst[:, t0:t0 + tw], in_=TN[b][:, 0:tw], func=AF.Identity, scale=QK8[:, gcol:gcol + 1]),
                                 reads=[("TN", b), "QK8"], writes=["QT" if gcol == 0 else "KT"])
                    for tcg in range(5):
                        bank = 4 + tcg % 2
                        n_in = 4 if tcg < 4 else 2
                        for q in range(n_in):
                            tc = tcg * 4 + q
                            ti = min(tc // 4, 4)
                            mm_group(ps[bank][:, q * 128:(q + 1) * 128], [(H[:, kc, tc * 128:(tc + 1) * 128], W3[k][2][:, kc, :]) for kc in range(16)],
                                     reads=[("W3", k, 2), ("H", ti)], writes=[psk[bank]])
                        P.op("act", lambda e, bank=bank, tcg=tcg, n_in=n_in: e.copy(
                            out=V[:, tcg * 4:tcg * 4 + n_in, :], in_=ps[bank][:, 0:n_in * 128].rearrange("p (a b) -> p a b", a=n_in)),
                            reads=[psk[bank]], writes=["V"])
                    items = []
                    for i4 in range(4):
                        loc = ATT_GROUPS[i4]
                        for hh in range(2):
                            n = len(loc) + 2
                            for ci, (kc, et) in enumerate(loc + [(16, None), (17, None)]):
                                items.append((i4, hh, kc, et, ci == 0, ci == n - 1))
                    for idx in range(len(items) + LEAD):
                        if idx < len(items):
                            i4, hh, kc, et, first, last = items[idx]
                            pb = hh * 64
                            sbank = idx % 3
                            pk = idx % NPB
                            q0 = i4 * 512
                            pairs = []
                            rd = ["KT", "QT"]
                            if et is not None:
                                pairs.append((ident_b[:], BT[hh][:, et, :]))
                                rd += ["ident_b", ("BT", hh)]
                            fns = []
                            if et is not None:
                                fns.append(lambda e, sbank=sbank, hh=hh, et=et: e.matmul(ps[sbank][:], lhsT=ident_b[:], rhs=BT[hh][:, et, :], start=True, stop=False))
                            fns.append(lambda e, sbank=sbank, pb=pb, kc=kc, q0=q0, st_=(et is None): e.matmul(
                                ps[sbank][:], lhsT=KT[pb:pb + 64, kc * 128:(kc + 1) * 128], rhs=QT[pb:pb + 64, q0:q0 + 512], start=st_, stop=True))
                            P.group("pe", fns, reads=rd, writes=[psk[sbank]])
                            P.op("act", lambda e, sbank=sbank, pk=pk: e.activation(out=PB[pk][:], in_=ps[sbank][:], func=AF.Exp),
                                 reads=[psk[sbank]], writes=[("PB", pk)])
                        j = idx - LEAD
                        if j >= 0:
                            i4, hh, kc, et, first, last = items[j]
                            pb = hh * 64
                            pk = j % NPB
                            ob, db = (3, 4) if i4 % 2 == 0 else (5, 6)
                            P.group("pe", [
                                lambda e, ob=ob, pb=pb, kc=kc, pk=pk, first=first, last=last: e.matmul(
                                    ps[ob][pb:pb + 64, :], lhsT=V[:, kc, pb:pb + 64], rhs=PB[pk][:], start=first, stop=last),
                                lambda e, db=db, pb=pb, pk=pk, first=first, last=last: e.matmul(
                                    ps[db][pb:pb + 64, :], lhsT=ones_b[:, 0:64], rhs=PB[pk][:], start=first, stop=last),
                            ], reads=["V", ("PB", pk), "ones_b"], writes=[psk[ob], psk[db]])
                            if last and hh == 1:
                                rk = i4 % 2
                                P.op("dve", lambda e, rk=rk, db=db: e.reciprocal(out=RD[rk][:], in_=ps[db][:]), reads=[psk[db]], writes=[("RD", rk)])
                                P.op("dve", lambda e, rk=rk, ob=ob: e.tensor_tensor(out=OS[rk][:], in0=ps[ob][:], in1=RD[rk][:], op=ALU.mult),
                                     reads=[psk[ob], ("RD", rk)], writes=[("OS", rk)])
                                P.dma("sp", MIXD[m * 128:(m + 1) * 128, i4 * 512:(i4 + 1) * 512], OS[rk][:], reads=[("OS", rk)], writes=["MIXD"])
            P.barrier()
            with contextlib.ExitStack() as sl_:
                W2 = [[sb(sl_, "W2_%d_%d" % (k, g), [128, 16, 128], BF16) for g in range(2)] for k in range(2)]
                GW = [sb(sl_, "GW%d" % k, [128, 4, 128], BF16) for k in range(2)]
                XR = sb(sl_, "XR", [128, NTOK + 6], F32)
                X1 = sb(sl_, "X1", [128, NTOK], F32)
                XBF = sb(sl_, "XBF", [128, NTOK], BF16)
                GEL = sb(sl_, "GEL", [128, NLAT], F32)
                A_ = [sb(sl_, "A%d" % k, [128, NTOK], F32) for k in range(2)]
                IA_ = [sb(sl_, "IA%d" % k, [128, NTOK], F32) for k in range(2)]
                HF = sb(sl_, "HF", [128, NTOK], F32)
                HB = sb(sl_, "HB", [128, NTOK], F32)
                T2 = sb(sl_, "T2", [128, NTOK], F32)
                RL = [sb(sl_, "RL%d" % k, [128, NLAT], BF16) for k in range(1)]
                RA_ = [XR, HB]
                xoff = {0: 1, 1: 2052}
                cnt = 0
                for m in range(8):
                    k = m % 2
                    for g in range(2):
                        P.dma("pool", W2[k][g][:], wv[:, :, (3 + g) * 1024 + m * 128:(3 + g) * 1024 + (m + 1) * 128], writes=[("W2", k, g)])
                    P.op("dve", lambda e, k=k: e.memset(GW[k][:], 0.0), writes=[("GW", k)])
                    for dr in range(2):
                        for g in range(2):
                            for nb in range(2):
                                P.dma("pool", GW[k][nb * 64:(nb + 1) * 64, dr * 2 + g, nb * 64:(nb + 1) * 64], lgw_d[dr, g, 2 * m + nb],
                                      reads=[("GW", k)], writes=[("GW", k)])
                    P.op("dve", lambda e: e.memset(XR[:, 0:1], 0.0), writes=["XR"])
                    P.op("dve", lambda e: e.memset(XR[:, 2049:2052], 0.0), writes=["XR"])
                    P.op("dve", lambda e: e.memset(XR[:, 2308:2310], 0.0), writes=["XR"])
                    for ti, (t0, tw, v) in enumerate(tiles):
                        bank = cnt % 4
                        cnt += 1
                        xo = xoff[v] + (t0 - (0 if v == 0 else NLAT))
                        mm_group(ps[bank][:, 0:tw], [(W2[k][0][:, kc, :], H[:, kc, t0:t0 + tw]) for kc in range(16)],
                                 reads=[("W2", k, 0), ("H", ti)], writes=[psk[bank]])
                        P.op("act", lambda e, bank=bank, tw=tw, xo=xo: e.copy(out=XR[:, xo:xo + tw], in_=ps[bank][:, 0:tw]), reads=[psk[bank]], writes=["XR"])
                    for ti, (t0, tw, v) in enumerate(LAT_TILES):
                        mm_group(ps[4 + ti][:, 0:tw], [(W2[k][1][:, kc, :], H[:, kc, t0:t0 + tw]) for kc in range(16)],
                                 reads=[("W2", k, 1), ("H", ti)], writes=[psk[4 + ti]])
                    for (s0, sl, v) in [(0, NLAT, 0), (NLAT, NCTX, 1)]:
                        xo = xoff[v] - 1
                        P.op("act", lambda e, s0=s0, sl=sl, xo=xo, m=m: e.activation(out=X1[:, s0:s0 + sl], in_=XR[:, xo:xo + sl], func=AF.Identity,
                                                                                    scale=vcol("lcw", 0 * 8 + m), bias=vcol("lcb", m)),
                             reads=["XR", "VP"], writes=["X1"])
                        for tp in range(1, 4):
                            P.op("dve", lambda e, s0=s0, sl=sl, xo=xo, m=m, tp=tp: e.scalar_tensor_tensor(
                                out=X1[:, s0:s0 + sl], in0=XR[:, xo + tp:xo + tp + sl], scalar=vcol("lcw", tp * 8 + m), in1=X1[:, s0:s0 + sl], op0=ALU.mult, op1=ALU.add),
                                reads=["XR", "X1", "VP"], writes=["X1"])
                    P.op("act", lambda e: e.copy(out=XBF[:], in_=X1[:]), reads=["X1"], writes=["XBF"])
                    for dr in range(2):
                        Hd = HF if dr == 0 else HB
                        hk = "HF" if dr == 0 else "HB"
                        A, IA, RA = A_[dr], IA_[dr], RA_[dr]
                        ak, ik, rk_ = ("A", dr), ("IA", dr), ("XR" if dr == 0 else "HB")
                        for ti, (t0, tw, v) in enumerate(tiles):
                            b1 = (cnt % 2) * 2
                            b2 = b1 + 1
                            cnt += 1
                            mm_group(ps[b1][:, 0:tw], [(GW[k][:, dr * 2 + 0, :], XBF[:, t0:t0 + tw])], reads=[("GW", k), "XBF"], writes=[psk[b1]])
                            mm_group(ps[b2][:, 0:tw], [(GW[k][:, dr * 2 + 1, :], XBF[:, t0:t0 + tw])], reads=[("GW", k), "XBF"], writes=[psk[b2]])
                            P.op("act", lambda e, b1=b1, tw=tw, t0=t0, dr=dr, m=m, RA=RA: e.activation(out=RA[:, t0:t0 + tw], in_=ps[b1][:, 0:tw], func=AF.Sigmoid, bias=vcol("lgb", (dr * 2 + 0) * 8 + m)),
                                 reads=[psk[b1], "VP"], writes=[rk_])
                            P.op("act", lambda e, b2=b2, tw=tw, t0=t0, dr=dr, m=m, IA=IA: e.activation(out=IA[:, t0:t0 + tw], in_=ps[b2][:, 0:tw], func=AF.Sigmoid, bias=vcol("lgb", (dr * 2 + 1) * 8 + m)),
                                 reads=[psk[b2], "VP"], writes=[ik])
                        P.op("act", lambda e, dr=dr, m=m, A=A, RA=RA: e.activation(out=A[:], in_=RA[:, 0:NTOK], func=AF.Exp, scale=LC[:, 0, dr * 8 + m:dr * 8 + m + 1]),
                             reads=[rk_, "LC"], writes=[ak])
                        P.op("act", lambda e, dr=dr, m=m, RA=RA: e.activation(out=T2[:], in_=RA[:, 0:NTOK], func=AF.Exp, scale=LC[:, 1, dr * 8 + m:dr * 8 + m + 1]),
                             reads=[rk_, "LC"], writes=["T2"])
                        P.op("act", lambda e: e.activation(out=T2[:], in_=T2[:], func=AF.Sqrt, scale=-1.0, bias=ONEB[:, 0:1]),
                             reads=["T2", "ONEB"], writes=["T2"])
                        P.op("dve", lambda e, IA=IA: e.tensor_tensor(out=IA[:], in0=IA[:], in1=T2[:], op=ALU.mult), reads=["T2", ik], writes=[ik])
                        P.op("dve", lambda e, IA=IA: e.tensor_tensor(out=IA[:], in0=IA[:], in1=X1[:], op=ALU.mult), reads=[ik, "X1"], writes=[ik])
                        c0, c1 = NLAT, NTOK
                        if dr == 0:
                            P.op("dve", lambda e, Hd=Hd, A=A, IA=IA: e.tensor_tensor_scan(out=Hd[:, c0:c1], data0=A[:, c0:c1], data1=IA[:, c0:c1], initial=0.0, op0=ALU.mult, op1=ALU.add),
                                 reads=[ak, ik], writes=[hk])
                            P.op("dve", lambda e, Hd=Hd, A=A, IA=IA: e.tensor_tensor_scan(out=Hd[:, 0:NLAT], data0=A[:, 0:NLAT], data1=IA[:, 0:NLAT], initial=Hd[:, c1 - 1:c1], op0=ALU.mult, op1=ALU.add),
                                 reads=[ak, ik, hk], writes=[hk])
                        else:
                            P.op("dve", lambda e, Hd=Hd, A=A, IA=IA: e.tensor_tensor_scan(out=Hd[:, c0:c1][:, ::-1], data0=A[:, c0:c1][:, ::-1], data1=IA[:, c0:c1][:, ::-1], initial=0.0, op0=ALU.mult, op1=ALU.add),
                                 reads=[ak, ik], writes=[hk])
                            P.op("dve", lambda e, Hd=Hd, A=A, IA=IA: e.tensor_tensor_scan(out=Hd[:, 0:NLAT][:, ::-1], data0=A[:, 0:NLAT][:, ::-1], data1=IA[:, 0:NLAT][:, ::-1], initial=Hd[:, c0:c0 + 1], op0=ALU.mult, op1=ALU.add),
                                 reads=[ak, ik, hk], writes=[hk])
                    for ti, (t0, tw, v) in enumerate(LAT_TILES):
                        P.op("act", lambda e, ti=ti, tw=tw, t0=t0: e.activation(out=GEL[:, t0:t0 + tw], in_=ps[4 + ti][:, 0:tw], func=AF.Gelu_apprx_tanh),
                             reads=[psk[4 + ti]], writes=["GEL"])
                    P.op("dve", lambda e: e.tensor_tensor(out=HF[:, 0:NLAT], in0=HF[:, 0:NLAT], in1=HB[:, 0:NLAT], op=ALU.add), reads=["HF", "HB"], writes=["HF"])
                    P.op("dve", lambda e, k=k: e.tensor_tensor(out=RL[0][:], in0=HF[:, 0:NLAT], in1=GEL[:], op=ALU.mult), reads=["HF", "GEL"], writes=[("RL", 0)])
                    P.dma("sp", MIXD[1024 + m * 128:1024 + (m + 1) * 128, 0:NLAT], RL[0][:], reads=[("RL", 0)], writes=["MIXD"])
        P.barrier()
        mixer_out(l, odo_d, LAT_TILES)

    def epilogue():
        with contextlib.ExitStack() as st:
            XB = [sb(st, "OXB%d" % i, [128, 16, 128], F32) for i in range(2)]
            OB = [sb(st, "OB%d" % i, [128, D], F32) for i in range(2)]
            for tb in range(16):
                xb, ob = XB[tb % 2], OB[tb % 2]
                P.dma("sp", xb[:], XTv[:, :, tb * 128:(tb + 1) * 128], reads=["XT"], writes=[("OXB", tb % 2)])
                for g in range(4):
                    bank = (tb % 2) * 4 + g
                    for q in range(4):
                        dc = g * 4 + q
                        P.op("pe", lambda e, bank=bank, q=q, dc=dc, xb=xb: e.transpose(
                            out=ps[bank][:, q * 128:(q + 1) * 128], in_=xb[:, dc, :], identity=ident_f[:]),
                            reads=[("OXB", tb % 2), "ident_f"], writes=[psk[bank]])
                    if g % 2:
                        P.op("act", lambda e, bank=bank, g=g, ob=ob: e.copy(out=ob[:, g * 512:(g + 1) * 512], in_=ps[bank][:]), reads=[psk[bank]], writes=[("OB", tb % 2)])
                    else:
                        P.op("dve", lambda e, bank=bank, g=g, ob=ob: e.tensor_copy(out=ob[:, g * 512:(g + 1) * 512], in_=ps[bank][:]), reads=[psk[bank]], writes=[("OB", tb % 2)])
                P.dma("pool", out_d[tb * 128:(tb + 1) * 128, :], ob[:], reads=[("OB", tb % 2)], writes=["out"])
        P.barrier()

    EPSB = nc.alloc_sbuf_tensor("EPSB", [128, 1], F32)
    ONEB = nc.alloc_sbuf_tensor("ONEB", [128, 1], F32)
    P.op("dve", lambda e: e.memset(EPSB[:], EPS), writes=["EPSB"])
    P.op("dve", lambda e: e.memset(ONEB[:], 1.0 + 2.4e-7), writes=["ONEB"])

    stages = build.stages
    with contextlib.ExitStack() as wst:
        WM.extend(sb(wst, "WM%d" % i, [128, 16, 512], BF16) for i in range(2))
        prologue()
        if stages >= 1:
            ffn(0, 0, TILES, first=True)
        if stages >= 2:
            even_mixer(0)
        if stages >= 3:
            ffn(0, 1, TILES)
        bg_tasks(72)
        P.barrier()
    if stages >= 4:
        ffn(1, 0, TILES)
    if stages >= 5:
        odd_mixer(1)
    if stages >= 6:
        ffn(1, 1, LAT_TILES)
    epilogue()
    P.emit()
    return P


build.stages = 6


def _rpb_tiles(rpb):
    H = rpb.shape[0]
    out = np.full((H, 20, 2, 64, 8, 64), NEG_INF, np.float32)
    col = np.arange(64)
    cs = np.clip(col - 8, 0, 48)
    kc = col[:, None]
    qc = col[None, :]
    col_in = (kc >= cs[None, :]) & (kc < cs[None, :] + 16)
    dcol = np.clip(kc - qc + 15, 0, 30)
    for i4, grp in [(0, ATT_GROUPS[0]), (1, ATT_GROUPS[1]), (3, ATT_GROUPS[3])]:
        for (chunk, t) in grp:
            for a in range(2):
                for b in range(8):
                    kr = 2 * chunk + a
                    r = 8 * i4 + b
                    r0 = min(max(r - 4, 0), 24)
                    if not (r0 <= kr <= r0 + 7):
                        continue
                    vals = rpb[:, kr - r + 7, :][:, dcol]
                    out[:, t, a, :, b, :] = np.where(col_in[None], vals, np.float32(NEG_INF))
    return np.ascontiguousarray(out.reshape(H, 20, 128, 512))


def _pack_vecs(b, inp):
    rows = np.zeros((NVROWS, 128), np.float32)

    def put(name, arr):
        o, n = VEC_LAYOUT[name]
        rows[o:o + n] = np.asarray(arr, np.float32).reshape(n, 128)

    put("c", inp["c"][b])
    put("c_ctx", inp["c_ctx"])
    put("b_mod0", inp["b_mod"][0])
    put("b_mod1", inp["b_mod"][1])
    for l in range(2):
        for n in range(3):
            put("g%d%d" % (l, n), inp["norm_g"][l, n])
    put("sc_w", inp["sc_w"][0])
    put("sc_b", inp["sc_b"][0])
    put("cc_w", inp["cc_w"][0])
    put("cc_b", inp["cc_b"][0])
    put("ln_g", inp["cc_ln_g"][0])
    put("ln_b", inp["cc_ln_b"][0])
    put("lcw", inp["lru_conv_w"][0])
    put("lcb", inp["lru_conv_b"][0])
    put("lgb", inp["lru_gate_b"][0])
    put("lam", inp["lru_lam"][0])
    put("qg", np.concatenate([inp["q_norm_g"][0], inp["q_norm_g"][0]]))
    put("kg", np.concatenate([inp["k_norm_g"][0], inp["k_norm_g"][0]]))
    return rows


_CACHE = {}


def kernel(**inp):
    inp = {k: np.asarray(v) for k, v in inp.items()}
    if "P" not in _CACHE:
        _CACHE["P"] = build()
    P = _CACHE["P"]
    rpbT = _rpb_tiles(inp["na_rpb"][0])
    shared = {
        "w_mod": inp["w_mod"], "ffn_w_in": inp["ffn_w_in"], "ffn_w_out": inp["ffn_w_out"],
        "ev_w_in": inp["ev_w_in"][0], "ev_w_out": inp["ev_w_out"][0],
        "od_w_in": inp["od_w_in"][0], "od_w_out": inp["od_w_out"][0],
        "lru_gate_w": inp["lru_gate_w"][0], "rpbT": rpbT,
    }
    shared = {k: np.ascontiguousarray(v, dtype=np.float32) for k, v in shared.items()}
    in_maps = []
    for b in range(8):
        m = dict(shared)
        m["x"] = np.ascontiguousarray(inp["x"][b], dtype=np.float32)
        m["ctx"] = np.ascontiguousarray(inp["ctx"][b], dtype=np.float32)
        m["vecs"] = _pack_vecs(b, inp)
        in_maps.append(m)
    res = run_bass_kernel_spmd(P.nc, in_maps, core_ids=list(range(8)))
    return np.stack([np.asarray(r["out"]) for r in res.results], axis=0).astype(np.float32)
```

```python
import contextlib
import numpy as np
import concourse.bass as bass
import concourse.mybir as mybir
from concourse.bass_utils import run_bass_kernel_spmd

F32 = mybir.dt.float32
BF16 = mybir.dt.bfloat16
AF = mybir.ActivationFunctionType
ALU = mybir.AluOpType

ENGS = ("pe", "act", "dve", "pool", "sp")
NDMA = {"sp": 12, "pool": 12, "act": 4}


class Prog:
    def __init__(self):
        self.nc = bass.Bass("TRN2", target_bir_lowering=False)
        nc = self.nc
        self.streams = {e: [] for e in ENGS}
        self.esem = {e: nc.alloc_semaphore("s_" + e) for e in ("pe", "act", "dve", "pool")}
        self.ecount = {e: 0 for e in self.esem}
        self.known = {e: {} for e in ENGS}
        self.dsem = {q: [nc.alloc_semaphore("d_%s%d" % (q, i)) for i in range(n)] for q, n in NDMA.items()}
        self.dval = {q: [0] * n for q, n in NDMA.items()}
        self.dnext = {q: 0 for q in NDMA}
        self.bufs = {}
        self.n_ops = 0

    def _st(self, key):
        s = self.bufs.get(key)
        if s is None:
            s = self.bufs[key] = [{}, {}]
        return s

    def _deps(self, reads, writes):
        deps = {}

        def add(d):
            for s, v in d.items():
                if deps.get(s, 0) < v:
                    deps[s] = v

        for k in reads:
            add(self._st(k)[0])
        for k in writes:
            st = self._st(k)
            add(st[0])
            add(st[1])
        return deps

    def _waits(self, eng, deps):
        kn = self.known[eng]
        pes = self.esem["pe"]
        for s, v in deps.items():
            if kn.get(s, 0) < v:
                if not (eng == "pe" and s is pes):
                    self.streams[eng].append(("wait", s, v))
                kn[s] = v

    def _record(self, sem, val, reads, writes):
        for k in reads:
            st = self._st(k)
            if st[1].get(sem, 0) < val:
                st[1][sem] = val
        for k in writes:
            st = self._st(k)
            st[0] = {sem: val}
            st[1] = {}

    def op(self, eng, fn, reads=(), writes=()):
        self._waits(eng, self._deps(reads, writes))
        self.ecount[eng] += 1
        sem, val = self.esem[eng], self.ecount[eng]
        self.streams[eng].append(("op", fn, sem))
        self._record(sem, val, reads, writes)
        self.n_ops += 1

    def group(self, eng, fns, reads=(), writes=()):
        self._waits(eng, self._deps(reads, writes))
        self.ecount[eng] += 1
        sem, val = self.esem[eng], self.ecount[eng]
        for f in fns[:-1]:
            self.streams[eng].append(("op", f, None))
        self.streams[eng].append(("op", fns[-1], sem))
        self._record(sem, val, reads, writes)
        self.n_ops += len(fns)

    def dma(self, q, out, in_, reads=(), writes=(), **kw):
        self._waits(q, self._deps(reads, writes))
        k = self.dnext[q]
        self.dnext[q] = (k + 1) % len(self.dsem[q])
        sem = self.dsem[q][k]
        prev = self.dval[q][k]
        if prev and self.known[q].get(sem, 0) < prev:
            self.streams[q].append(("wait", sem, prev))
            self.known[q][sem] = prev
        val = prev + 16
        self.dval[q][k] = val
        self.streams[q].append(("dma", out, in_, sem, kw))
        self._record(sem, val, reads, writes)
        self.n_ops += 1

    def barrier(self):
        deps = {self.esem[e]: self.ecount[e] for e in self.esem if self.ecount[e]}
        for q in NDMA:
            for s, v in zip(self.dsem[q], self.dval[q]):
                if v:
                    deps[s] = v
        for e in ENGS:
            kn = self.known[e]
            for s, v in deps.items():
                if kn.get(s, 0) < v:
                    self.streams[e].append(("wait", s, v))
                    kn[s] = v
        self.bufs = {}

    def emit(self):
        nc = self.nc
        streams = self.streams

        def replay(name, eng):
            for it in streams[name]:
                if it[0] == "wait":
                    eng.wait_ge(it[1], it[2])
                elif it[0] == "op":
                    ins = it[1](eng)
                    if it[2] is not None:
                        ins.then_inc(it[2], 1)
                else:
                    _, out, in_, sem, kw = it
                    eng.dma_start(out=out, in_=in_, **kw).then_inc(sem, 16)

        with nc.Block() as block:
            @block.sync
            def _(e):
                replay("sp", e)

            @block.tensor
            def _(e):
                replay("pe", e)

            @block.scalar
            def _(e):
                replay("act", e)

            @block.vector
            def _(e):
                replay("dve", e)

            @block.gpsimd
            def _(e):
                replay("pool", e)
        return nc


D = 2048
DFF = 5632
NLAT = 2048
NCTX = 256
NTOK = NLAT + NCTX
EPS = 1e-6
TILES = [(0, 512, 0), (512, 512, 0), (1024, 512, 0), (1536, 512, 0), (2048, 256, 1)]
LAT_TILES = TILES[:4]
NEG_INF = -1e30
STQ = "sp"

VEC_LAYOUT = {}
_off = 0
for _n, _r in [("c", 16), ("c_ctx", 16), ("b_mod0", 144), ("b_mod1", 144),
               ("g00", 16), ("g01", 16), ("g02", 16), ("g10", 16), ("g11", 16), ("g12", 16),
               ("sc_w", 24), ("sc_b", 8), ("cc_w", 248), ("cc_b", 8), ("ln_g", 8), ("ln_b", 8),
               ("lcw", 32), ("lcb", 8), ("lgb", 32), ("lam", 16), ("qg", 1), ("kg", 1)]:
    VEC_LAYOUT[_n] = (_off, _r)
    _off += _r
NVROWS = 896
assert _off <= NVROWS


ATT_GROUPS = [
    [(c, c) for c in range(6)],
    [(2 + c, 6 + c) for c in range(8)],
    [(6 + c, 6 + c) for c in range(8)],
    [(10 + c, 14 + c) for c in range(6)],
]


def build(debug_outs=False):
    P = Prog()
    nc = P.nc
    uid = [0]

    def din(name, shape, dt=F32):
        return nc.dram_tensor(name, list(shape), dt, kind="ExternalInput").ap()

    x_d = din("x", [NLAT, D])
    ctx_d = din("ctx", [NCTX, D])
    vecs_d = din("vecs", [NVROWS, 128])
    wmod_d = din("w_mod", [2, D, 9 * D])
    fwi_d = din("ffn_w_in", [2, 2, D, 2 * DFF])
    fwo_d = din("ffn_w_out", [2, 2, DFF, D])
    evi_d = din("ev_w_in", [D, 5120])
    evo_d = din("ev_w_out", [D, D])
    odi_d = din("od_w_in", [D, 5120])
    odo_d = din("od_w_out", [D, D])
    lgw_d = din("lru_gate_w", [2, 2, 16, 64, 64])
    rpb_d = din("rpbT", [16, 20, 128, 512])
    out_d = nc.dram_tensor("out", [NLAT, D], F32, kind="ExternalOutput").ap()

    XT = nc.dram_tensor("XT", [D, NTOK], F32).ap()
    ACTD = nc.dram_tensor("ACTD", [DFF, NTOK], BF16).ap()
    MIXD = nc.dram_tensor("MIXD", [D, NTOK], BF16).ap()
    UD = nc.dram_tensor("UD", [1024, NTOK], F32).ap()
    WOB = nc.dram_tensor("WOB", [8, 128, 44, 256], BF16).ap()
    XTv = XT.rearrange("(c p) t -> p c t", p=128)
    ACTDv = ACTD.rearrange("(c p) t -> p c t", p=128)
    MIXDv = MIXD.rearrange("(c p) t -> p c t", p=128)
    UDv = UD.rearrange("(c p) t -> p c t", p=128)

    def sb(stack, name, shape, dt):
        uid[0] += 1
        return stack.enter_context(nc.sbuf_tensor("%s_%d" % (name, uid[0]), list(shape), dt))

    ps = [nc.alloc_psum_tensor("psb%d" % i, [128, 512], F32) for i in range(8)]
    psk = [("ps", i) for i in range(8)]

    ident_f = nc.alloc_sbuf_tensor("ident_f", [128, 128], F32)
    ident_b = nc.alloc_sbuf_tensor("ident_b", [128, 128], BF16)
    ones_b = nc.alloc_sbuf_tensor("ones_b", [128, 128], BF16)
    blk_b = nc.alloc_sbuf_tensor("blk_b", [128, 128], BF16)
    VP = nc.alloc_sbuf_tensor("VP", [128, NVROWS], F32)
    MOD = nc.alloc_sbuf_tensor("MOD", [128, 2, 144, 2], F32)
    AM = nc.alloc_sbuf_tensor("AM", [128, 2, 3, 2, 3, 16], F32)
    S_bf = nc.alloc_sbuf_tensor("S_bf", [128, 16, 2], BF16)
    LC = nc.alloc_sbuf_tensor("LC", [128, 8, 16], F32)
    QK8 = nc.alloc_sbuf_tensor("QK8", [128, 2], F32)

    def vcol(name, i=0, n=1):
        o = VEC_LAYOUT[name][0] + i
        return VP[:, o:o + n]

    def mm_group(out_ap, pairs, reads, writes):
        n = len(pairs)
        fns = [(lambda e, l=l, r=r, i=i: e.matmul(out_ap, lhsT=l, rhs=r, start=(i == 0), stop=(i == n - 1)))
               for i, (l, r) in enumerate(pairs)]
        P.group("pe", fns, reads, writes)

    WM = []
    bg_state = {"next": 0}

    MR = [nc.alloc_sbuf_tensor("MR%d" % i, [2, 512], F32) for i in range(2)]
    pend = []

    def mod_finish():
        if not pend:
            return
        l, nb = pend.pop(0)
        mr = MR[nb % 2]
        for f4 in range(4):
            P.op("pe", lambda e, mr=mr, f4=f4: e.transpose(out=ps[6][:, 2 * f4:2 * f4 + 2], in_=mr[0:2, f4 * 128:(f4 + 1) * 128], identity=ident_f[0:2, 0:2]),
                 reads=[("MR", nb % 2), "ident_f"], writes=[psk[6]])
        bo = VEC_LAYOUT["b_mod%d" % l][0] + nb * 4
        for v in range(2):
            P.op("dve", lambda e, l=l, v=v, bo=bo, nb=nb: e.tensor_tensor(
                out=MOD[:, l, nb * 4:(nb + 1) * 4, v], in0=ps[6][:, v:8:2], in1=VP[:, bo:bo + 4], op=ALU.add),
                reads=[psk[6], "VP"], writes=["MOD"])
        mod_derive(l, nb)

    def mod_task(l, nb):
        w = WM[nb % 2]
        wk = ("WM", nb % 2)
        wvw = wmod_d[l].rearrange("(kc p) n -> p kc n", p=128)
        P.dma("pool", w[:], wvw[:, :, nb * 512:(nb + 1) * 512], writes=[wk])
        mod_finish()
        mm_group(ps[7][0:2, :], [(S_bf[:, kc, :], w[:, kc, :]) for kc in range(16)],
                 reads=[wk, "S_bf"], writes=[psk[7]])
        mr = MR[nb % 2]
        P.op("act", lambda e, mr=mr: e.copy(out=mr[:], in_=ps[7][0:2, :]), reads=[psk[7]], writes=[("MR", nb % 2)])
        pend.append((l, nb))

    def mod_derive(l, nb):
        n = nb // 12
        if nb % 12 == 7:
            go = VEC_LAYOUT["g%d%d" % (l, n)][0]
            for v in range(2):
                sh = MOD[:, l, (3 * n) * 16:(3 * n + 1) * 16, v]
                sc = MOD[:, l, (3 * n + 1) * 16:(3 * n + 2) * 16, v]
                P.op("dve", lambda e, l=l, n=n, v=v, sc=sc, go=go: e.scalar_tensor_tensor(
                    out=AM[:, l, n, v, 0, :], in0=sc, scalar=1.0, in1=VP[:, go:go + 16], op0=ALU.add, op1=ALU.mult),
                    reads=["MOD", "VP"], writes=["AM"])
                P.op("dve", lambda e, l=l, n=n, v=v, sh=sh: e.tensor_copy(out=AM[:, l, n, v, 1, :], in_=sh),
                     reads=["MOD"], writes=["AM"])
        if nb % 12 == 11:
            for v in range(2):
                ga = MOD[:, l, (3 * n + 2) * 16:(3 * n + 3) * 16, v]
                P.op("dve", lambda e, l=l, n=n, v=v, ga=ga: e.tensor_scalar(
                    out=AM[:, l, n, v, 2, :], in0=ga, scalar1=(1.0 if n == 1 else 0.5), scalar2=None, op0=ALU.mult),
                    reads=["MOD"], writes=["AM"])

    def bg_tasks(n):
        for _ in range(n):
            t = bg_state["next"]
            if t >= 72:
                mod_finish()
                return
            bg_state["next"] = t + 1
            mod_task(t // 36, t % 36)

    def prologue():
        with contextlib.ExitStack() as st:
            P.op("pool", lambda e: e.memset(ident_f[:], 0.0), writes=["ident_f"])
            P.op("pool", lambda e: e.affine_select(out=ident_f[:], in_=ident_f[:], pattern=[[-1, 128]],
                                                   compare_op=ALU.not_equal, fill=1.0, base=0, channel_multiplier=1),
                 reads=["ident_f"], writes=["ident_f"])
            P.op("dve", lambda e: e.tensor_copy(out=ident_b[:], in_=ident_f[:]), reads=["ident_f"], writes=["ident_b"])
            P.op("dve", lambda e: e.memset(ones_b[:], 1.0), writes=["ones_b"])
            P.op("dve", lambda e: e.memset(blk_b[:], 0.0), writes=["blk_b"])
            P.op("dve", lambda e: e.memset(blk_b[0:64, 0:64], 1.0), reads=["blk_b"], writes=["blk_b"])
            P.op("dve", lambda e: e.memset(blk_b[64:128, 64:128], 1.0), reads=["blk_b"], writes=["blk_b"])
            VR = sb(st, "VR", [128, 7, 128], F32)
            P.dma("sp", VR[:], vecs_d.rearrange("(b p) c -> p b c", p=128), writes=["VR"])
            for b in range(7):
                P.op("pe", lambda e, b=b: e.transpose(out=ps[b][:, 0:128], in_=VR[:, b, :], identity=ident_f[:]),
                     reads=["VR", "ident_f"], writes=[psk[b]])
                P.op("dve", lambda e, b=b: e.tensor_copy(out=VP[:, b * 128:(b + 1) * 128], in_=ps[b][:, 0:128]),
                     reads=[psk[b]], writes=["VP"])
            SC = sb(st, "SC", [128, 32], F32)
            P.op("act", lambda e: e.activation(out=SC[:], in_=VP[:, 0:32], func=AF.Silu), reads=["VP"], writes=["SC"])
            for v in range(2):
                P.op("dve", lambda e, v=v: e.tensor_copy(out=S_bf[:, :, v], in_=SC[:, v * 16:(v + 1) * 16]),
                     reads=["SC"], writes=["S_bf"])
            bg_tasks(8)
            lo = VEC_LAYOUT["lam"][0]
            e_ = LC[:, 2, :]
            z = LC[:, 3, :]
            z2 = LC[:, 4, :]
            pl = LC[:, 5, :]
            tm = LC[:, 6, :]
            P.op("act", lambda e: e.activation(out=e_, in_=VP[:, lo:lo + 16], func=AF.Exp, scale=-1.0), reads=["VP"], writes=["LC"])
            P.op("dve", lambda e: e.tensor_scalar(out=tm, in0=e_, scalar1=2.0, scalar2=None, op0=ALU.add), reads=["LC"], writes=["LC"])
            P.op("dve", lambda e: e.reciprocal(out=tm, in_=tm), reads=["LC"], writes=["LC"])
            P.op("dve", lambda e: e.tensor_tensor(out=z, in0=e_, in1=tm, op=ALU.mult), reads=["LC"], writes=["LC"])
            P.op("dve", lambda e: e.tensor_tensor(out=z2, in0=z, in1=z, op=ALU.mult), reads=["LC"], writes=["LC"])
            P.op("dve", lambda e: e.tensor_scalar(out=pl, in0=z2, scalar1=1.0 / 13, scalar2=1.0 / 11, op0=ALU.mult, op1=ALU.add), reads=["LC"], writes=["LC"])
            for cf in (1.0 / 9, 1.0 / 7, 1.0 / 5, 1.0 / 3, 1.0):
                P.op("dve", lambda e: e.tensor_tensor(out=pl, in0=pl, in1=z2, op=ALU.mult), reads=["LC"], writes=["LC"])
                P.op("dve", lambda e, cf=cf: e.tensor_scalar(out=pl, in0=pl, scalar1=cf, scalar2=None, op0=ALU.add), reads=["LC"], writes=["LC"])
            P.op("dve", lambda e: e.tensor_tensor(out=pl, in0=pl, in1=z, op=ALU.mult), reads=["LC"], writes=["LC"])
            P.op("dve", lambda e: e.tensor_scalar(out=LC[:, 0, :], in0=pl, scalar1=-16.0, scalar2=None, op0=ALU.mult), reads=["LC"], writes=["LC"])
            P.op("dve", lambda e: e.tensor_scalar(out=LC[:, 1, :], in0=pl, scalar1=-32.0, scalar2=None, op0=ALU.mult), reads=["LC"], writes=["LC"])
            P.op("dve", lambda e: e.tensor_scalar(out=QK8[:, 0:1], in0=vcol("qg"), scalar1=0.125, scalar2=None, op0=ALU.mult), reads=["VP"], writes=["QK8"])
            P.op("dve", lambda e: e.tensor_copy(out=QK8[:, 1:2], in_=vcol("kg")), reads=["VP"], writes=["QK8"])
        P.barrier()

    def norm_phase(st, H, l, n, tiles):
        NXB = 4
        SW = 256
        XS = [sb(st, "NX%d" % i, [128, 16, SW], F32) for i in range(NXB)]
        SQ = [sb(st, "NQ%d" % i, [128, 16, SW], BF16) for i in range(2)]
        TM = [sb(st, "NT%d" % i, [128, SW], F32) for i in range(4)]
        mod_finish()
        subs = []
        for ti, (t0, tw, v) in enumerate(tiles):
            for o in range(0, tw, SW):
                subs.append((ti, t0 + o, v))
        ns = len(subs)

        def load(si):
            ti, t0, v = subs[si]
            b = si % NXB
            P.dma("sp", XS[b][:], XTv[:, :, t0:t0 + SW], reads=["XT"], writes=[("NX", b)])

        def front(si):
            b = si % NXB
            q = si % 2
            bank = si % 4
            P.op("pool", lambda e, b=b, q=q: e.tensor_tensor(out=SQ[q][:, 0:8, :], in0=XS[b][:, 0:8, :], in1=XS[b][:, 0:8, :], op=ALU.mult),
                 reads=[("NX", b)], writes=[("NQ", q, 0)])
            P.op("act", lambda e, b=b, q=q: e.activation(out=SQ[q][:, 8:16, :], in_=XS[b][:, 8:16, :], func=AF.Square),
                 reads=[("NX", b)], writes=[("NQ", q, 1)])
            mm_group(ps[bank][:, 0:SW], [(ones_b[:], SQ[q][:, dc, :]) for dc in range(16)],
                     reads=[("NQ", q, 0), ("NQ", q, 1), "ones_b"], writes=[psk[bank]])

        def mid(si):
            bank = si % 4
            P.op("act", lambda e, bank=bank: e.activation(out=ps[bank][:, 0:SW], in_=ps[bank][:, 0:SW], func=AF.Ln, scale=1.0 / D, bias=EPSB[:, 0:1]),
                 reads=[psk[bank], "EPSB"], writes=[psk[bank]])
            P.op("act", lambda e, bank=bank: e.activation(out=ps[bank][:, 0:SW], in_=ps[bank][:, 0:SW], func=AF.Exp, scale=-0.5),
                 reads=[psk[bank]], writes=[psk[bank]])

        def back(si, dcs):
            ti, t0, v = subs[si]
            b = si % NXB
            bank = si % 4
            for dc in dcs:
                tm = TM[dc % 4]
                P.op("dve", lambda e, tm=tm, b=b, bank=bank, dc=dc: e.tensor_tensor(out=tm[:], in0=XS[b][:, dc, :], in1=ps[bank][:, 0:SW], op=ALU.mult),
                     reads=[("NX", b), psk[bank]], writes=[("NT", dc % 4)])
                if dc % 8 < 3:
                    P.op("dve", lambda e, tm=tm, dc=dc, t0=t0, v=v: e.tensor_scalar(
                        out=H[:, dc, t0:t0 + SW], in0=tm[:], scalar1=AM[:, l, n, v, 0, dc:dc + 1], scalar2=AM[:, l, n, v, 1, dc:dc + 1],
                        op0=ALU.mult, op1=ALU.add),
                        reads=[("NT", dc % 4), "AM"], writes=[("Hw", si, dc)])
                else:
                    P.op("act", lambda e, tm=tm, dc=dc, t0=t0, v=v: e.activation(
                        out=H[:, dc, t0:t0 + SW], in_=tm[:], func=AF.Identity,
                        scale=AM[:, l, n, v, 0, dc:dc + 1], bias=AM[:, l, n, v, 1, dc:dc + 1]),
                        reads=[("NT", dc % 4), "AM"], writes=[("Hw", si, dc)])

        for si in range(min(3, ns)):
            load(si)
        front(0)
        mid(0)
        for si in range(ns):
            if si + 3 < ns:
                load(si + 3)
            if si + 1 < ns:
                front(si + 1)
            back(si, range(0, 8))
            if si + 1 < ns:
                mid(si + 1)
            back(si, range(8, 16))

    def first_norm(st, H):
        l, n = 0, 0
        XB = [sb(st, "XB%d" % i, [128, D], F32) for i in range(2)]
        XS = [sb(st, "XS%d" % i, [128, 16, 128], F32) for i in range(3)]
        SQ = [sb(st, "FQ%d" % i, [128, 16, 128], BF16) for i in range(2)]
        TM = [sb(st, "FT%d" % i, [128, 128], F32) for i in range(4)]
        mod_finish()
        NBLK = 18

        def xload(tb):
            src = x_d[tb * 128:(tb + 1) * 128, :] if tb < 16 else ctx_d[(tb - 16) * 128:(tb - 15) * 128, :]
            P.dma("sp", XB[tb % 2][:], src, writes=[("XB", tb % 2)])

        def transposes(tb):
            xb = XB[tb % 2]
            for g in range(4):
                bank = 4 + (tb % 2) * 2 + (g % 2)
                for q in range(4):
                    dc = g * 4 + q
                    P.op("pe", lambda e, bank=bank, q=q, dc=dc, xb=xb: e.transpose(
                        out=ps[bank][:, q * 128:(q + 1) * 128], in_=xb[:, dc * 128:(dc + 1) * 128], identity=ident_f[:]),
                        reads=[("XB", tb % 2), "ident_f"], writes=[psk[bank]])
                yield g, bank

        def copy(tb, g, bank):
            xs = XS[tb % 3]
            if g % 2:
                P.op("act", lambda e, bank=bank, g=g, xs=xs: e.copy(out=xs[:, g * 4:(g + 1) * 4, :], in_=ps[bank][:].rearrange("p (a b) -> p a b", a=4)),
                     reads=[psk[bank]], writes=[("XS", tb % 3)])
            else:
                P.op("dve", lambda e, bank=bank, g=g, xs=xs: e.tensor_copy(out=xs[:, g * 4:(g + 1) * 4, :], in_=ps[bank][:].rearrange("p (a b) -> p a b", a=4)),
                     reads=[psk[bank]], writes=[("XS", tb % 3)])

        def nfront(tb):
            xs = XS[tb % 3]
            q2 = tb % 2
            sbank = tb % 2
            P.op("act", lambda e, xs=xs, q2=q2: e.activation(out=SQ[q2][:], in_=xs[:], func=AF.Square),
                 reads=[("XS", tb % 3)], writes=[("FQ", q2)])
            mm_group(ps[sbank][:, 0:128], [(ones_b[:], SQ[q2][:, dc, :]) for dc in range(16)],
                     reads=[("FQ", q2), "ones_b"], writes=[psk[sbank]])

        def nmid(tb):
            sbank = tb % 2
            P.op("act", lambda e, sbank=sbank: e.activation(out=ps[sbank][:, 0:128], in_=ps[sbank][:, 0:128], func=AF.Ln, scale=1.0 / D, bias=EPSB[:, 0:1]),
                 reads=[psk[sbank], "EPSB"], writes=[psk[sbank]])
            P.op("act", lambda e, sbank=sbank: e.activation(out=ps[sbank][:, 0:128], in_=ps[sbank][:, 0:128], func=AF.Exp, scale=-0.5),
                 reads=[psk[sbank]], writes=[psk[sbank]])

        def nback(tb):
            xs = XS[tb % 3]
            sbank = tb % 2
            v = 0 if tb < 16 else 1
            t0 = tb * 128
            for dc in range(16):
                tm = TM[dc % 4]
                P.op("dve", lambda e, tm=tm, xs=xs, sbank=sbank, dc=dc: e.tensor_tensor(out=tm[:], in0=xs[:, dc, :], in1=ps[sbank][:, 0:128], op=ALU.mult),
                     reads=[("XS", tb % 3), psk[sbank]], writes=[("FT", dc % 4)])
                if dc % 8 < 3:
                    P.op("dve", lambda e, tm=tm, dc=dc, t0=t0, v=v: e.tensor_scalar(
                        out=H[:, dc, t0:t0 + 128], in0=tm[:], scalar1=AM[:, l, n, v, 0, dc:dc + 1], scalar2=AM[:, l, n, v, 1, dc:dc + 1],
                        op0=ALU.mult, op1=ALU.add),
                        reads=[("FT", dc % 4), "AM"], writes=[("Hw", tb, dc)])
                else:
                    P.op("act", lambda e, tm=tm, dc=dc, t0=t0, v=v: e.activation(
                        out=H[:, dc, t0:t0 + 128], in_=tm[:], func=AF.Identity,
                        scale=AM[:, l, n, v, 0, dc:dc + 1], bias=AM[:, l, n, v, 1, dc:dc + 1]),
                        reads=[("FT", dc % 4), "AM"], writes=[("Hw", tb, dc)])

        xload(0)
        for tb in range(NBLK + 1):
            if tb + 1 < NBLK:
                xload(tb + 1)
            if tb >= 1:
                nfront(tb - 1)
            if tb < NBLK:
                for g, bank in transposes(tb):
                    copy(tb, g, bank)
                P.dma("pool", XTv[:, :, tb * 128:(tb + 1) * 128], XS[tb % 3][:], reads=[("XS", tb % 3)], writes=["XT"])
            if tb >= 1:
                nmid(tb - 1)
                nback(tb - 1)

    def ffn(l, i, tiles, first=False):
        nrm = 0 if i == 0 else 2
        wi = fwi_d[l, i].rearrange("(kc p) n -> p kc n", p=128)
        wo = fwo_d[l, i].rearrange("(jc p) n -> p jc n", p=128)
        with contextlib.ExitStack() as st:
            H = sb(st, "H", [128, 16, NTOK], BF16)
            with contextlib.ExitStack() as st2:
                if first:
                    first_norm(st2, H)
                else:
                    norm_phase(st2, H, l, nrm, tiles)
            P.barrier()
            hk = [("H", ti) for ti in range(len(tiles))]
            NB = 2
            WG = [sb(st, "WG%d" % k, [128, 16, 512], BF16) for k in range(NB)]
            WU = [sb(st, "WU%d" % k, [128, 16, 512], BF16) for k in range(NB)]
            AS = [sb(st, "AS%d" % k, [128, NTOK], BF16) for k in range(2)]
            SG = [sb(st, "SG%d" % k, [128, 512], F32) for k in range(2)]
            cnt = 0
            for jg in range(11):
                b = jg % NB
                P.dma("pool", WG[b][:], wi[:, :, jg * 512:(jg + 1) * 512], writes=[("WG", b)])
                P.dma("pool", WU[b][:], wi[:, :, DFF + jg * 512:DFF + (jg + 1) * 512], writes=[("WU", b)])
                for f2 in range(8):
                    P.dma("pool", WOB[f2, :, 4 * jg:4 * jg + 4, :],
                          fwo_d[l, i][jg * 512:(jg + 1) * 512, f2 * 256:(f2 + 1) * 256].rearrange("(jc p) c -> p jc c", p=128),
                          writes=["WOB"])
                bg_tasks(2)
                for j4 in range(4):
                    j = jg * 4 + j4
                    a_s = AS[j % 2]
                    for ti, (t0, tw, v) in enumerate(tiles):
                        bg = (cnt % 4) * 2
                        bu = bg + 1
                        sgk = cnt % 2
                        cnt += 1
                        mm_group(ps[bg][:, 0:tw], [(WG[b][:, kc, j4 * 128:(j4 + 1) * 128], H[:, kc, t0:t0 + tw]) for kc in range(16)],
                                 reads=[("WG", b), ("H", ti)], writes=[psk[bg]])
                        mm_group(ps[bu][:, 0:tw], [(WU[b][:, kc, j4 * 128:(j4 + 1) * 128], H[:, kc, t0:t0 + tw]) for kc in range(16)],
                                 reads=[("WU", b), ("H", ti)], writes=[psk[bu]])
                        sg = SG[sgk]
                        P.op("act", lambda e, sg=sg, bg=bg, tw=tw: e.activation(out=sg[:, 0:tw], in_=ps[bg][:, 0:tw], func=AF.Silu),
                             reads=[psk[bg]], writes=[("SG", sgk)])
                        P.op("dve", lambda e, sg=sg, bu=bu, tw=tw, t0=t0, a_s=a_s: e.tensor_tensor(
                            out=a_s[:, t0:t0 + tw], in0=ps[bu][:, 0:tw], in1=sg[:, 0:tw], op=ALU.mult),
                            reads=[psk[bu], ("SG", sgk)], writes=[("AS", j % 2)])
                    n0, n1 = tiles[0][0], tiles[-1][0] + tiles[-1][1]
                    P.dma("sp", ACTD[j * 128:(j + 1) * 128, n0:n1], a_s[:, n0:n1], reads=[("AS", j % 2)], writes=["ACTD"])
        P.barrier()
        if len(tiles) == 5:
            tiles = tiles[-1:] + tiles[:-1]
        with contextlib.ExitStack() as st:
            AT = [sb(st, "AT%d" % k, [128, 44, 512], BF16) for k in range(2)]
            NW = 2 if l == 0 else 3
            WO = [sb(st, "WO%d" % k, [128, 44, 256], BF16) for k in range(NW)]
            XC = [sb(st, "XC%d" % k, [128, 512], F32) for k in range(4)]
            XN = [sb(st, "XN%d" % k, [128, 512], F32) for k in range(4)]
            wcnt = 0
            cnt = 0
            def at_load(ti):
                t0, tw, v = tiles[ti]
                P.dma("sp", AT[ti % 2][:, :, 0:tw], ACTDv[:, :, t0:t0 + tw], reads=["ACTD"], writes=[("AT", ti % 2)])

            at_load(0)
            for ti, (t0, tw, v) in enumerate(tiles):
                at = AT[ti % 2]
                if ti + 1 < len(tiles):
                    at_load(ti + 1)
                for f2 in range(8):
                    wb = wcnt % NW
                    wcnt += 1
                    P.dma("act", WO[wb][:], WOB[f2], reads=["WOB"], writes=[("WO", wb)])
                    for fh in range(2):
                        fo = f2 * 2 + fh
                        k4 = cnt % 4
                        bank = cnt % 8
                        cnt += 1
                        P.dma("sp", XC[k4][:, 0:tw], XT[fo * 128:(fo + 1) * 128, t0:t0 + tw], reads=[("XT", fo, ti)], writes=[("XC", k4)])
                        mm_group(ps[bank][:, 0:tw], [(WO[wb][:, jc, fh * 128:(fh + 1) * 128], at[:, jc, 0:tw]) for jc in range(44)],
                                 reads=[("WO", wb), ("AT", ti % 2)], writes=[psk[bank]])
                        P.op("dve", lambda e, k4=k4, bank=bank, tw=tw, fo=fo, v=v: e.scalar_tensor_tensor(
                            out=XN[k4][:, 0:tw], in0=ps[bank][:, 0:tw], scalar=AM[:, l, nrm, v, 2, fo:fo + 1], in1=XC[k4][:, 0:tw],
                            op0=ALU.mult, op1=ALU.add),
                            reads=[psk[bank], ("XC", k4), "AM"], writes=[("XN", k4)])
                        P.dma("pool", XT[fo * 128:(fo + 1) * 128, t0:t0 + tw], XN[k4][:, 0:tw], reads=[("XN", k4)], writes=[("XT", fo, ti)])
        P.barrier()

    def mixer_out(l, w_d, tiles):
        wv = w_d.rearrange("(kc p) n -> p kc n", p=128)
        with contextlib.ExitStack() as st:
            M = sb(st, "MX", [128, 16, NTOK], BF16)
            for ti, (t0, tw, v) in enumerate(tiles):
                P.dma("sp", M[:, :, t0:t0 + tw], MIXDv[:, :, t0:t0 + tw], reads=["MIXD"], writes=[("MX", ti)])
            NW = 3
            WO = [sb(st, "WOM%d" % k, [128, 16, 256], BF16) for k in range(NW)]
            XC = [sb(st, "XC%d" % k, [128, 512], F32) for k in range(4)]
            XN = [sb(st, "XN%d" % k, [128, 512], F32) for k in range(4)]
            cnt = 0
            for f2 in range(8):
                wb = f2 % NW
                P.dma("pool", WO[wb][:], wv[:, :, f2 * 256:(f2 + 1) * 256], writes=[("WOM", wb)])
                for fh in range(2):
                    fo = f2 * 2 + fh
                    for ti, (t0, tw, v) in enumerate(tiles):
                        k4 = cnt % 4
                        bank = cnt % 8
                        cnt += 1
                        P.dma("act", XC[k4][:, 0:tw], XT[fo * 128:(fo + 1) * 128, t0:t0 + tw], reads=[("XT", fo, ti)], writes=[("XC", k4)])
                        mm_group(ps[bank][:, 0:tw], [(WO[wb][:, kc, fh * 128:(fh + 1) * 128], M[:, kc, t0:t0 + tw]) for kc in range(16)],
                                 reads=[("WOM", wb), ("MX", ti)], writes=[psk[bank]])
                        P.op("dve", lambda e, k4=k4, bank=bank, tw=tw, fo=fo, v=v: e.scalar_tensor_tensor(
                            out=XN[k4][:, 0:tw], in0=ps[bank][:, 0:tw], scalar=AM[:, l, 1, v, 2, fo:fo + 1], in1=XC[k4][:, 0:tw],
                            op0=ALU.mult, op1=ALU.add),
                            reads=[psk[bank], ("XC", k4), "AM"], writes=[("XN", k4)])
                        P.dma("sp", XT[fo * 128:(fo + 1) * 128, t0:t0 + tw], XN[k4][:, 0:tw], reads=[("XN", k4)], writes=[("XT", fo, ti)])
        P.barrier()

    def even_mixer(l):
        tiles = TILES
        wv = evi_d.rearrange("(kc p) n -> p kc n", p=128)
        segs = [(0, NLAT, LAT_TILES), (NLAT, NCTX, TILES[4:])]
        with contextlib.ExitStack() as st:
            H = sb(st, "H", [128, 16, NTOK], BF16)
            with contextlib.ExitStack() as st2:
                norm_phase(st2, H, l, 1, tiles)
            P.barrier()
            W5 = [[sb(st, "W5_%d_%d" % (k, g), [128, 16, 128], BF16) for g in range(5)] for k in range(2)]
            DG = [sb(st, "DG%d" % k, [128, 34, 128], BF16) for k in range(2)]
            CV = sb(st, "CV", [128, NLAT + 2], BF16)
            GL = sb(st, "GL", [128, NLAT + 30], BF16)
            BG = sb(st, "BG", [128, NLAT], F32)
            TS = [sb(st, "TS%d" % k, [128, 512], F32) for k in range(2)]
            YS = [sb(st, "YS%d" % k, [128, 512], BF16) for k in range(2)]
            UO = [sb(st, "UO%d" % k, [128, 512], F32) for k in range(2)]
            cnt = 0
            for m in range(8):
                k = m % 2
                for g in range(5):
                    P.dma("pool", W5[k][g][:], wv[:, :, g * 1024 + m * 128:g * 1024 + (m + 1) * 128], writes=[("W5", k, g)])
                bg_tasks(1)
                for tp in range(34):
                    col = vcol("sc_w", tp * 8 + m) if tp < 3 else vcol("cc_w", (tp - 3) * 8 + m)
                    P.op("dve", lambda e, k=k, tp=tp, col=col: e.tensor_scalar(out=DG[k][:, tp, :], in0=ident_f[:], scalar1=col, scalar2=None, op0=ALU.mult),
                         reads=["ident_f", "VP"], writes=[("DG", k)])
                for (s0, sl, stiles) in segs:
                    P.op("dve", lambda e: e.memset(CV[:, 0:1], 0.0), writes=["CV"])
                    P.op("dve", lambda e, sl=sl: e.memset(CV[:, sl + 1:sl + 2], 0.0), writes=["CV"])
                    P.op("dve", lambda e: e.memset(GL[:, 0:15], 0.0), writes=["GL"])
                    P.op("dve", lambda e, sl=sl: e.memset(GL[:, sl + 15:sl + 30], 0.0), writes=["GL"])
                    for (t0, tw, v) in stiles:
                        ti = [x[0] for x in TILES].index(t0)
                        lt = t0 - s0
                        bs = [(cnt * 5 + q) % 8 for q in range(5)]
                        cnt += 1
                        for g in range(5):
                            mm_group(ps[bs[g]][:, 0:tw], [(W5[k][g][:, kc, :], H[:, kc, t0:t0 + tw]) for kc in range(16)],
                                     reads=[("W5", k, g), ("H", ti)], writes=[psk[bs[g]]])
                        P.op("act", lambda e, lt=lt, tw=tw, bk=bs[0]: e.copy(out=BG[:, lt:lt + tw], in_=ps[bk][:, 0:tw]),
                             reads=[psk[bs[0]]], writes=["BG"])
                        P.op("act", lambda e, tw=tw, bk=bs[2]: e.copy(out=TS[0][:, 0:tw], in_=ps[bk][:, 0:tw]),
                             reads=[psk[bs[2]]], writes=[("TS", 0)])
                        P.op("dve", lambda e, lt=lt, tw=tw, bk=bs[1]: e.tensor_tensor(out=CV[:, 1 + lt:1 + lt + tw], in0=ps[bk][:, 0:tw], in1=TS[0][:, 0:tw], op=ALU.mult),
                             reads=[psk[bs[1]], ("TS", 0)], writes=["CV"])
                        P.op("act", lambda e, tw=tw, bk=bs[4]: e.activation(out=TS[1][:, 0:tw], in_=ps[bk][:, 0:tw], func=AF.Sigmoid),
                             reads=[psk[bs[4]]], writes=[("TS", 1)])
                        P.op("dve", lambda e, lt=lt, tw=tw, bk=bs[3]: e.tensor_tensor(out=GL[:, 15 + lt:15 + lt + tw], in0=ps[bk][:, 0:tw], in1=TS[1][:, 0:tw], op=ALU.mult),
                             reads=[psk[bs[3]], ("TS", 1)], writes=["GL"])
                    bg_tasks(1)
                    for (t0, tw, v) in stiles:
                        lt = t0 - s0
                        b1 = (cnt * 2) % 8
                        b2 = (cnt * 2 + 1) % 8
                        y = cnt % 2
                        cnt += 1
                        mm_group(ps[b1][:, 0:tw], [(DG[k][:, tp, :], CV[:, lt + tp:lt + tp + tw]) for tp in range(3)],
                                 reads=[("DG", k), "CV"], writes=[psk[b1]])
                        P.op("dve", lambda e, b1=b1, tw=tw, lt=lt, y=y, m=m: e.scalar_tensor_tensor(
                            out=YS[y][:, 0:tw], in0=ps[b1][:, 0:tw], scalar=vcol("sc_b", m), in1=BG[:, lt:lt + tw], op0=ALU.add, op1=ALU.mult),
                            reads=[psk[b1], "BG", "VP"], writes=[("YS", y)])
                        P.dma("sp", MIXD[m * 128:(m + 1) * 128, t0:t0 + tw], YS[y][:, 0:tw], reads=[("YS", y)], writes=["MIXD"])
                        mm_group(ps[b2][:, 0:tw], [(DG[k][:, 3 + tp, :], GL[:, lt + tp:lt + tp + tw]) for tp in range(31)],
                                 reads=[("DG", k), "GL"], writes=[psk[b2]])
                        P.op("act", lambda e, b2=b2, tw=tw, y=y, m=m: e.activation(out=UO[y][:, 0:tw], in_=ps[b2][:, 0:tw], func=AF.Identity, bias=vcol("cc_b", m)),
                             reads=[psk[b2], "VP"], writes=[("UO", y)])
                        P.dma("sp", UD[m * 128:(m + 1) * 128, t0:t0 + tw], UO[y][:, 0:tw], reads=[("UO", y)], writes=["UD"])
                    if s0 == 0:
                        bg_tasks(1)
        P.barrier()
        with contextlib.ExitStack() as st:
            UT = [sb(st, "UT%d" % k, [128, 8, 512], F32) for k in range(3)]
            UB = [sb(st, "UB%d" % k, [128, 8, 512], BF16) for k in range(1)]
            UQ = [sb(st, "UQ%d" % k, [128, 8, 512], BF16) for k in range(1)]
            MSQ = [sb(st, "MSQ%d" % k, [128, 512], F32) for k in range(2)]
            T1 = [sb(st, "T1_%d" % k, [128, 512], F32) for k in range(4)]
            YC = [sb(st, "YC%d" % k, [128, 8, 512], BF16) for k in range(2)]
            nt = len(tiles)

            def lload(ti):
                t0, tw, v = tiles[ti]
                u3 = ti % 3
                P.dma("sp", UT[u3][:, :, 0:tw], UDv[:, :, t0:t0 + tw], reads=["UD"], writes=[("UT", u3)])

            def lfront(ti):
                t0, tw, v = tiles[ti]
                b = ti % 2
                u3 = ti % 3
                ut, ub, uq = UT[u3], UB[0], UQ[0]
                b1, b2 = b * 2, b * 2 + 1
                P.op("act", lambda e, ut=ut, ub=ub, tw=tw: e.copy(out=ub[:, :, 0:tw], in_=ut[:, :, 0:tw]), reads=[("UT", u3)], writes=[("UB", 0)])
                P.op("act", lambda e, ut=ut, uq=uq, tw=tw: e.activation(out=uq[:, :, 0:tw], in_=ut[:, :, 0:tw], func=AF.Square), reads=[("UT", u3)], writes=[("UQ", 0)])
                mm_group(ps[b1][:, 0:tw], [(ones_b[:], ub[:, mc, 0:tw]) for mc in range(8)], reads=[("UB", 0), "ones_b"], writes=[psk[b1]])
                mm_group(ps[b2][:, 0:tw], [(ones_b[:], uq[:, mc, 0:tw]) for mc in range(8)], reads=[("UQ", 0), "ones_b"], writes=[psk[b2]])

            def lmid(ti):
                t0, tw, v = tiles[ti]
                b = ti % 2
                b1, b2 = b * 2, b * 2 + 1
                msq = MSQ[b]
                P.op("dve", lambda e, b1=b1, tw=tw: e.tensor_scalar(out=ps[b1][:, 0:tw], in0=ps[b1][:, 0:tw], scalar1=1.0 / 1024, scalar2=None, op0=ALU.mult),
                     reads=[psk[b1]], writes=[psk[b1]])
                P.op("act", lambda e, msq=msq, b1=b1, tw=tw: e.activation(out=msq[:, 0:tw], in_=ps[b1][:, 0:tw], func=AF.Square),
                     reads=[psk[b1]], writes=[("MSQ", b)])
                P.op("dve", lambda e, msq=msq, b2=b2, tw=tw: e.scalar_tensor_tensor(out=ps[b2][:, 0:tw], in0=ps[b2][:, 0:tw], scalar=1.0 / 1024, in1=msq[:, 0:tw], op0=ALU.mult, op1=ALU.subtract),
                     reads=[psk[b2], ("MSQ", b)], writes=[psk[b2]])
                P.op("act", lambda e, b2=b2, tw=tw: e.activation(out=ps[b2][:, 0:tw], in_=ps[b2][:, 0:tw], func=AF.Ln, bias=EPSB[:, 0:1]), reads=[psk[b2], "EPSB"], writes=[psk[b2]])
                P.op("act", lambda e, b2=b2, tw=tw: e.activation(out=ps[b2][:, 0:tw], in_=ps[b2][:, 0:tw], func=AF.Exp, scale=-0.5), reads=[psk[b2]], writes=[psk[b2]])

            def lback(ti, mcs):
                t0, tw, v = tiles[ti]
                b = ti % 2
                b1, b2 = b * 2, b * 2 + 1
                u3 = ti % 3
                ut, yc = UT[u3], YC[b]
                for mc in mcs:
                    t1 = T1[mc % 4]
                    P.op("dve", lambda e, t1=t1, ut=ut, b1=b1, mc=mc, tw=tw: e.tensor_tensor(out=t1[:, 0:tw], in0=ut[:, mc, 0:tw], in1=ps[b1][:, 0:tw], op=ALU.subtract),
                         reads=[("UT", u3), psk[b1]], writes=[("T1", mc % 4)])
                    P.op("dve", lambda e, t1=t1, b2=b2, tw=tw: e.tensor_tensor(out=t1[:, 0:tw], in0=t1[:, 0:tw], in1=ps[b2][:, 0:tw], op=ALU.mult),
                         reads=[("T1", mc % 4), psk[b2]], writes=[("T1", mc % 4)])
                    P.op("act", lambda e, t1=t1, yc=yc, mc=mc, tw=tw: e.activation(out=yc[:, mc, 0:tw], in_=t1[:, 0:tw], func=AF.Silu, scale=vcol("ln_g", mc), bias=vcol("ln_b", mc)),
                         reads=[("T1", mc % 4), "VP"], writes=[("YC", b)])
                if mcs[-1] == 7:
                    P.dma("sp", MIXDv[:, 8:16, t0:t0 + tw], yc[:, :, 0:tw], reads=[("YC", b)], writes=["MIXD"])

            lload(0)
            if nt > 1:
                lload(1)
            lfront(0)
            lmid(0)
            for ti in range(nt):
                if ti + 2 < nt:
                    lload(ti + 2)
                if ti + 1 < nt:
                    lfront(ti + 1)
                lback(ti, list(range(0, 4)))
                if ti + 1 < nt:
                    lmid(ti + 1)
                lback(ti, list(range(4, 8)))
        P.barrier()
        mixer_out(l, evo_d, tiles)

    def odd_mixer(l):
        tiles = TILES
        wv = odi_d.rearrange("(kc p) n -> p kc n", p=128)
        with contextlib.ExitStack() as st:
            H = sb(st, "H", [128, 16, NTOK], BF16)
            with contextlib.ExitStack() as st2:
                norm_phase(st2, H, l, 1, tiles)
            P.barrier()
            with contextlib.ExitStack() as sa:
                W3 = [[sb(sa, "W3_%d_%d" % (k, g), [128, 16, 128], BF16) for g in range(3)] for k in range(2)]
                QT = sb(sa, "QT", [128, NLAT], BF16)
                KT = sb(sa, "KT", [128, NTOK], BF16)
                V = sb(sa, "V", [128, 18, 128], BF16)
                BT = [sb(sa, "BT%d" % k, [128, 20, 512], BF16) for k in range(2)]
                SQ = [sb(sa, "SQ%d" % k, [128, 512], BF16) for k in range(2)]
                RS = [sb(sa, "RS%d" % k, [128, 512], F32) for k in range(2)]
                TN = [sb(sa, "TN%d" % k, [128, 512], F32) for k in range(2)]
                NPB = 4
                PB = [sb(sa, "PB%d" % k, [128, 512], BF16) for k in range(NPB)]
                RD = [sb(sa, "RD%d" % k, [128, 512], F32) for k in range(2)]
                OS = [sb(sa, "OS%d" % k, [128, 512], BF16) for k in range(2)]
                cq = 0
                LEAD = 2
                for m in range(8):
                    k = m % 2
                    for g in range(3):
                        P.dma("pool", W3[k][g][:], wv[:, :, g * 1024 + m * 128:g * 1024 + (m + 1) * 128], writes=[("W3", k, g)])
                    for hh in range(2):
                        P.dma("pool", BT[hh][:], rpb_d[2 * m + hh].rearrange("t p q -> p t q"), writes=[("BT", hh)])
                    for g, (dst, tl, gcol) in enumerate([(QT, LAT_TILES, 0), (KT, TILES, 1)]):
                        for (t0, tw, v) in tl:
                            ti = [x[0] for x in TILES].index(t0)
                            b = cq % 2
                            bank = (cq % 2) * 2
                            cq += 1
                            mm_group(ps[bank][:, 0:tw], [(W3[k][g][:, kc, :], H[:, kc, t0:t0 + tw]) for kc in range(16)],
                                     reads=[("W3", k, g), ("H", ti)], writes=[psk[bank]])
                            P.op("act", lambda e, b=b, bank=bank, tw=tw: e.activation(out=SQ[b][:, 0:tw], in_=ps[bank][:, 0:tw], func=AF.Square),
                                 reads=[psk[bank]], writes=[("SQ", b)])
                            mm_group(ps[bank + 1][:, 0:tw], [(blk_b[:], SQ[b][:, 0:tw])], reads=[("SQ", b), "blk_b"], writes=[psk[bank + 1]])
                            P.op("act", lambda e, b=b, bank=bank, tw=tw: e.activation(out=RS[b][:, 0:tw], in_=ps[bank + 1][:, 0:tw], func=AF.Ln, scale=1.0 / 64, bias=EPSB[:, 0:1]),
                                 reads=[psk[bank + 1], "EPSB"], writes=[("RS", b)])
                            P.op("act", lambda e, b=b, tw=tw: e.activation(out=RS[b][:, 0:tw], in_=RS[b][:, 0:tw], func=AF.Exp, scale=-0.5),
                                 reads=[("RS", b)], writes=[("RS", b)])
                            P.op("dve", lambda e, b=b, bank=bank, tw=tw: e.tensor_tensor(out=TN[b][:, 0:tw], in0=ps[bank][:, 0:tw], in1=RS[b][:, 0:tw], op=ALU.mult),
                                 reads=[psk[bank], ("RS", b)], writes=[("TN", b)])
                            P.op("act", lambda e, b=b, tw=tw, t0=t0, dst=dst, gcol=gcol: e.activation(out=dst[:, t0:t0 + tw], in_=TN[b][:, 0:tw], func=AF.Identity, scale=QK8[:, gcol:gcol + 1]),
                                 reads=[("TN", b), "QK8"], writes=["QT" if gcol == 0 else "KT"])
                    for tcg in range(5):
                        bank = 4 + tcg % 2
                        n_in = 4 if tcg < 4 else 2
                        for q in range(n_in):
                            tc = tcg * 4 + q
                            ti = min(tc // 4, 4)
                            mm_group(ps[bank][:, q * 128:(q + 1) * 128], [(H[:, kc, tc * 128:(tc + 1) * 128], W3[k][2][:, kc, :]) for kc in range(16)],
                                     reads=[("W3", k, 2), ("H", ti)], writes=[psk[bank]])
                        P.op("act", lambda e, bank=bank, tcg=tcg, n_in=n_in: e.copy(
                            out=V[:, tcg * 4:tcg * 4 + n_in, :], in_=ps[bank][:, 0:n_in * 128].rearrange("p (a b) -> p a b", a=n_in)),
                            reads=[psk[bank]], writes=["V"])
                    items = []
                    for i4 in range(4):
                        loc = ATT_GROUPS[i4]
                        for hh in range(2):
                            n = len(loc) + 2
                            for ci, (kc, et) in enumerate(loc + [(16, None), (17, None)]):
                                items.append((i4, hh, kc, et, ci == 0, ci == n - 1))
                    for idx in range(len(items) + LEAD):
                        if idx < len(items):
                            i4, hh, kc, et, first, last = items[idx]
                            pb = hh * 64
                            sbank = idx % 3
                            pk = idx % NPB
                            q0 = i4 * 512
                            pairs = []
                            rd = ["KT", "QT"]
                            if et is not None:
                                pairs.append((ident_b[:], BT[hh][:, et, :]))
                                rd += ["ident_b", ("BT", hh)]
                            fns = []
                            if et is not None:
                                fns.append(lambda e, sbank=sbank, hh=hh, et=et: e.matmul(ps[sbank][:], lhsT=ident_b[:], rhs=BT[hh][:, et, :], start=True, stop=False))
                            fns.append(lambda e, sbank=sbank, pb=pb, kc=kc, q0=q0, st_=(et is None): e.matmul(
                                ps[sbank][:], lhsT=KT[pb:pb + 64, kc * 128:(kc + 1) * 128], rhs=QT[pb:pb + 64, q0:q0 + 512], start=st_, stop=True))
                            P.group("pe", fns, reads=rd, writes=[psk[sbank]])
                            P.op("act", lambda e, sbank=sbank, pk=pk: e.activation(out=PB[pk][:], in_=ps[sbank][:], func=AF.Exp),
                                 reads=[psk[sbank]], writes=[("PB", pk)])
                        j = idx - LEAD
                        if j >= 0:
                            i4, hh, kc, et, first, last = items[j]
                            pb = hh * 64
                            pk = j % NPB
                            ob, db = (3, 4) if i4 % 2 == 0 else (5, 6)
                            P.group("pe", [
                                lambda e, ob=ob, pb=pb, kc=kc, pk=pk, first=first, last=last: e.matmul(
                                    ps[ob][pb:pb + 64, :], lhsT=V[:, kc, pb:pb + 64], rhs=PB[pk][:], start=first, stop=last),
                                lambda e, db=db, pb=pb, pk=pk, first=first, last=last: e.matmul(
                                    ps[db][pb:pb + 64, :], lhsT=ones_b[:, 0:64], rhs=PB[pk][:], start=first, stop=last),
                            ], reads=["V", ("PB", pk), "ones_b"], writes=[psk[ob], psk[db]])
                            if last and hh == 1:
                                rk = i4 % 2
                                P.op("dve", lambda e, rk=rk, db=db: e.reciprocal(out=RD[rk][:], in_=ps[db][:]), reads=[psk[db]], writes=[("RD", rk)])
                                P.op("dve", lambda e, rk=rk, ob=ob: e.tensor_tensor(out=OS[rk][:], in0=ps[ob][:], in1=RD[rk][:], op=ALU.mult),
                                     reads=[psk[ob], ("RD", rk)], writes=[("OS", rk)])
                                P.dma("sp", MIXD[m * 128:(m + 1) * 128, i4 * 512:(i4 + 1) * 512], OS[rk][:], reads=[("OS", rk)], writes=["MIXD"])
            P.barrier()
            with contextlib.ExitStack() as sl_:
                W2 = [[sb(sl_, "W2_%d_%d" % (k, g), [128, 16, 128], BF16) for g in range(2)] for k in range(2)]
                GW = [sb(sl_, "GW%d" % k, [128, 4, 128], BF16) for k in range(2)]
                XR = sb(sl_, "XR", [128, NTOK + 6], F32)
                X1 = sb(sl_, "X1", [128, NTOK], F32)
                XBF = sb(sl_, "XBF", [128, NTOK], BF16)
                GEL = sb(sl_, "GEL", [128, NLAT], F32)
                A_ = [sb(sl_, "A%d" % k, [128, NTOK], F32) for k in range(2)]
                IA_ = [sb(sl_, "IA%d" % k, [128, NTOK], F32) for k in range(2)]
                HF = sb(sl_, "HF", [128, NTOK], F32)
                HB = sb(sl_, "HB", [128, NTOK], F32)
                T2 = sb(sl_, "T2", [128, NTOK], F32)
                RL = [sb(sl_, "RL%d" % k, [128, NLAT], BF16) for k in range(1)]
                RA_ = [XR, HB]
                xoff = {0: 1, 1: 2052}
                cnt = 0
                for m in range(8):
                    k = m % 2
                    for g in range(2):
                        P.dma("pool", W2[k][g][:], wv[:, :, (3 + g) * 1024 + m * 128:(3 + g) * 1024 + (m + 1) * 128], writes=[("W2", k, g)])
                    P.op("dve", lambda e, k=k: e.memset(GW[k][:], 0.0), writes=[("GW", k)])
                    for dr in range(2):
                        for g in range(2):
                            for nb in range(2):
                                P.dma("pool", GW[k][nb * 64:(nb + 1) * 64, dr * 2 + g, nb * 64:(nb + 1) * 64], lgw_d[dr, g, 2 * m + nb],
                                      reads=[("GW", k)], writes=[("GW", k)])
                    P.op("dve", lambda e: e.memset(XR[:, 0:1], 0.0), writes=["XR"])
                    P.op("dve", lambda e: e.memset(XR[:, 2049:2052], 0.0), writes=["XR"])
                    P.op("dve", lambda e: e.memset(XR[:, 2308:2310], 0.0), writes=["XR"])
                    for ti, (t0, tw, v) in enumerate(tiles):
                        bank = cnt % 4
                        cnt += 1
                        xo = xoff[v] + (t0 - (0 if v == 0 else NLAT))
                        mm_group(ps[bank][:, 0:tw], [(W2[k][0][:, kc, :], H[:, kc, t0:t0 + tw]) for kc in range(16)],
                                 reads=[("W2", k, 0), ("H", ti)], writes=[psk[bank]])
                        P.op("act", lambda e, bank=bank, tw=tw, xo=xo: e.copy(out=XR[:, xo:xo + tw], in_=ps[bank][:, 0:tw]), reads=[psk[bank]], writes=["XR"])
                    for ti, (t0, tw, v) in enumerate(LAT_TILES):
                        mm_group(ps[4 + ti][:, 0:tw], [(W2[k][1][:, kc, :], H[:, kc, t0:t0 + tw]) for kc in range(16)],
                                 reads=[("W2", k, 1), ("H", ti)], writes=[psk[4 + ti]])
                    for (s0, sl, v) in [(0, NLAT, 0), (NLAT, NCTX, 1)]:
                        xo = xoff[v] - 1
                        P.op("act", lambda e, s0=s0, sl=sl, xo=xo, m=m: e.activation(out=X1[:, s0:s0 + sl], in_=XR[:, xo:xo + sl], func=AF.Identity,
                                                                                    scale=vcol("lcw", 0 * 8 + m), bias=vcol("lcb", m)),
                             reads=["XR", "VP"], writes=["X1"])
                        for tp in range(1, 4):
                            P.op("dve", lambda e, s0=s0, sl=sl, xo=xo, m=m, tp=tp: e.scalar_tensor_tensor(
                                out=X1[:, s0:s0 + sl], in0=XR[:, xo + tp:xo + tp + sl], scalar=vcol("lcw", tp * 8 + m), in1=X1[:, s0:s0 + sl], op0=ALU.mult, op1=ALU.add),
                                reads=["XR", "X1", "VP"], writes=["X1"])
                    P.op("act", lambda e: e.copy(out=XBF[:], in_=X1[:]), reads=["X1"], writes=["XBF"])
                    for dr in range(2):
                        Hd = HF if dr == 0 else HB
                        hk = "HF" if dr == 0 else "HB"
                        A, IA, RA = A_[dr], IA_[dr], RA_[dr]
                        ak, ik, rk_ = ("A", dr), ("IA", dr), ("XR" if dr == 0 else "HB")
                        for ti, (t0, tw, v) in enumerate(tiles):
                            b1 = (cnt % 2) * 2
                            b2 = b1 + 1
                            cnt += 1
                            mm_group(ps[b1][:, 0:tw], [(GW[k][:, dr * 2 + 0, :], XBF[:, t0:t0 + tw])], reads=[("GW", k), "XBF"], writes=[psk[b1]])
                            mm_group(ps[b2][:, 0:tw], [(GW[k][:, dr * 2 + 1, :], XBF[:, t0:t0 + tw])], reads=[("GW", k), "XBF"], writes=[psk[b2]])
                            P.op("act", lambda e, b1=b1, tw=tw, t0=t0, dr=dr, m=m, RA=RA: e.activation(out=RA[:, t0:t0 + tw], in_=ps[b1][:, 0:tw], func=AF.Sigmoid, bias=vcol("lgb", (dr * 2 + 0) * 8 + m)),
                                 reads=[psk[b1], "VP"], writes=[rk_])
                            P.op("act", lambda e, b2=b2, tw=tw, t0=t0, dr=dr, m=m, IA=IA: e.activation(out=IA[:, t0:t0 + tw], in_=ps[b2][:, 0:tw], func=AF.Sigmoid, bias=vcol("lgb", (dr * 2 + 1) * 8 + m)),
                                 reads=[psk[b2], "VP"], writes=[ik])
                        P.op("act", lambda e, dr=dr, m=m, A=A, RA=RA: e.activation(out=A[:], in_=RA[:, 0:NTOK], func=AF.Exp, scale=LC[:, 0, dr * 8 + m:dr * 8 + m + 1]),
                             reads=[rk_, "LC"], writes=[ak])
                        P.op("act", lambda e, dr=dr, m=m, RA=RA: e.activation(out=T2[:], in_=RA[:, 0:NTOK], func=AF.Exp, scale=LC[:, 1, dr * 8 + m:dr * 8 + m + 1]),
                             reads=[rk_, "LC"], writes=["T2"])
                        P.op("act", lambda e: e.activation(out=T2[:], in_=T2[:], func=AF.Sqrt, scale=-1.0, bias=ONEB[:, 0:1]),
                             reads=["T2", "ONEB"], writes=["T2"])
                        P.op("dve", lambda e, IA=IA: e.tensor_tensor(out=IA[:], in0=IA[:], in1=T2[:], op=ALU.mult), reads=["T2", ik], writes=[ik])
                        P.op("dve", lambda e, IA=IA: e.tensor_tensor(out=IA[:], in0=IA[:], in1=X1[:], op=ALU.mult), reads=[ik, "X1"], writes=[ik])
                        c0, c1 = NLAT, NTOK
                        if dr == 0:
                            P.op("dve", lambda e, Hd=Hd, A=A, IA=IA: e.tensor_tensor_scan(out=Hd[:, c0:c1], data0=A[:, c0:c1], data1=IA[:, c0:c1], initial=0.0, op0=ALU.mult, op1=ALU.add),
                                 reads=[ak, ik], writes=[hk])
                            P.op("dve", lambda e, Hd=Hd, A=A, IA=IA: e.tensor_tensor_scan(out=Hd[:, 0:NLAT], data0=A[:, 0:NLAT], data1=IA[:, 0:NLAT], initial=Hd[:, c1 - 1:c1], op0=ALU.mult, op1=ALU.add),
                                 reads=[ak, ik, hk], writes=[hk])
                        else:
                            P.op("dve", lambda e, Hd=Hd, A=A, IA=IA: e.tensor_tensor_scan(out=Hd[:, c0:c1][:, ::-1], data0=A[:, c0:c1][:, ::-1], data1=IA[:, c0:c1][:, ::-1], initial=0.0, op0=ALU.mult, op1=ALU.add),
                                 reads=[ak, ik], writes=[hk])
                            P.op("dve", lambda e, Hd=Hd, A=A, IA=IA: e.tensor_tensor_scan(out=Hd[:, 0:NLAT][:, ::-1], data0=A[:, 0:NLAT][:, ::-1], data1=IA[:, 0:NLAT][:, ::-1], initial=Hd[:, c0:c0 + 1], op0=ALU.mult, op1=ALU.add),
                                 reads=[ak, ik, hk], writes=[hk])
                    for ti, (t0, tw, v) in enumerate(LAT_TILES):
                        P.op("act", lambda e, ti=ti, tw=tw, t0=t0: e.activation(out=GEL[:, t0:t0 + tw], in_=ps[4 + ti][:, 0:tw], func=AF.Gelu_apprx_tanh),
                             reads=[psk[4 + ti]], writes=["GEL"])
                    P.op("dve", lambda e: e.tensor_tensor(out=HF[:, 0:NLAT], in0=HF[:, 0:NLAT], in1=HB[:, 0:NLAT], op=ALU.add), reads=["HF", "HB"], writes=["HF"])
                    P.op("dve", lambda e, k=k: e.tensor_tensor(out=RL[0][:], in0=HF[:, 0:NLAT], in1=GEL[:], op=ALU.mult), reads=["HF", "GEL"], writes=[("RL", 0)])
                    P.dma("sp", MIXD[1024 + m * 128:1024 + (m + 1) * 128, 0:NLAT], RL[0][:], reads=[("RL", 0)], writes=["MIXD"])
        P.barrier()
        mixer_out(l, odo_d, LAT_TILES)

    def epilogue():
        with contextlib.ExitStack() as st:
            XB = [sb(st, "OXB%d" % i, [128, 16, 128], F32) for i in range(2)]
            OB = [sb(st, "OB%d" % i, [128, D], F32) for i in range(2)]
            for tb in range(16):
                xb, ob = XB[tb % 2], OB[tb % 2]
                P.dma("sp", xb[:], XTv[:, :, tb * 128:(tb + 1) * 128], reads=["XT"], writes=[("OXB", tb % 2)])
                for g in range(4):
                    bank = (tb % 2) * 4 + g
                    for q in range(4):
                        dc = g * 4 + q
                        P.op("pe", lambda e, bank=bank, q=q, dc=dc, xb=xb: e.transpose(
                            out=ps[bank][:, q * 128:(q + 1) * 128], in_=xb[:, dc, :], identity=ident_f[:]),
                            reads=[("OXB", tb % 2), "ident_f"], writes=[psk[bank]])
                    if g % 2:
                        P.op("act", lambda e, bank=bank, g=g, ob=ob: e.copy(out=ob[:, g * 512:(g + 1) * 512], in_=ps[bank][:]), reads=[psk[bank]], writes=[("OB", tb % 2)])
                    else:
                        P.op("dve", lambda e, bank=bank, g=g, ob=ob: e.tensor_copy(out=ob[:, g * 512:(g + 1) * 512], in_=ps[bank][:]), reads=[psk[bank]], writes=[("OB", tb % 2)])
                P.dma("pool", out_d[tb * 128:(tb + 1) * 128, :], ob[:], reads=[("OB", tb % 2)], writes=["out"])
        P.barrier()

    EPSB = nc.alloc_sbuf_tensor("EPSB", [128, 1], F32)
    ONEB = nc.alloc_sbuf_tensor("ONEB", [128, 1], F32)
    P.op("dve", lambda e: e.memset(EPSB[:], EPS), writes=["EPSB"])
    P.op("dve", lambda e: e.memset(ONEB[:], 1.0 + 2.4e-7), writes=["ONEB"])

    stages = build.stages
    with contextlib.ExitStack() as wst:
        WM.extend(sb(wst, "WM%d" % i, [128, 16, 512], BF16) for i in range(2))
        prologue()
        if stages >= 1:
            ffn(0, 0, TILES, first=True)
        if stages >= 2:
            even_mixer(0)
        if stages >= 3:
            ffn(0, 1, TILES)
        bg_tasks(72)
        P.barrier()
    if stages >= 4:
        ffn(1, 0, TILES)
    if stages >= 5:
        odd_mixer(1)
    if stages >= 6:
        ffn(1, 1, LAT_TILES)
    epilogue()
    P.emit()
    return P


build.stages = 6


def _rpb_tiles(rpb):
    H = rpb.shape[0]
    out = np.full((H, 20, 2, 64, 8, 64), NEG_INF, np.float32)
    col = np.arange(64)
    cs = np.clip(col - 8, 0, 48)
    kc = col[:, None]
    qc = col[None, :]
    col_in = (kc >= cs[None, :]) & (kc < cs[None, :] + 16)
    dcol = np.clip(kc - qc + 15, 0, 30)
    for i4, grp in [(0, ATT_GROUPS[0]), (1, ATT_GROUPS[1]), (3, ATT_GROUPS[3])]:
        for (chunk, t) in grp:
            for a in range(2):
                for b in range(8):
                    kr = 2 * chunk + a
                    r = 8 * i4 + b
                    r0 = min(max(r - 4, 0), 24)
                    if not (r0 <= kr <= r0 + 7):
                        continue
                    vals = rpb[:, kr - r + 7, :][:, dcol]
                    out[:, t, a, :, b, :] = np.where(col_in[None], vals, np.float32(NEG_INF))
    return np.ascontiguousarray(out.reshape(H, 20, 128, 512))


def _pack_vecs(b, inp):
    rows = np.zeros((NVROWS, 128), np.float32)

    def put(name, arr):
        o, n = VEC_LAYOUT[name]
        rows[o:o + n] = np.asarray(arr, np.float32).reshape(n, 128)

    put("c", inp["c"][b])
    put("c_ctx", inp["c_ctx"])
    put("b_mod0", inp["b_mod"][0])
    put("b_mod1", inp["b_mod"][1])
    for l in range(2):
        for n in range(3):
            put("g%d%d" % (l, n), inp["norm_g"][l, n])
    put("sc_w", inp["sc_w"][0])
    put("sc_b", inp["sc_b"][0])
    put("cc_w", inp["cc_w"][0])
    put("cc_b", inp["cc_b"][0])
    put("ln_g", inp["cc_ln_g"][0])
    put("ln_b", inp["cc_ln_b"][0])
    put("lcw", inp["lru_conv_w"][0])
    put("lcb", inp["lru_conv_b"][0])
    put("lgb", inp["lru_gate_b"][0])
    put("lam", inp["lru_lam"][0])
    put("qg", np.concatenate([inp["q_norm_g"][0], inp["q_norm_g"][0]]))
    put("kg", np.concatenate([inp["k_norm_g"][0], inp["k_norm_g"][0]]))
    return rows


_CACHE = {}


def kernel(**inp):
    inp = {k: np.asarray(v) for k, v in inp.items()}
    if "P" not in _CACHE:
        _CACHE["P"] = build()
    P = _CACHE["P"]
    rpbT = _rpb_tiles(inp["na_rpb"][0])
    shared = {
        "w_mod": inp["w_mod"], "ffn_w_in": inp["ffn_w_in"], "ffn_w_out": inp["ffn_w_out"],
        "ev_w_in": inp["ev_w_in"][0], "ev_w_out": inp["ev_w_out"][0],
        "od_w_in": inp["od_w_in"][0], "od_w_out": inp["od_w_out"][0],
        "lru_gate_w": inp["lru_gate_w"][0], "rpbT": rpbT,
    }
    shared = {k: np.ascontiguousarray(v, dtype=np.float32) for k, v in shared.items()}
    in_maps = []
    for b in range(8):
        m = dict(shared)
        m["x"] = np.ascontiguousarray(inp["x"][b], dtype=np.float32)
        m["ctx"] = np.ascontiguousarray(inp["ctx"][b], dtype=np.float32)
        m["vecs"] = _pack_vecs(b, inp)
        in_maps.append(m)
    res = run_bass_kernel_spmd(P.nc, in_maps, core_ids=list(range(8)))
    return np.stack([np.asarray(r["out"]) for r in res.results], axis=0).astype(np.float32)
```
